# Optimizing a Trainium2 kernel written in Bass

```python
import math
import jax, jax.numpy as jnp
from jax import lax
import numpy as np


D_MODEL = 1024
BATCH = 4
SEQ = 4096
DEPTH = 2

GRID_W = 64
CTX_LEN = 256
N_DIFF_HEADS = 6
DIFF_HEAD_DIM = 64
DIFF_V_DIM = 2 * DIFF_HEAD_DIM
N_FOURIER_GROUPS = 4
FOURIER_GROUP_DIM = 64
ATTN_QK_W = N_DIFF_HEADS * 2 * DIFF_HEAD_DIM
ATTN_V_W = N_DIFF_HEADS * DIFF_V_DIM
FOURIER_W = N_FOURIER_GROUPS * FOURIER_GROUP_DIM
EVEN_IN_W = 2 * ATTN_QK_W + ATTN_V_W + FOURIER_W
EVEN_MIX_W = ATTN_V_W + FOURIER_W
CONV_K = 3
N_EXPERTS = 16
EC_CAPACITY_FACTOR = 2
D_EXPERT = 1024
ROPE_BASE = 10000.0
Q_BLOCK = 128
EPS = 1e-6
N_EVEN = (DEPTH + 1) // 2
N_ODD = DEPTH // 2

kernel_name = 'hybrid_diffattn_fourier_shortconv_ecmoe_dit'


def rmsnorm(x, g):
    xf = x.astype(jnp.float32)
    y = xf * lax.rsqrt(jnp.mean(xf * xf, axis=-1, keepdims=True) + EPS)
    return (y * g.astype(jnp.float32)).astype(x.dtype)


def modulate(x, g, shift, scale):
    return rmsnorm(x, g) * (1 + scale) + shift


def ada_params(cond, w_ada, b_ada):
    m = jax.nn.silu(cond) @ w_ada + b_ada
    return jnp.split(m[..., None, :], 6, axis=-1)


def axial_rope(n, dtype):
    rows = n // GRID_W
    r = jnp.repeat(jnp.arange(rows, dtype=jnp.float32), GRID_W)
    col = jnp.tile(jnp.arange(GRID_W, dtype=jnp.float32), rows)
    n_freq = DIFF_HEAD_DIM // 4
    inv = ROPE_BASE ** (-jnp.arange(n_freq, dtype=jnp.float32) / n_freq)
    ar = r[:, None] * inv
    ac = col[:, None] * inv
    ang = jnp.concatenate([ar, ar, ac, ac], axis=-1)
    return jnp.cos(ang).astype(dtype), jnp.sin(ang).astype(dtype)


def apply_rope(x, cos, sin):
    x1, x2, x3, x4 = jnp.split(x, 4, axis=-1)
    rot = jnp.concatenate([-x2, x1, -x4, x3], axis=-1)
    return x * cos[None, :, None, None, :] + rot * sin[None, :, None, None, :]


def qk_heads(t):
    return t.reshape(t.shape[0], t.shape[1], N_DIFF_HEADS, 2, DIFF_HEAD_DIM)


def v_heads(t):
    return t.reshape(t.shape[0], t.shape[1], N_DIFF_HEADS, DIFF_V_DIM)


def diff_attend(q, k, v, lam):
    s = jnp.einsum('bqhmd,bkhmd->bhmqk', q, k).astype(jnp.float32)
    p = jax.nn.softmax(s, axis=-1)
    a = p[:, :, 0] - lam * p[:, :, 1]
    return jnp.einsum('bhqk,bkhe->bqhe', a.astype(v.dtype), v)


def blocked_diff_attend(q, k, v, lam):
    b, n = q.shape[0], q.shape[1]
    qb = q.reshape(b, n // Q_BLOCK, Q_BLOCK, N_DIFF_HEADS, 2, DIFF_HEAD_DIM).swapaxes(0, 1)
    out = lax.map(lambda qq: diff_attend(qq, k, v, lam), qb)
    return out.swapaxes(0, 1).reshape(b, n, N_DIFF_HEADS, DIFF_V_DIM)


def fourier_mix(f):
    b, n = f.shape[0], f.shape[1]
    fg = f.reshape(b, n, N_FOURIER_GROUPS, FOURIER_GROUP_DIM).astype(jnp.float32)
    y = jnp.fft.fft2(fg, axes=(1, 3), norm='ortho').real
    return y.reshape(b, n, FOURIER_W).astype(f.dtype)


def merge_even(att, f, subln, lam_init, w_out):
    b, n = att.shape[0], att.shape[1]
    att = rmsnorm(att, subln) * (1.0 - lam_init)
    o = jnp.concatenate([att.reshape(b, n, ATTN_V_W), fourier_mix(f)], axis=-1)
    return o @ w_out


def even_mixer(h, hc, w_in, q_norm, k_norm, lam, subln, w_out, lam_init, cos, sin, ctx_out):
    scale = DIFF_HEAD_DIM ** -0.5
    cuts = [ATTN_QK_W, 2 * ATTN_QK_W, 2 * ATTN_QK_W + ATTN_V_W]
    q, k, v, f = jnp.split(h @ w_in, cuts, axis=-1)
    q = apply_rope(rmsnorm(qk_heads(q), q_norm), cos, sin) * scale
    k = apply_rope(rmsnorm(qk_heads(k), k_norm), cos, sin)
    qc = None
    fc = None
    if ctx_out:
        qc, kc, vc, fc = jnp.split(hc @ w_in, cuts, axis=-1)
    else:
        kc, vc = jnp.split(hc @ w_in[:, ATTN_QK_W:2 * ATTN_QK_W + ATTN_V_W], [ATTN_QK_W], axis=-1)
    kc = rmsnorm(qk_heads(kc), k_norm)
    vc = v_heads(vc)
    keys = jnp.concatenate([k, kc], axis=1)
    vals = jnp.concatenate([v_heads(v), vc], axis=1)
    y = merge_even(blocked_diff_attend(q, keys, vals, lam), f, subln, lam_init, w_out)
    yc = None
    if ctx_out:
        qc = rmsnorm(qk_heads(qc), q_norm) * scale
        yc = merge_even(diff_attend(qc, kc, vc, lam), fc, subln, lam_init, w_out)
    return y, yc


def shortconv(z, w):
    zp = jnp.pad(z, ((0, 0), (1, 1), (0, 0)))
    return w[0] * zp[:, :-2] + w[1] * zp[:, 1:-1] + w[2] * zp[:, 2:]


def odd_mixer(h, w_in, conv_w, w_out):
    bg, cg, u = jnp.split(h @ w_in, 3, axis=-1)
    return (bg * shortconv(cg * u, conv_w)) @ w_out


def ec_moe(h, w_router, w_gate, w_up, w_down):
    b, n, d = h.shape
    cap = EC_CAPACITY_FACTOR * n // N_EXPERTS
    aff = jax.nn.softmax(h.astype(jnp.float32) @ w_router.astype(jnp.float32), axis=-1)
    g, idx = lax.top_k(aff.swapaxes(1, 2), cap)
    xs = jax.vmap(lambda hb, ib: hb[ib])(h, idx)
    a = jax.nn.silu(jnp.einsum('becd,edf->becf', xs, w_gate)) * jnp.einsum('becd,edf->becf', xs, w_up)
    y = jnp.einsum('becf,efd->becd', a, w_down) * g[..., None].astype(h.dtype)
    return jax.vmap(lambda yb, ib: jnp.zeros((n, d), yb.dtype).at[ib.reshape(-1)].add(yb.reshape(-1, d)))(y, idx)


def setup_inputs(seed: int = 0) -> dict:
    key = jax.random.key(seed)
    ks = jax.random.split(key, 24)

    def nrm(k, shape, s):
        return jax.random.normal(k, shape, jnp.float32) * s

    D = D_MODEL
    return {
        'x': nrm(ks[0], (BATCH, SEQ, D), 1.0),
        'c': nrm(ks[1], (BATCH, D), 1.0),
        'ctx': nrm(ks[2], (BATCH, CTX_LEN, D), 1.0),
        'c_ctx': nrm(ks[3], (D,), 1.0),
        'ada_w': nrm(ks[4], (DEPTH, D, 6 * D), 0.5 * D ** -0.5),
        'ada_b': nrm(ks[5], (DEPTH, 6 * D), 0.02),
        'norm_mix': 1.0 + nrm(ks[6], (DEPTH, D), 0.02),
        'norm_ffn': 1.0 + nrm(ks[7], (DEPTH, D), 0.02),
        'attn_w_in': nrm(ks[8], (N_EVEN, D, EVEN_IN_W), D ** -0.5),
        'attn_q_norm': 1.0 + nrm(ks[9], (N_EVEN, DIFF_HEAD_DIM), 0.02),
        'attn_k_norm': 1.0 + nrm(ks[10], (N_EVEN, DIFF_HEAD_DIM), 0.02),
        'lam_q1': nrm(ks[11], (N_EVEN, DIFF_HEAD_DIM), 0.1),
        'lam_k1': nrm(ks[12], (N_EVEN, DIFF_HEAD_DIM), 0.1),
        'lam_q2': nrm(ks[13], (N_EVEN, DIFF_HEAD_DIM), 0.1),
        'lam_k2': nrm(ks[14], (N_EVEN, DIFF_HEAD_DIM), 0.1),
        'attn_subln': 1.0 + nrm(ks[15], (N_EVEN, DIFF_V_DIM), 0.02),
        'attn_w_out': nrm(ks[16], (N_EVEN, EVEN_MIX_W, D), EVEN_MIX_W ** -0.5),
        'conv_w_in': nrm(ks[17], (N_ODD, D, 3 * D), D ** -0.5),
        'conv_w': nrm(ks[18], (N_ODD, CONV_K, D), CONV_K ** -0.5),
        'conv_w_out': nrm(ks[19], (N_ODD, D, D), D ** -0.5),
        'router_w': nrm(ks[20], (DEPTH, D, N_EXPERTS), D ** -0.5),
        'moe_w_gate': nrm(ks[21], (DEPTH, N_EXPERTS, D, D_EXPERT), D ** -0.5),
        'moe_w_up': nrm(ks[22], (DEPTH, N_EXPERTS, D, D_EXPERT), D ** -0.5),
        'moe_w_down': nrm(ks[23], (DEPTH, N_EXPERTS, D_EXPERT, D), D_EXPERT ** -0.5),
    }


def reference(x, c, ctx, c_ctx, ada_w, ada_b, norm_mix, norm_ffn,
              attn_w_in, attn_q_norm, attn_k_norm, lam_q1, lam_k1, lam_q2, lam_k2,
              attn_subln, attn_w_out, conv_w_in, conv_w, conv_w_out,
              router_w, moe_w_gate, moe_w_up, moe_w_down):
    cos, sin = axial_rope(x.shape[1], x.dtype)
    xc = ctx
    for l in range(DEPTH):
        ctx_out = any(j % 2 == 0 for j in range(l + 1, DEPTH))
        ctx_in = (l % 2 == 0) or ctx_out
        sh_m, sc_m, g_m, sh_f, sc_f, g_f = ada_params(c, ada_w[l], ada_b[l])
        h = modulate(x, norm_mix[l], sh_m, sc_m)
        hc = None
        cg_m = cg_f = csh_f = csc_f = None
        if ctx_in:
            csh_m, csc_m, cg_m, csh_f, csc_f, cg_f = ada_params(c_ctx, ada_w[l], ada_b[l])
            hc = modulate(xc, norm_mix[l], csh_m, csc_m)
        if l % 2 == 0:
            e = l // 2
            lam_init = 0.8 - 0.6 * math.exp(-0.3 * l)
            lam = (jnp.exp(jnp.sum(lam_q1[e].astype(jnp.float32) * lam_k1[e].astype(jnp.float32)))
                   - jnp.exp(jnp.sum(lam_q2[e].astype(jnp.float32) * lam_k2[e].astype(jnp.float32)))
                   + lam_init)
            y, yc = even_mixer(h, hc, attn_w_in[e], attn_q_norm[e], attn_k_norm[e], lam,
                               attn_subln[e], attn_w_out[e], lam_init, cos, sin, ctx_out)
        else:
            o = l // 2
            y = odd_mixer(h, conv_w_in[o], conv_w[o], conv_w_out[o])
            yc = odd_mixer(hc, conv_w_in[o], conv_w[o], conv_w_out[o]) if ctx_out else None
        x = x + g_m * y
        x = x + g_f * ec_moe(modulate(x, norm_ffn[l], sh_f, sc_f),
                             router_w[l], moe_w_gate[l], moe_w_up[l], moe_w_down[l])
        if ctx_out:
            xc = xc + cg_m * yc
            xc = xc + cg_f * ec_moe(modulate(xc, norm_ffn[l], csh_f, csc_f),
                                    router_w[l], moe_w_gate[l], moe_w_up[l], moe_w_down[l])
    return x
```

```python
from concourse.bass_utils import run_bass_kernel_spmd
import numpy as np
import concourse.bass as bass
import concourse.mybir as mybir

F32 = mybir.dt.float32
BF16 = mybir.dt.bfloat16
I32 = mybir.dt.int32
ALU = mybir.AluOpType
AF = mybir.ActivationFunctionType
AX = mybir.AxisListType
ENG = ("pe", "act", "dve", "pool", "sp")


class Op:
    __slots__ = ("eng", "fn", "deps", "sig", "sem", "val", "dma", "waits", "semkey")


class Sched:
    def __init__(self, nc):
        self.nc = nc
        self.streams = {e: [] for e in ENG}
        self.lw = {}
        self.lr = {}
        self.pending_dma = []
        self.last_real = {}
        self.slot_sem = {}
        self.slot_cnt = {}
        self.nsem = 0
        self.sb_off = 0
        self.sb_base = 0
        self.uid = 0

    def alloc(self, shape, dtype, name="t"):
        if not hasattr(self, "views"):
            big = self.nc.alloc_sbuf_tensor("arena", [128, 103 * 1024], BF16)
            self.views = {BF16: big, F32: big.bitcast(F32), I32: big.bitcast(I32)}
        esz = mybir.dt.size(dtype)
        n = int(np.prod(shape[1:]))
        off = (self.sb_off + 63) // 64 * 64
        self.sb_off = off + n * esz
        assert self.sb_off <= 206 * 1024, (name, self.sb_off)
        ap = self.views[dtype][0:shape[0], off // esz: off // esz + n]
        if len(shape) == 3:
            ap = ap.rearrange("p (a b) -> p a b", a=shape[1])
        elif len(shape) == 4:
            ap = ap.rearrange("p (a b c) -> p a b c", a=shape[1], b=shape[2])
        return ap

    def mark_persistent(self):
        self.sb_base = self.sb_off

    def reset_phase(self):
        self.sb_off = self.sb_base

    def newsem(self, name):
        self.nsem += 1
        return self.nc.alloc_semaphore(f"{name}_{self.nsem}")

    def op(self, eng, fn, r=(), w=(), dma=None):
        o = Op()
        o.eng, o.fn, o.sig, o.dma = eng, fn, False, dma
        o.sem = None
        o.val = 0
        deps = {}
        for k in r:
            for d in self.lw.get(k, ()):
                deps[id(d)] = d
        for k in w:
            for d in self.lw.get(k, ()):
                if d.dma or dma or d.eng != eng:
                    deps[id(d)] = d
            for d in self.lr.get(k, ()):
                if d.dma or dma or d.eng != eng:
                    deps[id(d)] = d
        o.deps = list(deps.values())
        for d in o.deps:
            d.sig = True
        for k in w:
            self.lw[k] = [o]
            self.lr[k] = []
        for k in r:
            lst = self.lr.setdefault(k, [])
            if not dma:
                lst[:] = [x for x in lst if x.dma or x.eng != eng]
            lst.append(o)
        self.streams[eng].append(o)
        if dma:
            o.sig = True
            self.pending_dma.append(o)
        elif fn is not None:
            self.last_real[eng] = o
        return o

    def barrier(self):
        lasts = list(self.last_real.values()) + list(self.pending_dma)
        for d in lasts:
            d.sig = True
        for e in ENG:
            o = Op()
            o.eng, o.fn, o.sig, o.dma, o.sem, o.val = e, None, False, None, None, 0
            o.deps = [d for d in lasts if d.dma or d.eng != e]
            self.streams[e].append(o)
        self.lw, self.lr, self.pending_dma = {}, {}, []

    def finalize(self):
        for e in ENG:
            cnt = 0
            cur = None
            for o in self.streams[e]:
                if o.dma:
                    key = (e, o.dma)
                    if key not in self.slot_sem:
                        self.slot_sem[key] = self.newsem("d")
                        self.slot_cnt[key] = 0
                    self.slot_cnt[key] += 16
                    o.sem, o.val, o.semkey = self.slot_sem[key], self.slot_cnt[key], ("d",) + key
                elif o.sig:
                    if cur is None or cnt >= 30000:
                        cur = self.newsem("e" + e)
                        curkey = ("e", e, self.nsem)
                        cnt = 0
                    cnt += 1
                    o.sem, o.val, o.semkey = cur, cnt, curkey
        nw = 0
        for e in ENG:
            known = {}
            for o in self.streams[e]:
                need = {}
                for d in o.deps:
                    assert d.sem is not None
                    if known.get(d.semkey, 0) < d.val:
                        if need.get(d.semkey, (None, 0))[1] < d.val:
                            need[d.semkey] = (d.sem, d.val)
                for k, (sm, v) in need.items():
                    known[k] = v
                o.waits = list(need.values())
                nw += len(o.waits)
        self.nwaits = nw

    def replay(self, eng, e):
        for o in self.streams[eng]:
            for sm, v in o.waits:
                e.wait_ge(sm, v)
            if o.fn is not None:
                ins = o.fn(e)
                if o.sig:
                    ins.then_inc(o.sem, 16 if o.dma else 1)

    def emit(self):
        self.barrier()
        self.finalize()
        with self.nc.Block() as blk:
            @blk.tensor
            def _(e):
                self.replay("pe", e)

            @blk.scalar
            def _(e):
                self.replay("act", e)

            @blk.vector
            def _(e):
                self.replay("dve", e)

            @blk.gpsimd
            def _(e):
                self.replay("pool", e)

            @blk.sync
            def _(e):
                self.replay("sp", e)


EPS = 1e-6
T = 4096
NT = 32
NKEY = 4352
NKT = 34
DEBUG = False


def build_nc(stop=None, debug=False):
    nc = bass.Bass("TRN2", target_bir_lowering=False)
    S = Sched(nc)

    def din(name, shape, dt=F32):
        return nc.dram_tensor(name, list(shape), dt, kind="ExternalInput").ap()

    def dscr(name, shape, dt, out=False):
        return nc.dram_tensor(name, list(shape), dt, kind="ExternalOutput" if debug else "Internal").ap()

    x_d = din("x", [T, 1024])
    ctx_d = din("ctx", [256, 1024])
    c2_d = din("c2", [128, 8, 2])
    adaw_d = din("ada_w", [2, 1024, 6144])
    adab_d = din("adab2", [2, 2, 6144])
    nm_d = din("nmrep", [2, 128, 1024])
    nf_d = din("nfrep", [2, 128, 1024])
    win_d = din("attn_w_in", [1024, 2560])
    wout_d = din("attn_w_out", [1024, 1024])
    cwin_d = din("conv_w_in", [1024, 3072])
    cwout_d = din("conv_w_out", [1024, 1024])
    qk_d = din("qkcol", [128, 2])
    lam_d = din("lamrep", [128, 256])
    sub_d = din("sublnrep", [128, 128])
    convw_d = din("convw", [128, 8, 3])
    rw_d = din("router_w", [2, 1024, 16])
    wg_d = din("moe_w_gate", [2, 16, 1024, 1024])
    wu_d = din("moe_w_up", [2, 16, 1024, 1024])
    wd_d = din("moe_w_down", [2, 16, 1024, 1024])
    cpk_d = din("cpk", [128, 1152])
    tok_d = din("tokhl", [128, 1024])
    sel_d = din("sel", [2, 256])
    rope_d = din("rope", [2, 128, T])
    cs_d = din("csd", [256, 512])
    cos_d = din("cosd", [T, T], BF16)
    sin_d = din("sind", [T, T], BF16)
    out_d = nc.dram_tensor("out", [T, 1024], F32, kind="ExternalOutput").ap()

    kTd = dscr("kTd", [6, 128, NKEY], BF16)
    qTd = dscr("qTd", [6, 128, T], BF16)
    V1d = dscr("V1d", [NKEY, 774], BF16)
    fcsd = dscr("fcsd", [T, 512], BF16)
    oTd = dscr("oTd", [8, 128, T], BF16)
    x1d = dscr("x1d", [T, 1024], F32, DEBUG)
    x2d = dscr("x2d", [T, 1024], F32, DEBUG)
    x3d = dscr("x3d", [T, 1024], F32, DEBUG)
    hfd = dscr("hfd", [T, 1024], BF16)
    ymoe = dscr("ymoe", [T, 1024], F32)

    ps = [nc.alloc_psum_tensor(f"ps{i}", [128, 512], F32) for i in range(8)]
    psb = [p.bitcast(BF16) for p in ps]

    def DMA(eng, out, in_, r=(), w=(), slot=None):
        S.op(eng, lambda e: e.dma_start(out=out, in_=in_), r=r, w=w, dma=slot)

    def MM(out, lhsT, rhs, start, stop, r=(), w=()):
        S.op("pe", lambda e: e.matmul(out, lhsT, rhs, start=start, stop=stop), r=r, w=w)

    def TR(out, in_, ident, r=(), w=()):
        S.op("pe", lambda e: e.transpose(out, in_, ident), r=r, w=w)

    def ACT(out, in_, func, r=(), w=(), **kw):
        S.op("act", lambda e: e.activation(out=out, in_=in_, func=func, **kw), r=r, w=w)

    def TT(eng, out, in0, in1, op, r=(), w=()):
        S.op(eng, lambda e: e.tensor_tensor(out=out, in0=in0, in1=in1, op=op), r=r, w=w)

    def TS(eng, out, in0, s1, op0, s2=None, op1=None, r=(), w=(), accum=None):
        if op1 is None:
            S.op(eng, lambda e: e.tensor_single_scalar(out=out, in_=in0, scalar=s1, op=op0), r=r, w=w)
        else:
            S.op(eng, lambda e: e.tensor_scalar(out=out, in0=in0, scalar1=s1, scalar2=s2, op0=op0, op1=op1,
                                                accum_out=accum), r=r, w=w)

    def STT(eng, out, in0, scalar, in1, op0, op1, r=(), w=()):
        S.op(eng, lambda e: e.scalar_tensor_tensor(out=out, in0=in0, scalar=scalar, in1=in1, op0=op0, op1=op1),
             r=r, w=w)

    def CP(eng, out, in_, r=(), w=()):
        if eng == "act":
            ACT(out, in_, AF.Copy, r=r, w=w)
        else:
            S.op(eng, lambda e: e.tensor_copy(out=out, in_=in_), r=r, w=w)

    def RECIP(out, in_, r=(), w=()):
        S.op("dve", lambda e: e.reciprocal(out=out, in_=in_), r=r, w=w)

    def MSET(eng, ap, v, w=()):
        S.op(eng, lambda e: e.memset(ap, v), w=w)

    identf = S.alloc([128, 128], F32)
    identb = S.alloc([128, 128], BF16)
    bones = S.alloc([128, 128], BF16)
    RT = S.alloc([128, 128], BF16)
    Ust = S.alloc([128, 128], BF16)
    onesb = S.alloc([128, 128], BF16)
    iota = S.alloc([128, 512], F32)
    tokhl = S.alloc([128, 32, 16, 2], BF16)
    sel = S.alloc([2, 256], F32)
    MOD = S.alloc([128, 6, 1024], F32)
    CMOD = S.alloc([128, 2, 1024], F32)
    AFF = S.alloc([128, 32, 16], F32)
    small = S.alloc([128, 64], F32)
    DMA("sp", identf[:], cpk_d[:, 0:128], w=["identf"], slot="c0")
    DMA("sp", iota[:], cpk_d[:, 640:1152], w=["iota"], slot="c1")
    DMA("sp", sel[:], sel_d, w=["sel"], slot="c2")
    DMA("pool", identb[:], cpk_d[:, 0:128], w=["identb"], slot="c3")
    DMA("pool", bones[:], cpk_d[:, 128:256], w=["bones"], slot="c4")
    DMA("pool", RT[:], cpk_d[:, 256:384], w=["RT"], slot="c5")
    DMA("pool", Ust[:], cpk_d[:, 384:512], w=["Ust"], slot="c6")
    DMA("pool", onesb[:], cpk_d[:, 512:640], w=["onesb"], slot="c7")
    DMA("pool", tokhl.rearrange("p a b c -> p (a b c)"), tok_d, w=["tokhl"], slot="c8")
    S.mark_persistent()
    S.barrier()

    def ada(l, with_ctx):
        S.reset_phase()
        sc = S.alloc([128, 8, 2], F32)
        m2 = S.alloc([2, 6144], F32)
        ab = S.alloc([2, 6144], F32)
        wb = [S.alloc([128, 8, 512], F32) for _ in range(2)]
        nmt = S.alloc([128, 1024], F32)
        nft = S.alloc([128, 1024], F32)
        DMA("sp", sc[:], c2_d, w=["sc"], slot="a0")
        DMA("sp", ab[:], adab_d[l], w=["ab"], slot="a1")
        DMA("sp", nmt[:], nm_d[l], w=["nmt"], slot="a2")
        DMA("sp", nft[:], nf_d[l], w=["nft"], slot="a3")
        ACT(sc[:], sc[:], AF.Silu, r=["sc"], w=["sc"])
        wv = adaw_d[l].rearrange("(k p) n -> p k n", p=128)
        for cg in range(12):
            wt = wb[cg % 2]
            DMA("sp", wt[:], wv[:, :, cg * 512:(cg + 1) * 512], w=[("wb", cg % 2)], slot=f"aw{cg % 2}")
            pp = ps[cg % 2]
            for k in range(8):
                MM(pp[0:2, :], sc[:, k, :], wt[:, k, :], k == 0, k == 7, r=["sc", ("wb", cg % 2)], w=[("ps", cg % 2)])
            TT("dve", m2[:, cg * 512:(cg + 1) * 512], pp[0:2, :], ab[:, cg * 512:(cg + 1) * 512], ALU.add,
               r=[("ps", cg % 2), "ab"], w=["m2"])
        for j in range(12):
            pp = ps[2 + j % 2]
            MM(pp[:, :], sel[0:2, 0:128], m2[0:2, j * 512:(j + 1) * 512], True, True, r=["sel", "m2"], w=[("ps", 2 + j % 2)])
            CP("act" if j % 2 else "dve", MOD[:, j // 2, (j % 2) * 512:(j % 2 + 1) * 512], pp[:, :],
               r=[("ps", 2 + j % 2)], w=["MOD"])
            if with_ctx and j < 4:
                pq = ps[4 + j % 2]
                MM(pq[:, :], sel[0:2, 128:256], m2[0:2, j * 512:(j + 1) * 512], True, True, r=["sel", "m2"],
                   w=[("ps", 4 + j % 2)])
                CP("act" if j % 2 else "dve", CMOD[:, j // 2, (j % 2) * 512:(j % 2 + 1) * 512], pq[:, :],
                   r=[("ps", 4 + j % 2)], w=["CMOD"])
        STT("dve", MOD[:, 1, :], MOD[:, 1, :], 1.0, nmt[:], ALU.add, ALU.mult, r=["MOD", "nmt"], w=["MOD"])
        TS("dve", MOD[:, 1, :], MOD[:, 1, :], 32.0, ALU.mult, r=["MOD"], w=["MOD"])
        STT("dve", MOD[:, 4, :], MOD[:, 4, :], 1.0, nft[:], ALU.add, ALU.mult, r=["MOD", "nft"], w=["MOD"])
        TS("dve", MOD[:, 4, :], MOD[:, 4, :], 32.0, ALU.mult, r=["MOD"], w=["MOD"])
        if with_ctx:
            STT("dve", CMOD[:, 1, :], CMOD[:, 1, :], 1.0, nmt[:], ALU.add, ALU.mult, r=["CMOD", "nmt"], w=["CMOD"])
            TS("dve", CMOD[:, 1, :], CMOD[:, 1, :], 32.0, ALU.mult, r=["CMOD"], w=["CMOD"])
        S.barrier()

    def modulate(xt, xkey, G32, SH, out, okey, st, skey, tmp, tkey, junk, jkey):
        ACT(junk, xt, AF.Square, r=[xkey], w=[jkey, skey], accum_out=st[:, 0:1])
        ACT(st[:, 1:2], st[:, 0:1], AF.Sqrt, r=[skey], w=[skey], bias=1024.0 * EPS, scale=1.0)
        RECIP(st[:, 2:3], st[:, 1:2], r=[skey], w=[skey])
        STT("dve", tmp, xt, st[:, 2:3], G32, ALU.mult, ALU.mult, r=[xkey, skey, "MOD", "CMOD"], w=[tkey])
        TT("pool", out, tmp, SH, ALU.add, r=[tkey, "MOD", "CMOD"], w=[okey])

    def proj0():
        S.reset_phase()
        w = S.alloc([128, 8, 2560], BF16)
        cosT = S.alloc([128, T], F32)
        sinT = S.alloc([128, T], F32)
        qk = S.alloc([128, 2], F32)
        CS = S.alloc([128, 2, 512], BF16)
        xb = [S.alloc([128, 1024], F32) for _ in range(2)]
        tmpb = [S.alloc([128, 1024], F32) for _ in range(2)]
        hb = [S.alloc([128, 1024], BF16) for _ in range(2)]
        junk = S.alloc([128, 1024], BF16)
        stt = [S.alloc([128, 4], F32) for _ in range(2)]
        hTb = [S.alloc([128, 8, 512], BF16) for _ in range(2)]
        sqb = S.alloc([128, 512], BF16)
        rs = S.alloc([128, 512], F32)
        knb = [S.alloc([128, 512], BF16) for _ in range(2)]
        t1 = S.alloc([128, 512], F32)
        t2 = S.alloc([128, 512], F32)
        kout = [S.alloc([128, 512], BF16) for _ in range(2)]
        v1t = [S.alloc([128, 6, 129], BF16) for _ in range(2)]
        fT = S.alloc([128, 2, 512], BF16)
        fcst = [S.alloc([128, 512], BF16) for _ in range(2)]
        wv = win_d.rearrange("(k p) n -> p k n", p=128)
        for q in range(4):
            DMA("pool", w[:, :, q * 640:(q + 1) * 640], wv[:, :, q * 640:(q + 1) * 640], w=["w"], slot=f"pw{q}")
        DMA("sp", cosT[:], rope_d[0], w=["cos"], slot="p0")
        DMA("sp", sinT[:], rope_d[1], w=["sin"], slot="p1")
        DMA("sp", qk[:], qk_d, w=["qk"], slot="p2")
        DMA("pool", CS[:], cs_d.rearrange("(c p) n -> p c n", p=128), w=["CS"], slot="p3")
        TS("dve", qk[:, 1:2], qk[:, 1:2], 8.0, ALU.mult, r=["qk"], w=["qk"])
        for i in range(2):
            MSET("pool", v1t[i][:], 1.0, w=[("v1t", i)])
        nrc = [0]

        def normrope(pp, pkey, nt, gain, rope, pos0, dst):
            c = nrc[0]
            nrc[0] += 1
            kn = knb[c % 2]
            ko = kout[c % 2]
            ACT(sqb[:, :nt], pp[:, :nt], AF.Square, r=[pkey], w=["sqb"])
            MM(ps[4][:, :nt], bones[:], sqb[:, :nt], True, True, r=["bones", "sqb"], w=[("ps", 4)])
            ACT(rs[:, :nt], ps[4][:, :nt], AF.Sqrt, r=[("ps", 4)], w=["rs"], bias=64.0 * EPS, scale=1.0)
            RECIP(rs[:, :nt], rs[:, :nt], r=["rs"], w=["rs"])
            STT("dve", kn[:, :nt], pp[:, :nt], gain, rs[:, :nt], ALU.mult, ALU.mult, r=[pkey, "qk", "rs"],
                w=[("kn", c % 2)])
            if rope:
                MM(ps[5][:, :nt], RT[:], kn[:, :nt], True, True, r=["RT", ("kn", c % 2)], w=[("ps", 5)])
                TT("pool", t1[:, :nt], kn[:, :nt], cosT[:, pos0:pos0 + nt], ALU.mult, r=[("kn", c % 2), "cos"], w=["t1"])
                TT("dve", t2[:, :nt], ps[5][:, :nt], sinT[:, pos0:pos0 + nt], ALU.mult, r=[("ps", 5), "sin"], w=["t2"])
                TT("pool", ko[:, :nt], t1[:, :nt], t2[:, :nt], ALU.add, r=["t1", "t2"], w=[("ko", c % 2)])
                DMA("sp", dst, ko[:, :nt], r=[("ko", c % 2)], slot=f"ko{c % 2}")
            else:
                DMA("sp", dst, kn[:, :nt], r=[("kn", c % 2)], slot=f"kn{c % 2}")

        pc = [0]
        for b in range(9):
            isctx = b == 8
            ntile = 2 if isctx else 4
            nt = ntile * 128
            hT = hTb[b % 2]
            hk = ("hT", b % 2)
            for t in range(ntile):
                i2 = t % 2
                src = ctx_d[t * 128:(t + 1) * 128, :] if isctx else x_d[b * 512 + t * 128: b * 512 + (t + 1) * 128, :]
                DMA("sp", xb[i2][:], src, w=[("xb", i2)], slot=f"xb{i2}")
                G = CMOD[:, 1, :] if isctx else MOD[:, 1, :]
                SHh = CMOD[:, 0, :] if isctx else MOD[:, 0, :]
                modulate(xb[i2][:], ("xb", i2), G, SHh, hb[i2][:], ("hb", i2), stt[i2], ("st", i2),
                         tmpb[i2][:], ("tmp", i2), junk[:], "junk")
                pT = psb[t % 2]
                for k in range(8):
                    TR(pT[:, k * 128:(k + 1) * 128], hb[i2][:, k * 128:(k + 1) * 128], identb[:],
                       r=[("hb", i2), "identb"], w=[("ps", t % 2)])
                CP("act", hT[:, :, t * 128:(t + 1) * 128], pT[:, 0:1024].rearrange("p (k n) -> p k n", k=8),
                   r=[("ps", t % 2)], w=[hk])
            for h in range(6):
                for which in ((0, 1) if not isctx else (1,)):
                    bank = 2 + pc[0] % 2
                    pc[0] += 1
                    col0 = (768 if which == 1 else 0) + h * 128
                    for k in range(8):
                        MM(ps[bank][:, :nt], w[:, k, col0:col0 + 128], hT[:, k, :nt], k == 0, k == 7,
                           r=["w", hk], w=[("ps", bank)])
                    dst = (kTd if which == 1 else qTd)[h, :, b * 512:b * 512 + nt]
                    normrope(ps[bank], ("ps", bank), nt, qk[:, which:which + 1], not isctx, b * 512 if not isctx else 0, dst)
            for t in range(ntile):
                vt = v1t[t % 2]
                for half in range(2):
                    bank = 6 + half
                    for k in range(8):
                        MM(ps[bank][:, 0:384], hT[:, k, t * 128:(t + 1) * 128],
                           w[:, k, 1536 + half * 384:1536 + (half + 1) * 384], k == 0, k == 7, r=["w", hk],
                           w=[("ps", bank)])
                    CP("act" if half else "dve", vt[:, half * 3:(half + 1) * 3, 0:128],
                       ps[bank][:, 0:384].rearrange("p (a b) -> p a b", a=3), r=[("ps", bank)], w=[("v1t", t % 2)])
                row0 = b * 512 + t * 128
                DMA("sp", V1d[row0:row0 + 128, :], vt.rearrange("p a b -> p (a b)"), r=[("v1t", t % 2)], slot=f"v1t{t % 2}")
            if not isctx:
                for c in range(2):
                    bank = 2 + pc[0] % 2
                    pc[0] += 1
                    for k in range(8):
                        MM(ps[bank][:, :], w[:, k, 2304 + c * 128:2304 + (c + 1) * 128], hT[:, k, :], k == 0, k == 7,
                           r=["w", hk], w=[("ps", bank)])
                    CP("act", fT[:, c, :], ps[bank][:, :], r=[("ps", bank)], w=["fT"])
                for t in range(4):
                    bank = 6 + t % 2
                    for c in range(2):
                        MM(ps[bank][:, :], fT[:, c, t * 128:(t + 1) * 128], CS[:, c, :], c == 0, c == 1, r=["fT", "CS"],
                           w=[("ps", bank)])
                    CP("dve", fcst[t % 2][:], ps[bank][:, :], r=[("ps", bank)], w=[("fcst", t % 2)])
                    row0 = b * 512 + t * 128
                    DMA("sp", fcsd[row0:row0 + 128, :], fcst[t % 2][:], r=[("fcst", t % 2)], slot=f"fc{t % 2}")
        S.barrier()

    def attn():
        S.reset_phase()
        kT = S.alloc([128, 6, NKEY], BF16)
        V1 = S.alloc([128, NKT, 774], BF16)
        lamt = S.alloc([128, 256], F32)
        SUB = S.alloc([128, 128], F32)
        qb = [S.alloc([128, 6, 512], BF16) for _ in range(2)]
        ob = [S.alloc([128, 6, 512], BF16) for _ in range(2)]
        PT = [S.alloc([128, 512], BF16) for _ in range(3)]
        a1 = S.alloc([128, 128], F32)
        att = S.alloc([128, 128], F32)
        attb = S.alloc([128, 128], BF16)
        junk = S.alloc([128, 128], BF16)
        sm = S.alloc([128, 16], F32)
        accS = S.alloc([128, 3, 387], F32)
        for h in range(6):
            DMA("sp", kT[:, h, :], kTd[h], w=["kT"], slot=f"kT{h % 2}")
        v1v = V1d.rearrange("(t p) c -> p t c", p=128)
        for q in range(2):
            DMA("sp", V1[:, q * 17:(q + 1) * 17, :], v1v[:, q * 17:(q + 1) * 17, :], w=["V1"], slot=f"V1{q}")
        DMA("sp", lamt[:], lam_d, w=["lamt"], slot="lam")
        DMA("sp", SUB[:], sub_d, w=["SUB"], slot="sub")
        TS("dve", SUB[:], SUB[:], 0.8, ALU.mult, r=["SUB"], w=["SUB"])
        TT("dve", lamt[:, 0:64], lamt[:, 0:64], lamt[:, 64:128], ALU.mult, r=["lamt"], w=["lamt"])
        TT("dve", lamt[:, 128:192], lamt[:, 128:192], lamt[:, 192:256], ALU.mult, r=["lamt"], w=["lamt"])
        S.op("dve", lambda e: e.reduce_sum(out=sm[:, 0:1], in_=lamt[:, 0:64], axis=AX.X), r=["lamt"], w=["sm"])
        S.op("dve", lambda e: e.reduce_sum(out=sm[:, 1:2], in_=lamt[:, 128:192], axis=AX.X), r=["lamt"], w=["sm"])
        ACT(sm[:, 0:2], sm[:, 0:2], AF.Exp, r=["sm"], w=["sm"])
        TT("dve", sm[:, 2:3], sm[:, 1:2], sm[:, 0:1], ALU.subtract, r=["sm"], w=["sm"])
        TS("dve", sm[:, 2:3], sm[:, 2:3], -0.2, ALU.add, r=["sm"], w=["sm"])
        nlam = sm[:, 2:3]
        accs = {}
        for m in range(2):
            for qs in range(4):
                j = m * 4 + qs
                accs[(m, qs)] = ps[3 + j // 3][:, (j % 3) * 129:(j % 3) * 129 + 129]
        for g in range(8):
            qT = qb[g % 2]
            DMA("sp", qT[:], qTd[:, :, g * 512:(g + 1) * 512].rearrange("h p n -> p h n"), w=[("qT", g % 2)], slot=f"qT{g % 2}")
            oT = ob[g % 2]
            for h in range(6):
                for m in range(2):
                    for kt in range(NKT):
                        b3 = kt % 3
                        MM(ps[b3][:, :], kT[m * 64:(m + 1) * 64, h, kt * 128:(kt + 1) * 128], qT[m * 64:(m + 1) * 64, h, :],
                           True, True, r=["kT", ("qT", g % 2)], w=[("ps", b3)])
                        ACT(PT[b3][:], ps[b3][:, :], AF.Exp, r=[("ps", b3)], w=[("PT", b3)])
                        for qs in range(4):
                            st_ = (kt == 0 and (m * 4 + qs) % 3 == 0)
                            S.op("pe", lambda e, o_=accs[(m, qs)], l_=PT[b3][:, qs * 128:(qs + 1) * 128],
                                 r_=V1[:, kt, h * 129:(h + 1) * 129], st_=st_, sp_=(kt == NKT - 1):
                                 e.matmul(o_, l_, r_, start=st_, stop=sp_, skip_group_check=True),
                                 r=[("PT", b3), "V1"], w=[("ps", 3 + (m * 4 + qs) // 3)])
                for bk in range(3):
                    wdt = 387 if bk < 2 else 258
                    CP("act", accS[:, bk, 0:wdt], ps[3 + bk][:, 0:wdt], r=[("ps", 3 + bk)], w=["accS"])
                for qs in range(4):
                    j0, j1 = qs, 4 + qs
                    A0 = accS[:, j0 // 3, (j0 % 3) * 129:(j0 % 3) * 129 + 129]
                    A1 = accS[:, j1 // 3, (j1 % 3) * 129:(j1 % 3) * 129 + 129]
                    RECIP(sm[:, 4:5], A0[:, 128:129], r=["accS"], w=["sm4"])
                    RECIP(sm[:, 5:6], A1[:, 128:129], r=["accS"], w=["sm5"])
                    TT("dve", sm[:, 6:7], sm[:, 5:6], nlam, ALU.mult, r=["sm5", "sm"], w=["sm6"])
                    TS("dve", a1[:], A0[:, 0:128], sm[:, 4:5], ALU.mult, r=["accS", "sm4"], w=["a1"])
                    STT("dve", att[:], A1[:, 0:128], sm[:, 6:7], a1[:], ALU.mult, ALU.add, r=["accS", "sm6", "a1"],
                        w=["att"])
                    ACT(junk[:], att[:], AF.Square, r=["att"], w=["junkA", "sm7"], accum_out=sm[:, 7:8])
                    ACT(sm[:, 8:9], sm[:, 7:8], AF.Sqrt, r=["sm7"], w=["sm8"], bias=EPS, scale=1.0 / 128.0)
                    RECIP(sm[:, 9:10], sm[:, 8:9], r=["sm8"], w=["sm9"])
                    STT("dve", attb[:], att[:], sm[:, 9:10], SUB[:], ALU.mult, ALU.mult, r=["att", "sm9", "SUB"], w=["attb"])
                    TR(psb[6][:, 0:128], attb[:], identb[:], r=["attb", "identb"], w=[("ps", 6)])
                    CP("act", oT[:, h, qs * 128:(qs + 1) * 128], psb[6][:, 0:128], r=[("ps", 6)], w=[("oT", g % 2)])
            DMA("sp", oTd[0:6, :, g * 512:(g + 1) * 512].rearrange("c p n -> p c n"), oT[:], r=[("oT", g % 2)], slot=f"oT{g % 2}")
        S.barrier()

    def fourier():
        S.reset_phase()
        fcs = S.alloc([128, 32, 512], BF16)
        cb = [S.alloc([128, 16, 512], BF16) for _ in range(2)]
        sb = [S.alloc([128, 16, 512], BF16) for _ in range(2)]
        oF = [S.alloc([128, 2, 512], BF16) for _ in range(2)]
        DMA("sp", fcs[:], fcsd.rearrange("(t p) c -> p t c", p=128), w=["fcs"], slot="fcs")
        cv = cos_d.rearrange("(t p) k -> p t k", p=128)
        sv = sin_d.rearrange("(t p) k -> p t k", p=128)
        n = 0
        for g in range(8):
            for half in range(2):
                cbb, sbb = cb[n % 2], sb[n % 2]
                DMA("sp", cbb[:], cv[:, half * 16:(half + 1) * 16, g * 512:(g + 1) * 512], w=[("cb", n % 2)], slot=f"cb{n % 2}")
                DMA("act", sbb[:], sv[:, half * 16:(half + 1) * 16, g * 512:(g + 1) * 512], w=[("sb", n % 2)], slot=f"sb{n % 2}")
                for c in range(2):
                    for t in range(16):
                        tt = half * 16 + t
                        MM(ps[c][:, :], fcs[:, tt, c * 128:(c + 1) * 128], cbb[:, t, :], tt == 0, False,
                           r=["fcs", ("cb", n % 2)], w=[("ps", c)])
                        MM(ps[c][:, :], fcs[:, tt, 256 + c * 128:256 + (c + 1) * 128], sbb[:, t, :], False, tt == 31,
                           r=["fcs", ("sb", n % 2)], w=[("ps", c)])
                n += 1
            for c in range(2):
                CP("dve" if c else "act", oF[g % 2][:, c, :], ps[c][:, :], r=[("ps", c)], w=[("oF", g % 2)])
            DMA("sp", oTd[6:8, :, g * 512:(g + 1) * 512].rearrange("c p n -> p c n"), oF[g % 2][:], r=[("oF", g % 2)],
                slot=f"oF{g % 2}")
        S.barrier()

    def outproj(l, wsrc, xin, xout):
        S.reset_phase()
        wo = S.alloc([128, 8, 1024], BF16)
        wr = S.alloc([128, 8, 16], F32)
        ob = [S.alloc([128, 8, 128], BF16) for _ in range(2)]
        xb = [S.alloc([128, 1024], F32) for _ in range(2)]
        yb = [S.alloc([128, 1024], F32) for _ in range(2)]
        x1b = [S.alloc([128, 1024], F32) for _ in range(2)]
        tmpb = [S.alloc([128, 1024], F32) for _ in range(2)]
        hfb = [S.alloc([128, 1024], F32) for _ in range(2)]
        hbb = [S.alloc([128, 1024], BF16) for _ in range(2)]
        hfT = [S.alloc([128, 8, 128], F32) for _ in range(2)]
        junk = S.alloc([128, 1024], BF16)
        stt = [S.alloc([128, 8], F32) for _ in range(2)]
        ex = S.alloc([128, 16], F32)
        wv = wsrc.rearrange("(k p) n -> p k n", p=128)
        for q in range(2):
            DMA("pool", wo[:, :, q * 512:(q + 1) * 512], wv[:, :, q * 512:(q + 1) * 512], w=["wo"], slot=f"wo{q}")
        DMA("sp", wr[:], rw_d[l].rearrange("(k p) e -> p k e", p=128), w=["wr"], slot="wr")
        for i in range(NT):
            i2 = i % 2
            DMA("sp", ob[i2][:], oTd[:, :, i * 128:(i + 1) * 128].rearrange("c p n -> p c n"), w=[("ob", i2)], slot=f"ob{i2}")
            DMA("sp", xb[i2][:], xin[i * 128:(i + 1) * 128, :], w=[("xb", i2)], slot=f"xb{i2}")
            for half in range(2):
                for c in range(8):
                    MM(ps[half][:, :], ob[i2][:, c, :], wo[:, c, half * 512:(half + 1) * 512], c == 0, c == 7,
                       r=[("ob", i2), "wo"], w=[("ps", half)])
                TT("dve", yb[i2][:, half * 512:(half + 1) * 512], ps[half][:, :], MOD[:, 2, half * 512:(half + 1) * 512],
                   ALU.mult, r=[("ps", half), "MOD"], w=[("yb", i2)])
            TT("pool", x1b[i2][:], yb[i2][:], xb[i2][:], ALU.add, r=[("yb", i2), ("xb", i2)], w=[("x1b", i2)])
            DMA("sp", xout[i * 128:(i + 1) * 128, :], x1b[i2][:], r=[("x1b", i2)], slot=f"x1o{i2}")
            modulate(x1b[i2][:], ("x1b", i2), MOD[:, 4, :], MOD[:, 3, :], hfb[i2][:], ("hfb", i2), stt[i2], ("st", i2),
                     tmpb[i2][:], ("tmp", i2), junk[:], "junk")
            CP("act", hbb[i2][:], hfb[i2][:], r=[("hfb", i2)], w=[("hbb", i2)])
            DMA("sp", hfd[i * 128:(i + 1) * 128, :], hbb[i2][:], r=[("hbb", i2)], slot=f"hfo{i2}")
            for k in range(8):
                bank = 2 + k // 4
                TR(ps[bank][:, (k % 4) * 128:(k % 4 + 1) * 128], hfb[i2][:, k * 128:(k + 1) * 128], identf[:],
                   r=[("hfb", i2), "identf"], w=[("ps", bank)])
            CP("act", hfT[i2][:, 0:4, :], ps[2][:, :].rearrange("p (k n) -> p k n", k=4), r=[("ps", 2)], w=[("hfT", i2)])
            CP("dve", hfT[i2][:, 4:8, :], ps[3][:, :].rearrange("p (k n) -> p k n", k=4), r=[("ps", 3)], w=[("hfT", i2)])
            for k in range(8):
                MM(ps[4][:, 0:16], hfT[i2][:, k, :], wr[:, k, :], k == 0, k == 7, r=[("hfT", i2), "wr"], w=[("ps", 4)])
            st = stt[i2]
            S.op("dve", lambda e, st=st: e.reduce_max(out=st[:, 4:5], in_=ps[4][:, 0:16], axis=AX.X), r=[("ps", 4)],
                 w=[("st5", i2)])
            TS("dve", st[:, 5:6], st[:, 4:5], -1.0, ALU.mult, r=[("st5", i2)], w=[("st6", i2)])
            ACT(ex[:], ps[4][:, 0:16], AF.Exp, r=[("ps", 4), ("st6", i2)], w=["ex", ("st7", i2)], bias=st[:, 5:6], scale=1.0,
                accum_out=st[:, 6:7])
            RECIP(st[:, 7:8], st[:, 6:7], r=[("st7", i2)], w=[("st8", i2)])
            TS("dve", AFF[:, i, :], ex[:], st[:, 7:8], ALU.mult, r=["ex", ("st8", i2)], w=["AFF"])
        S.barrier()

    def moe(l):
        S.reset_phase()
        slotm = S.alloc([128, 32, 16], F32)
        VALS = S.alloc([128, 32, 16, 5], BF16)
        mark = S.sb_off
        affT = S.alloc([16, T], F32)
        junkb = S.alloc([16, T], BF16)
        maskT = S.alloc([16, T], BF16)
        bs = S.alloc([16, 8], F32)
        MASK = S.alloc([128, 512], F32)
        MASKb = S.alloc([128, 512], BF16)
        tots = S.alloc([128, 32, 16], F32)
        base = S.alloc([128, 32, 16], F32)
        r1 = S.alloc([128, 512], F32)
        gtmp = S.alloc([128, 512], BF16)
        zt = S.alloc([128, 2, 1024], F32)
        AFFf = AFF.rearrange("p a b -> p (a b)")
        MSET("pool", zt[:], 0.0, w=["zt"])
        yv = ymoe.rearrange("(t p) d -> p t d", p=128)
        for q in range(16):
            DMA("sp", yv[:, q * 2:(q + 1) * 2, :], zt[:], r=["zt"], w=[("ymoe", 0)], slot="zy")
        for i in range(NT):
            bank = i // 4 % 2
            TR(ps[bank][0:16, (i % 4) * 128:(i % 4 + 1) * 128], AFF[:, i, :], identf[:], r=["AFF", "identf"], w=[("ps", bank)])
            if i % 4 == 3:
                CP("act", affT[:, (i // 4) * 512:(i // 4 + 1) * 512], ps[bank][0:16, :], r=[("ps", bank)], w=["affT"])
        MSET("dve", bs[:], 0.0, w=["bs"])
        for it in range(28):
            step = 2.0 ** -(it + 1)
            TS("dve", bs[:, 1:2], bs[:, 0:1], step, ALU.add, r=["bs"], w=["bs1"])
            TS("dve", junkb[:], affT[:], bs[:, 1:2], ALU.is_gt, 0.0, ALU.add, r=["affT", "bs1"], w=["junkb", "bs2"],
               accum=bs[:, 2:3])
            TS("dve", bs[:, 3:4], bs[:, 2:3], 511.5, ALU.is_gt, step, ALU.mult, r=["bs2"], w=["bs3"])
            TT("dve", bs[:, 0:1], bs[:, 0:1], bs[:, 3:4], ALU.add, r=["bs", "bs3"], w=["bs"])
        TS("dve", maskT[:], affT[:], bs[:, 0:1], ALU.is_gt, r=["affT", "bs"], w=["maskT"])
        for i in range(NT):
            TR(psb[2][:, i * 16:(i + 1) * 16], maskT[:, i * 128:(i + 1) * 128], identb[0:16, 0:16], r=["maskT", "identb"],
               w=[("ps", 2)])
        CP("act", MASKb[:], psb[2][:, 0:512], r=[("ps", 2)], w=["MASKb"])
        CP("dve", MASK[:], MASKb[:], r=["MASKb"], w=["MASK"])
        MM(ps[3][:, :], Ust[:], MASKb[:], True, True, r=["Ust", "MASKb"], w=[("ps", 3)])
        MM(ps[4][:, :], onesb[:], MASKb[:], True, True, r=["onesb", "MASKb"], w=[("ps", 4)])
        CP("act", tots.rearrange("p a b -> p (a b)"), ps[4][:, :], r=[("ps", 4)], w=["tots"])
        MSET("dve", base[:, 0, :], 0.0, w=["base"])
        for i in range(1, NT):
            TT("dve", base[:, i, :], base[:, i - 1, :], tots[:, i - 1, :], ALU.add, r=["base", "tots"], w=["base"])
        sf = slotm.rearrange("p a b -> p (a b)")
        TT("dve", sf, ps[3][:, :], base.rearrange("p a b -> p (a b)"), ALU.add, r=[("ps", 3), "base"], w=["slotm"])
        STT("dve", sf, sf, 1.0, MASK[:], ALU.add, ALU.mult, r=["slotm", "MASK"], w=["slotm"])
        TS("dve", sf, sf, -1.0, ALU.add, r=["slotm"], w=["slotm"])
        CP("pool", VALS[:, :, :, 0:2], tokhl[:], r=["tokhl"], w=["VALS"])
        Vg = lambda j: VALS[:, :, :, j].rearrange("p a b -> p (a b)")
        CP("dve", gtmp[:], AFFf, r=["AFF"], w=["gtmp"])
        CP("dve", VALS[:, :, :, 2], gtmp.rearrange("p (a b) -> p a b", a=32), r=["gtmp"], w=["VALS"])
        TT("dve", r1[:], AFFf, gtmp[:], ALU.subtract, r=["AFF", "gtmp"], w=["r1"])
        CP("dve", gtmp[:], r1[:], r=["r1"], w=["gtmp"])
        CP("dve", VALS[:, :, :, 3], gtmp.rearrange("p (a b) -> p a b", a=32), r=["gtmp"], w=["VALS"])
        TT("dve", r1[:], r1[:], gtmp[:], ALU.subtract, r=["r1", "gtmp"], w=["r1"])
        CP("dve", VALS[:, :, :, 4], r1.rearrange("p (a b) -> p a b", a=32), r=["r1"], w=["VALS"])

        S.barrier()
        S.sb_off = mark
        wgb = [S.alloc([128, 8, 1024], BF16) for _ in range(2)]
        wub = [S.alloc([128, 8, 1024], BF16) for _ in range(2)]
        wdb = [S.alloc([128, 8, 1024], BF16) for _ in range(2)]
        selb = [S.alloc([128, 512], BF16) for _ in range(4)]
        iv = S.alloc([5, 512], F32)
        ivt = S.alloc([128, 4, 8], F32)
        tokf = S.alloc([128, 4], F32)
        idx = [S.alloc([128, 4], I32) for _ in range(2)]
        gg = [S.alloc([128, 4], F32) for _ in range(2)]
        xs = [S.alloc([128, 1024], BF16) for _ in range(4)]
        xsT = S.alloc([128, 8, 512], BF16)
        sg = [S.alloc([128, 512], F32) for _ in range(2)]
        aT = S.alloc([128, 8, 512], BF16)
        ysb = [S.alloc([128, 1024], F32) for _ in range(2)]
        yc = 0
        for ex_ in range(16):
            e2 = ex_ % 2
            for (buf, src, nm) in ((wgb, wg_d, "wg"), (wub, wu_d, "wu"), (wdb, wd_d, "wd")):
                sv = src[l, ex_].rearrange("(k p) n -> p k n", p=128)
                for q in range(2):
                    DMA("pool", buf[e2][:, :, q * 512:(q + 1) * 512], sv[:, :, q * 512:(q + 1) * 512], w=[(nm, e2)],
                        slot=f"{nm}{e2}{q}")
            for i in range(NT):
                sb_ = selb[i % 4]
                TS("dve" if i % 2 else "pool", sb_[:], iota[:], slotm[:, i, ex_:ex_ + 1], ALU.is_equal,
                   r=["iota", "slotm"], w=[("selb", i % 4)])
                MM(ps[5][0:5, :], VALS[:, i, ex_, :], sb_[:], i == 0, i == NT - 1, r=["VALS", ("selb", i % 4)], w=[("ps", 5)])
            CP("act", iv[:], ps[5][0:5, :], r=[("ps", 5)], w=["iv"])
            for grp in range(4):
                TR(ps[6][:, grp * 8:grp * 8 + 5], iv[0:5, grp * 128:(grp + 1) * 128], identf[0:5, 0:5], r=["iv", "identf"],
                   w=[("ps", 6)])
            for grp in range(4):
                CP("dve", ivt[:, grp, 0:5], ps[6][:, grp * 8:grp * 8 + 5], r=[("ps", 6)], w=["ivt"])
            STT("dve", tokf[:], ivt[:, :, 0], 64.0, ivt[:, :, 1], ALU.mult, ALU.add, r=["ivt"], w=["tokf"])
            CP("dve", idx[e2][:], tokf[:], r=["tokf"], w=[("idx", e2)])
            TT("dve", gg[e2][:], ivt[:, :, 2], ivt[:, :, 3], ALU.add, r=["ivt"], w=[("gg", e2)])
            TT("dve", gg[e2][:], gg[e2][:], ivt[:, :, 4], ALU.add, r=["ivt", ("gg", e2)], w=[("gg", e2)])
            for grp in range(4):
                S.op("pool", lambda e, grp=grp, e2=e2: e.indirect_dma_start(
                    out=xs[grp][:], out_offset=None, in_=hfd,
                    in_offset=bass.IndirectOffsetOnAxis(ap=idx[e2][:, grp:grp + 1], axis=0)),
                    r=[("idx", e2)], w=[("xs", grp)], dma=f"xs{grp}")
                pT = psb[7]
                for k in range(8):
                    TR(pT[:, k * 128:(k + 1) * 128], xs[grp][:, k * 128:(k + 1) * 128], identb[:], r=[("xs", grp), "identb"],
                       w=[("ps", 7)])
                CP("act" if grp % 2 else "dve", xsT[:, :, grp * 128:(grp + 1) * 128],
                   pT[:, 0:1024].rearrange("p (k n) -> p k n", k=8), r=[("ps", 7)], w=["xsT"])
            for fc in range(8):
                for k in range(8):
                    MM(ps[0][:, :], wgb[e2][:, k, fc * 128:(fc + 1) * 128], xsT[:, k, :], k == 0, k == 7, r=[("wg", e2), "xsT"],
                       w=[("ps", 0)])
                for k in range(8):
                    MM(ps[1][:, :], wub[e2][:, k, fc * 128:(fc + 1) * 128], xsT[:, k, :], k == 0, k == 7, r=[("wu", e2), "xsT"],
                       w=[("ps", 1)])
                ACT(sg[fc % 2][:], ps[0][:, :], AF.Silu, r=[("ps", 0)], w=[("sg", fc % 2)])
                TT("dve", aT[:, fc, :], sg[fc % 2][:], ps[1][:, :], ALU.mult, r=[("sg", fc % 2), ("ps", 1)], w=["aT"])
            for grp in range(4):
                y2 = yc % 2
                yc += 1
                for half in range(2):
                    bank = 2 + half
                    for fc in range(8):
                        MM(ps[bank][:, :], aT[:, fc, grp * 128:(grp + 1) * 128], wdb[e2][:, fc, half * 512:(half + 1) * 512],
                           fc == 0, fc == 7, r=["aT", ("wd", e2)], w=[("ps", bank)])
                    if half:
                        ACT(ysb[y2][:, 512:1024], ps[bank][:, :], AF.Copy, r=[("ps", bank), ("gg", e2)], w=[("ysb", y2)],
                            scale=gg[e2][:, grp:grp + 1])
                    else:
                        TS("dve", ysb[y2][:, 0:512], ps[bank][:, :], gg[e2][:, grp:grp + 1], ALU.mult,
                           r=[("ps", bank), ("gg", e2)], w=[("ysb", y2)])
                S.op("pool", lambda e, grp=grp, e2=e2, y2=y2: e.indirect_dma_start(
                    out=ymoe, out_offset=bass.IndirectOffsetOnAxis(ap=idx[e2][:, grp:grp + 1], axis=0),
                    in_=ysb[y2][:], in_offset=None, compute_op=ALU.add),
                    r=[("idx", e2), ("ysb", y2), ("ymoe", ex_)], w=[("ymoe", ex_ + 1), ("ymoeg", grp)], dma=f"ys{y2}")
        S.barrier()

    def combine(src, dst):
        S.reset_phase()
        xb = [S.alloc([128, 1024], F32) for _ in range(2)]
        yb = [S.alloc([128, 1024], F32) for _ in range(2)]
        ob = [S.alloc([128, 1024], F32) for _ in range(2)]
        for i in range(NT):
            i2 = i % 2
            DMA("sp", xb[i2][:], src[i * 128:(i + 1) * 128, :], w=[("xb", i2)], slot=f"cx{i2}")
            DMA("act", yb[i2][:], ymoe[i * 128:(i + 1) * 128, :], w=[("yb", i2)], slot=f"cy{i2}")
            TT("dve", yb[i2][:], yb[i2][:], MOD[:, 5, :], ALU.mult, r=[("yb", i2), "MOD"], w=[("yb", i2)])
            TT("pool", ob[i2][:], yb[i2][:], xb[i2][:], ALU.add, r=[("yb", i2), ("xb", i2)], w=[("ob", i2)])
            DMA("sp", dst[i * 128:(i + 1) * 128, :], ob[i2][:], r=[("ob", i2)], slot=f"co{i2}")
        S.barrier()

    def conv():
        S.reset_phase()
        hT = S.alloc([128, 8, T], BF16)
        xb = [S.alloc([128, 1024], F32) for _ in range(2)]
        tmpb = [S.alloc([128, 1024], F32) for _ in range(2)]
        hb = [S.alloc([128, 1024], BF16) for _ in range(2)]
        junk = S.alloc([128, 1024], BF16)
        stt = [S.alloc([128, 4], F32) for _ in range(2)]
        cw = S.alloc([128, 8, 3], F32)
        w3 = [S.alloc([128, 8, 3, 128], BF16) for _ in range(2)]
        z = S.alloc([128, T + 2], F32)
        tt_ = S.alloc([128, T], F32)
        bgs = S.alloc([128, T], BF16)
        cgs = [S.alloc([128, 512], F32) for _ in range(2)]
        vT = [S.alloc([128, T], BF16) for _ in range(2)]
        DMA("sp", cw[:], convw_d, w=["cw"], slot="cw")
        MSET("pool", z[:], 0.0, w=["z"])
        for i in range(NT):
            i2 = i % 2
            DMA("sp", xb[i2][:], x2d[i * 128:(i + 1) * 128, :], w=[("xb", i2)], slot=f"xb{i2}")
            modulate(xb[i2][:], ("xb", i2), MOD[:, 1, :], MOD[:, 0, :], hb[i2][:], ("hb", i2), stt[i2], ("st", i2),
                     tmpb[i2][:], ("tmp", i2), junk[:], "junk")
            pT = psb[i % 2]
            for k in range(8):
                TR(pT[:, k * 128:(k + 1) * 128], hb[i2][:, k * 128:(k + 1) * 128], identb[:], r=[("hb", i2), "identb"],
                   w=[("ps", i % 2)])
            CP("act", hT[:, :, i * 128:(i + 1) * 128], pT[:, 0:1024].rearrange("p (k n) -> p k n", k=8), r=[("ps", i % 2)],
               w=["hT"])
        wv = cwin_d.rearrange("(k p) n -> p k n", p=128)
        for fc in range(8):
            f2 = fc % 2
            for j in range(3):
                DMA("pool", w3[f2][:, :, j, :], wv[:, :, j * 1024 + fc * 128:j * 1024 + (fc + 1) * 128], w=[("w3", f2)],
                    slot=f"w3{f2}{j}")
            for b in range(8):
                for j in range(3):
                    bank = 2 + j * 2 + b % 2
                    for k in range(8):
                        MM(ps[bank][:, :], w3[f2][:, k, j, :], hT[:, k, b * 512:(b + 1) * 512], k == 0, k == 7,
                           r=[("w3", f2), "hT"], w=[("ps", bank)])
                CP("act", bgs[:, b * 512:(b + 1) * 512], ps[2 + b % 2][:, :], r=[("ps", 2 + b % 2)], w=["bgs"])
                CP("act", cgs[b % 2][:], ps[4 + b % 2][:, :], r=[("ps", 4 + b % 2)], w=[("cgs", b % 2)])
                TT("dve", z[:, 1 + b * 512:1 + (b + 1) * 512], cgs[b % 2][:], ps[6 + b % 2][:, :], ALU.mult,
                   r=[("cgs", b % 2), ("ps", 6 + b % 2)], w=["z"])
            ACT(tt_[:], z[:, 0:T], AF.Copy, r=["z", "cw"], w=["tt"], scale=cw[:, fc, 0:1])
            STT("dve", tt_[:], z[:, 1:T + 1], cw[:, fc, 1:2], tt_[:], ALU.mult, ALU.add, r=["z", "cw", "tt"], w=["tt"])
            STT("dve", tt_[:], z[:, 2:T + 2], cw[:, fc, 2:3], tt_[:], ALU.mult, ALU.add, r=["z", "cw", "tt"], w=["tt"])
            TT("pool", vT[f2][:], bgs[:], tt_[:], ALU.mult, r=["bgs", "tt"], w=[("vT", f2)])
            DMA("sp", oTd[fc], vT[f2][:], r=[("vT", f2)], slot=f"vT{f2}")
        S.barrier()

    phases = [
        lambda: ada(0, True),
        proj0,
        attn,
        fourier,
        lambda: outproj(0, wout_d, x_d, x1d),
        lambda: moe(0),
        lambda: combine(x1d, x2d),
        lambda: ada(1, False),
        conv,
        lambda: outproj(1, cwout_d, x2d, x3d),
        lambda: moe(1),
        lambda: combine(x3d, out_d),
    ]
    for i, ph in enumerate(phases):
        if stop is not None and i >= stop:
            break
        ph()
    S.emit()
    return nc, S


import ml_dtypes

_CACHE = {}


def _consts():
    if "c" in _CACHE:
        return _CACHE["c"]
    bf = ml_dtypes.bfloat16
    cpk = np.zeros((128, 1152), np.float32)
    cpk[:, 0:128] = np.eye(128)
    bo = np.zeros((128, 128), np.float32)
    bo[0:64, 0:64] = 1.0
    bo[64:128, 64:128] = 1.0
    cpk[:, 128:256] = bo
    R = np.zeros((64, 64), np.float32)
    for j in range(64):
        q = j // 16
        if q % 2 == 0:
            R[j, j + 16] = -1.0
        else:
            R[j, j - 16] = 1.0
    R2 = np.zeros((128, 128), np.float32)
    R2[0:64, 0:64] = R
    R2[64:128, 64:128] = R
    cpk[:, 256:384] = R2.T
    cpk[:, 384:512] = np.triu(np.ones((128, 128), np.float32), 1)
    cpk[:, 512:640] = 1.0
    cpk[:, 640:1152] = np.arange(512, dtype=np.float32)[None, :]
    tok = (np.arange(32)[None, :] * 128 + np.arange(128)[:, None])
    tokhl = np.zeros((128, 32, 16, 2), np.float32)
    tokhl[:, :, :, 0] = (tok // 64)[:, :, None]
    tokhl[:, :, :, 1] = (tok % 64)[:, :, None]
    sel = np.zeros((2, 256), np.float32)
    sel[0, 0:128] = 1.0
    sel[1, 128:256] = 1.0
    n = 4096
    r = np.repeat(np.arange(n // 64, dtype=np.float32), 64)
    col = np.tile(np.arange(64, dtype=np.float32), n // 64)
    inv = (np.float32(10000.0) ** (-np.arange(16, dtype=np.float32) / np.float32(16))).astype(np.float32)
    ar = r[:, None] * inv
    ac = col[:, None] * inv
    ang = np.concatenate([ar, ar, ac, ac], axis=-1).astype(np.float32)
    cosT = np.cos(ang).astype(np.float32).T
    sinT = np.sin(ang).astype(np.float32).T
    rope = np.stack([np.concatenate([cosT, cosT], 0), np.concatenate([sinT, sinT], 0)]).astype(np.float32)
    cm = np.arange(64)[:, None] * np.arange(64)[None, :]
    cb = np.cos(2 * np.pi * (cm % 64) / 64.0) / 512.0
    sb = np.sin(2 * np.pi * (cm % 64) / 64.0) / 512.0
    csd = np.zeros((256, 512), np.float32)
    for g in range(4):
        csd[g * 64:(g + 1) * 64, g * 64:(g + 1) * 64] = cb
        csd[g * 64:(g + 1) * 64, 256 + g * 64:256 + (g + 1) * 64] = sb
    kn = (np.arange(n, dtype=np.int64)[:, None] * np.arange(n, dtype=np.int64)[None, :]) % n
    angp = kn.astype(np.float64) * (2 * np.pi / n)
    cosd = np.cos(angp).astype(bf)
    sind = (-np.sin(angp)).astype(bf)
    c = dict(cpk=cpk, tokhl=tokhl.reshape(128, 1024), sel=sel, rope=rope, csd=csd, cosd=cosd, sind=sind)
    _CACHE["c"] = c
    return c


def kernel(x, c, ctx, c_ctx, ada_w, ada_b, norm_mix, norm_ffn, attn_w_in, attn_q_norm, attn_k_norm,
           lam_q1, lam_k1, lam_q2, lam_k2, attn_subln, attn_w_out, conv_w_in, conv_w, conv_w_out,
           router_w, moe_w_gate, moe_w_up, moe_w_down, _stop=None, _debug=False, _ncores=4):
    f = lambda a: np.ascontiguousarray(np.asarray(a, dtype=np.float32))
    x, c, ctx, c_ctx = f(x), f(c), f(ctx), f(c_ctx)
    K = _consts()
    ck = ("nc", _stop, _debug)
    if ck not in _CACHE:
        _CACHE[ck] = build_nc(_stop, _debug)[0]
    nc = _CACHE[ck]
    shared = dict(K)
    shared["ada_w"] = f(ada_w)
    shared["adab2"] = f(np.repeat(f(ada_b)[:, None, :], 2, axis=1))
    shared["nmrep"] = f(np.repeat(f(norm_mix)[:, None, :], 128, axis=1))
    shared["nfrep"] = f(np.repeat(f(norm_ffn)[:, None, :], 128, axis=1))
    shared["attn_w_in"] = f(attn_w_in)[0]
    shared["attn_w_out"] = f(attn_w_out)[0]
    shared["conv_w_in"] = f(conv_w_in)[0]
    shared["conv_w_out"] = f(conv_w_out)[0]
    shared["qkcol"] = f(np.stack([np.tile(f(attn_q_norm)[0], 2), np.tile(f(attn_k_norm)[0], 2)], axis=1))
    lamrow = np.concatenate([f(lam_q1)[0], f(lam_k1)[0], f(lam_q2)[0], f(lam_k2)[0]])
    shared["lamrep"] = f(np.repeat(lamrow[None, :], 128, axis=0))
    shared["sublnrep"] = f(np.repeat(f(attn_subln)[0][None, :], 128, axis=0))
    shared["convw"] = f(f(conv_w)[0].T.reshape(8, 128, 3).transpose(1, 0, 2))
    shared["router_w"] = f(router_w)
    shared["moe_w_gate"] = f(moe_w_gate)
    shared["moe_w_up"] = f(moe_w_up)
    shared["moe_w_down"] = f(moe_w_down)
    in_maps = []
    for b in range(_ncores):
        m = dict(shared)
        m["x"] = x[b]
        m["ctx"] = ctx[b]
        c2 = np.stack([c[b].reshape(8, 128).T, c_ctx.reshape(8, 128).T], axis=-1)
        m["c2"] = f(c2)
        in_maps.append(m)
    res = run_bass_kernel_spmd(nc, in_maps, core_ids=list(range(_ncores)))
    _CACHE["res"] = res
    if _debug:
        return res
    return np.stack([np.asarray(res.results[b]["out"], dtype=np.float32) for b in range(4)], axis=0)
```

```python
from concourse.bass_utils import run_bass_kernel_spmd
import numpy as np
import concourse.bass as bass
import concourse.mybir as mybir

F32 = mybir.dt.float32
BF16 = mybir.dt.bfloat16
I32 = mybir.dt.int32
ALU = mybir.AluOpType
AF = mybir.ActivationFunctionType
AX = mybir.AxisListType
ENG = ("pe", "act", "dve", "pool", "sp")


class Op:
    __slots__ = ("eng", "fn", "deps", "sig", "sem", "val", "dma", "waits", "semkey")


class Sched:
    def __init__(self, nc):
        self.nc = nc
        self.streams = {e: [] for e in ENG}
        self.lw = {}
        self.lr = {}
        self.pending_dma = []
        self.last_real = {}
        self.slot_sem = {}
        self.slot_cnt = {}
        self.nsem = 0
        self.sb_off = 0
        self.sb_base = 0
        self.uid = 0

    def alloc(self, shape, dtype, name="t"):
        if not hasattr(self, "views"):
            big = self.nc.alloc_sbuf_tensor("arena", [128, 103 * 1024], BF16)
            self.views = {BF16: big, F32: big.bitcast(F32), I32: big.bitcast(I32)}
        esz = mybir.dt.size(dtype)
        n = int(np.prod(shape[1:]))
        off = (self.sb_off + 63) // 64 * 64
        self.sb_off = off + n * esz
        assert self.sb_off <= 206 * 1024, (name, self.sb_off)
        ap = self.views[dtype][0:shape[0], off // esz: off // esz + n]
        if len(shape) == 3:
            ap = ap.rearrange("p (a b) -> p a b", a=shape[1])
        elif len(shape) == 4:
            ap = ap.rearrange("p (a b c) -> p a b c", a=shape[1], b=shape[2])
        return ap

    def mark_persistent(self):
        self.sb_base = self.sb_off

    def reset_phase(self):
        self.sb_off = self.sb_base

    def newsem(self, name):
        self.nsem += 1
        return self.nc.alloc_semaphore(f"{name}_{self.nsem}")

    def op(self, eng, fn, r=(), w=(), dma=None):
        o = Op()
        o.eng, o.fn, o.sig, o.dma = eng, fn, False, dma
        o.sem = None
        o.val = 0
        deps = {}
        for k in r:
            for d in self.lw.get(k, ()):
                deps[id(d)] = d
        for k in w:
            for d in self.lw.get(k, ()):
                if d.dma or dma or d.eng != eng:
                    deps[id(d)] = d
            for d in self.lr.get(k, ()):
                if d.dma or dma or d.eng != eng:
                    deps[id(d)] = d
        o.deps = list(deps.values())
        for d in o.deps:
            d.sig = True
        for k in w:
            self.lw[k] = [o]
            self.lr[k] = []
        for k in r:
            lst = self.lr.setdefault(k, [])
            if not dma:
                lst[:] = [x for x in lst if x.dma or x.eng != eng]
            lst.append(o)
        self.streams[eng].append(o)
        if dma:
            o.sig = True
            self.pending_dma.append(o)
        elif fn is not None:
            self.last_real[eng] = o
        return o

    def barrier(self):
        lasts = list(self.last_real.values()) + list(self.pending_dma)
        for d in lasts:
            d.sig = True
        for e in ENG:
            o = Op()
            o.eng, o.fn, o.sig, o.dma, o.sem, o.val = e, None, False, None, None, 0
            o.deps = [d for d in lasts if d.dma or d.eng != e]
            self.streams[e].append(o)
        self.lw, self.lr, self.pending_dma = {}, {}, []

    def finalize(self):
        for e in ENG:
            cnt = 0
            cur = None
            for o in self.streams[e]:
                if o.dma:
                    key = (e, o.dma)
                    if key not in self.slot_sem:
                        self.slot_sem[key] = self.newsem("d")
                        self.slot_cnt[key] = 0
                    self.slot_cnt[key] += 16
                    o.sem, o.val, o.semkey = self.slot_sem[key], self.slot_cnt[key], ("d",) + key
                elif o.sig:
                    if cur is None or cnt >= 30000:
                        cur = self.newsem("e" + e)
                        curkey = ("e", e, self.nsem)
                        cnt = 0
                    cnt += 1
                    o.sem, o.val, o.semkey = cur, cnt, curkey
        nw = 0
        for e in ENG:
            known = {}
            for o in self.streams[e]:
                need = {}
                for d in o.deps:
                    assert d.sem is not None
                    if known.get(d.semkey, 0) < d.val:
                        if need.get(d.semkey, (None, 0))[1] < d.val:
                            need[d.semkey] = (d.sem, d.val)
                for k, (sm, v) in need.items():
                    known[k] = v
                o.waits = list(need.values())
                nw += len(o.waits)
        self.nwaits = nw

    def replay(self, eng, e):
        for o in self.streams[eng]:
            for sm, v in o.waits:
                e.wait_ge(sm, v)
            if o.fn is not None:
                ins = o.fn(e)
                if o.sig:
                    ins.then_inc(o.sem, 16 if o.dma else 1)

    def emit(self):
        self.barrier()
        self.finalize()
        with self.nc.Block() as blk:
            @blk.tensor
            def _(e):
                self.replay("pe", e)

            @blk.scalar
            def _(e):
                self.replay("act", e)

            @blk.vector
            def _(e):
                self.replay("dve", e)

            @blk.gpsimd
            def _(e):
                self.replay("pool", e)

            @blk.sync
            def _(e):
                self.replay("sp", e)


EPS = 1e-6
T = 4096
NT = 32
NKEY = 4352
NKT = 34
DEBUG = False


def build_nc(stop=None, debug=False):
    nc = bass.Bass("TRN2", target_bir_lowering=False)
    S = Sched(nc)

    def din(name, shape, dt=F32):
        return nc.dram_tensor(name, list(shape), dt, kind="ExternalInput").ap()

    def dscr(name, shape, dt, out=False):
        return nc.dram_tensor(name, list(shape), dt, kind="ExternalOutput" if debug else "Internal").ap()

    x_d = din("x", [T, 1024])
    ctx_d = din("ctx", [256, 1024])
    c2_d = din("c2", [128, 8, 2])
    adaw_d = din("ada_w", [2, 1024, 6144])
    adab_d = din("adab2", [2, 2, 6144])
    nm_d = din("nmrep", [2, 128, 1024])
    nf_d = din("nfrep", [2, 128, 1024])
    win_d = din("attn_w_in", [1024, 2560])
    wout_d = din("attn_w_out", [1024, 1024])
    cwin_d = din("conv_w_in", [1024, 3072])
    cwout_d = din("conv_w_out", [1024, 1024])
    qk_d = din("qkcol", [128, 2])
    lam_d = din("lamrep", [128, 256])
    sub_d = din("sublnrep", [128, 128])
    convw_d = din("convw", [128, 8, 3])
    rw_d = din("router_w", [2, 1024, 16])
    wg_d = din("moe_w_gate", [2, 16, 1024, 1024])
    wu_d = din("moe_w_up", [2, 16, 1024, 1024])
    wd_d = din("moe_w_down", [2, 16, 1024, 1024])
    cpk_d = din("cpk", [128, 1152])
    tok_d = din("tokhl", [128, 1024])
    sel_d = din("sel", [2, 256])
    rope_d = din("rope", [2, 128, T])
    cs_d = din("csd", [256, 512])
    cos_d = din("cosd", [T, T], BF16)
    sin_d = din("sind", [T, T], BF16)
    out_d = nc.dram_tensor("out", [T, 1024], F32, kind="ExternalOutput").ap()

    kTd = dscr("kTd", [6, 128, NKEY], BF16)
    qTd = dscr("qTd", [6, 128, T], BF16)
    V1d = dscr("V1d", [NKEY, 774], BF16)
    fcsd = dscr("fcsd", [T, 512], BF16)
    oTd = dscr("oTd", [8, 128, T], BF16)
    x1d = dscr("x1d", [T, 1024], F32, DEBUG)
    x2d = dscr("x2d", [T, 1024], F32, DEBUG)
    x3d = dscr("x3d", [T, 1024], F32, DEBUG)
    hfd = dscr("hfd", [T, 1024], BF16)
    ymoe = dscr("ymoe", [T, 1024], F32)

    pbig = [nc.alloc_psum_tensor(f"pq{j}", [128, 1024], F32) for j in range(4)]
    pbigb = [p.bitcast(BF16) for p in pbig]
    ps = [pbig[i // 2][:, (i % 2) * 512:(i % 2 + 1) * 512] for i in range(8)]
    psb = [pbigb[i // 2][:, (i % 2) * 1024:(i % 2 + 1) * 1024] for i in range(8)]

    def DMA(eng, out, in_, r=(), w=(), slot=None):
        S.op(eng, lambda e: e.dma_start(out=out, in_=in_), r=r, w=w, dma=slot)

    def MM(out, lhsT, rhs, start, stop, r=(), w=()):
        S.op("pe", lambda e: e.matmul(out, lhsT, rhs, start=start, stop=stop), r=r, w=w)

    def TR(out, in_, ident, r=(), w=()):
        S.op("pe", lambda e: e.transpose(out, in_, ident), r=r, w=w)

    def ACT(out, in_, func, r=(), w=(), **kw):
        S.op("act", lambda e: e.activation(out=out, in_=in_, func=func, **kw), r=r, w=w)

    def TT(eng, out, in0, in1, op, r=(), w=()):
        S.op(eng, lambda e: e.tensor_tensor(out=out, in0=in0, in1=in1, op=op), r=r, w=w)

    def TS(eng, out, in0, s1, op0, s2=None, op1=None, r=(), w=(), accum=None):
        if op1 is None:
            S.op(eng, lambda e: e.tensor_single_scalar(out=out, in_=in0, scalar=s1, op=op0), r=r, w=w)
        else:
            S.op(eng, lambda e: e.tensor_scalar(out=out, in0=in0, scalar1=s1, scalar2=s2, op0=op0, op1=op1,
                                                accum_out=accum), r=r, w=w)

    def STT(eng, out, in0, scalar, in1, op0, op1, r=(), w=()):
        S.op(eng, lambda e: e.scalar_tensor_tensor(out=out, in0=in0, scalar=scalar, in1=in1, op0=op0, op1=op1),
             r=r, w=w)

    def CP(eng, out, in_, r=(), w=()):
        if eng == "act":
            ACT(out, in_, AF.Copy, r=r, w=w)
        else:
            S.op(eng, lambda e: e.tensor_copy(out=out, in_=in_), r=r, w=w)

    def RECIP(out, in_, r=(), w=()):
        S.op("dve", lambda e: e.reciprocal(out=out, in_=in_), r=r, w=w)

    def MSET(eng, ap, v, w=()):
        S.op(eng, lambda e: e.memset(ap, v), w=w)

    identf = S.alloc([128, 128], F32)
    identb = S.alloc([128, 128], BF16)
    bones = S.alloc([128, 128], BF16)
    RT = S.alloc([128, 128], BF16)
    Ust = S.alloc([128, 128], BF16)
    onesb = S.alloc([128, 128], BF16)
    iota = S.alloc([128, 512], F32)
    tokhl = S.alloc([128, 32, 16, 2], BF16)
    sel = S.alloc([2, 256], F32)
    MOD = S.alloc([128, 6, 1024], F32)
    CMOD = S.alloc([128, 2, 1024], F32)
    AFF = S.alloc([128, 32, 16], F32)
    small = S.alloc([128, 64], F32)
    DMA("sp", identf[:], cpk_d[:, 0:128], w=["identf"], slot="c0")
    DMA("sp", iota[:], cpk_d[:, 640:1152], w=["iota"], slot="c1")
    DMA("sp", sel[:], sel_d, w=["sel"], slot="c2")
    DMA("pool", identb[:], cpk_d[:, 0:128], w=["identb"], slot="c3")
    DMA("pool", bones[:], cpk_d[:, 128:256], w=["bones"], slot="c4")
    DMA("pool", RT[:], cpk_d[:, 256:384], w=["RT"], slot="c5")
    DMA("pool", Ust[:], cpk_d[:, 384:512], w=["Ust"], slot="c6")
    DMA("pool", onesb[:], cpk_d[:, 512:640], w=["onesb"], slot="c7")
    DMA("pool", tokhl.rearrange("p a b c -> p (a b c)"), tok_d, w=["tokhl"], slot="c8")
    S.mark_persistent()
    S.barrier()

    def ada(l, with_ctx):
        S.reset_phase()
        sc = S.alloc([128, 8, 2], F32)
        m2 = S.alloc([2, 6144], F32)
        ab = S.alloc([2, 6144], F32)
        wb = [S.alloc([128, 8, 512], F32) for _ in range(2)]
        nmt = S.alloc([128, 1024], F32)
        nft = S.alloc([128, 1024], F32)
        DMA("sp", sc[:], c2_d, w=["sc"], slot="a0")
        DMA("sp", ab[:], adab_d[l], w=["ab"], slot="a1")
        DMA("sp", nmt[:], nm_d[l], w=["nmt"], slot="a2")
        DMA("sp", nft[:], nf_d[l], w=["nft"], slot="a3")
        ACT(sc[:], sc[:], AF.Silu, r=["sc"], w=["sc"])
        wv = adaw_d[l].rearrange("(k p) n -> p k n", p=128)
        for cg in range(12):
            wt = wb[cg % 2]
            DMA("sp", wt[:], wv[:, :, cg * 512:(cg + 1) * 512], w=[("wb", cg % 2)], slot=f"aw{cg % 2}")
            pp = ps[cg % 2]
            for k in range(8):
                MM(pp[0:2, :], sc[:, k, :], wt[:, k, :], k == 0, k == 7, r=["sc", ("wb", cg % 2)], w=[("ps", cg % 2)])
            TT("dve", m2[:, cg * 512:(cg + 1) * 512], pp[0:2, :], ab[:, cg * 512:(cg + 1) * 512], ALU.add,
               r=[("ps", cg % 2), "ab"], w=["m2"])
        for j in range(12):
            pp = ps[2 + j % 2]
            MM(pp[:, :], sel[0:2, 0:128], m2[0:2, j * 512:(j + 1) * 512], True, True, r=["sel", "m2"], w=[("ps", 2 + j % 2)])
            CP("act" if j % 2 else "dve", MOD[:, j // 2, (j % 2) * 512:(j % 2 + 1) * 512], pp[:, :],
               r=[("ps", 2 + j % 2)], w=["MOD"])
            if with_ctx and j < 4:
                pq = ps[4 + j % 2]
                MM(pq[:, :], sel[0:2, 128:256], m2[0:2, j * 512:(j + 1) * 512], True, True, r=["sel", "m2"],
                   w=[("ps", 4 + j % 2)])
                CP("act" if j % 2 else "dve", CMOD[:, j // 2, (j % 2) * 512:(j % 2 + 1) * 512], pq[:, :],
                   r=[("ps", 4 + j % 2)], w=["CMOD"])
        STT("dve", MOD[:, 1, :], MOD[:, 1, :], 1.0, nmt[:], ALU.add, ALU.mult, r=["MOD", "nmt"], w=["MOD"])
        TS("dve", MOD[:, 1, :], MOD[:, 1, :], 32.0, ALU.mult, r=["MOD"], w=["MOD"])
        STT("dve", MOD[:, 4, :], MOD[:, 4, :], 1.0, nft[:], ALU.add, ALU.mult, r=["MOD", "nft"], w=["MOD"])
        TS("dve", MOD[:, 4, :], MOD[:, 4, :], 32.0, ALU.mult, r=["MOD"], w=["MOD"])
        if with_ctx:
            STT("dve", CMOD[:, 1, :], CMOD[:, 1, :], 1.0, nmt[:], ALU.add, ALU.mult, r=["CMOD", "nmt"], w=["CMOD"])
            TS("dve", CMOD[:, 1, :], CMOD[:, 1, :], 32.0, ALU.mult, r=["CMOD"], w=["CMOD"])
        S.barrier()

    def modulate(xt, xkey, G32, SH, out, okey, st, skey, tmp, tkey, junk, jkey):
        ACT(junk, xt, AF.Square, r=[xkey], w=[jkey, skey], accum_out=st[:, 0:1])
        ACT(st[:, 1:2], st[:, 0:1], AF.Sqrt, r=[skey], w=[skey], bias=1024.0 * EPS, scale=1.0)
        RECIP(st[:, 2:3], st[:, 1:2], r=[skey], w=[skey])
        STT("dve", tmp, xt, st[:, 2:3], G32, ALU.mult, ALU.mult, r=[xkey, skey, "MOD", "CMOD"], w=[tkey])
        TT("pool", out, tmp, SH, ALU.add, r=[tkey, "MOD", "CMOD"], w=[okey])

    def proj0():
        S.reset_phase()
        w = S.alloc([128, 8, 2560], BF16)
        cosT = S.alloc([128, T], F32)
        sinT = S.alloc([128, T], F32)
        qk = S.alloc([128, 2], F32)
        CS = S.alloc([128, 2, 512], BF16)
        xb = [S.alloc([128, 1024], F32) for _ in range(2)]
        tmpb = [S.alloc([128, 1024], F32) for _ in range(2)]
        hb = [S.alloc([128, 1024], BF16) for _ in range(2)]
        junk = S.alloc([128, 1024], BF16)
        stt = [S.alloc([128, 4], F32) for _ in range(2)]
        hTb = [S.alloc([128, 8, 512], BF16) for _ in range(2)]
        sqb = S.alloc([128, 512], BF16)
        rs = S.alloc([128, 512], F32)
        knb = [S.alloc([128, 512], BF16) for _ in range(2)]
        t1 = S.alloc([128, 512], F32)
        t2 = S.alloc([128, 512], F32)
        kout = [S.alloc([128, 512], BF16) for _ in range(2)]
        v1t = [S.alloc([128, 6, 129], BF16) for _ in range(2)]
        fT = S.alloc([128, 2, 512], BF16)
        fcst = [S.alloc([128, 512], BF16) for _ in range(2)]
        wv = win_d.rearrange("(k p) n -> p k n", p=128)
        for q in range(4):
            DMA("pool", w[:, :, q * 640:(q + 1) * 640], wv[:, :, q * 640:(q + 1) * 640], w=["w"], slot=f"pw{q}")
        DMA("sp", cosT[:], rope_d[0], w=["cos"], slot="p0")
        DMA("sp", sinT[:], rope_d[1], w=["sin"], slot="p1")
        DMA("sp", qk[:], qk_d, w=["qk"], slot="p2")
        DMA("pool", CS[:], cs_d.rearrange("(c p) n -> p c n", p=128), w=["CS"], slot="p3")
        TS("dve", qk[:, 1:2], qk[:, 1:2], 8.0, ALU.mult, r=["qk"], w=["qk"])
        for i in range(2):
            MSET("pool", v1t[i][:], 1.0, w=[("v1t", i)])
        nrc = [0]

        def normrope(pp, pkey, nt, gain, rope, pos0, dst):
            c = nrc[0]
            nrc[0] += 1
            kn = knb[c % 2]
            ko = kout[c % 2]
            ACT(sqb[:, :nt], pp[:, :nt], AF.Square, r=[pkey], w=["sqb"])
            MM(ps[4][:, :nt], bones[:], sqb[:, :nt], True, True, r=["bones", "sqb"], w=[("ps", 4)])
            ACT(rs[:, :nt], ps[4][:, :nt], AF.Sqrt, r=[("ps", 4)], w=["rs"], bias=64.0 * EPS, scale=1.0)
            RECIP(rs[:, :nt], rs[:, :nt], r=["rs"], w=["rs"])
            STT("dve", kn[:, :nt], pp[:, :nt], gain, rs[:, :nt], ALU.mult, ALU.mult, r=[pkey, "qk", "rs"],
                w=[("kn", c % 2)])
            if rope:
                MM(ps[5][:, :nt], RT[:], kn[:, :nt], True, True, r=["RT", ("kn", c % 2)], w=[("ps", 5)])
                TT("pool", t1[:, :nt], kn[:, :nt], cosT[:, pos0:pos0 + nt], ALU.mult, r=[("kn", c % 2), "cos"], w=["t1"])
                TT("dve", t2[:, :nt], ps[5][:, :nt], sinT[:, pos0:pos0 + nt], ALU.mult, r=[("ps", 5), "sin"], w=["t2"])
                TT("pool", ko[:, :nt], t1[:, :nt], t2[:, :nt], ALU.add, r=["t1", "t2"], w=[("ko", c % 2)])
                DMA("sp", dst, ko[:, :nt], r=[("ko", c % 2)], slot=f"ko{c % 2}")
            else:
                DMA("sp", dst, kn[:, :nt], r=[("kn", c % 2)], slot=f"kn{c % 2}")

        pc = [0]
        for b in range(9):
            isctx = b == 8
            ntile = 2 if isctx else 4
            nt = ntile * 128
            hT = hTb[b % 2]
            hk = ("hT", b % 2)
            for t in range(ntile):
                i2 = t % 2
                src = ctx_d[t * 128:(t + 1) * 128, :] if isctx else x_d[b * 512 + t * 128: b * 512 + (t + 1) * 128, :]
                DMA("sp", xb[i2][:], src, w=[("xb", i2)], slot=f"xb{i2}")
                G = CMOD[:, 1, :] if isctx else MOD[:, 1, :]
                SHh = CMOD[:, 0, :] if isctx else MOD[:, 0, :]
                modulate(xb[i2][:], ("xb", i2), G, SHh, hb[i2][:], ("hb", i2), stt[i2], ("st", i2),
                         tmpb[i2][:], ("tmp", i2), junk[:], "junk")
                pT = psb[t % 2]
                for k in range(8):
                    TR(pT[:, k * 128:(k + 1) * 128], hb[i2][:, k * 128:(k + 1) * 128], identb[:],
                       r=[("hb", i2), "identb"], w=[("ps", t % 2)])
                CP("act", hT[:, :, t * 128:(t + 1) * 128], pT[:, 0:1024].rearrange("p (k n) -> p k n", k=8),
                   r=[("ps", t % 2)], w=[hk])
            for h in range(6):
                for which in ((0, 1) if not isctx else (1,)):
                    bank = 2 + pc[0] % 2
                    pc[0] += 1
                    col0 = (768 if which == 1 else 0) + h * 128
                    for k in range(8):
                        MM(ps[bank][:, :nt], w[:, k, col0:col0 + 128], hT[:, k, :nt], k == 0, k == 7,
                           r=["w", hk], w=[("ps", bank)])
                    dst = (kTd if which == 1 else qTd)[h, :, b * 512:b * 512 + nt]
                    normrope(ps[bank], ("ps", bank), nt, qk[:, which:which + 1], not isctx, b * 512 if not isctx else 0, dst)
            for t in range(ntile):
                vt = v1t[t % 2]
                for half in range(2):
                    bank = 6 + half
                    for k in range(8):
                        MM(ps[bank][:, 0:384], hT[:, k, t * 128:(t + 1) * 128],
                           w[:, k, 1536 + half * 384:1536 + (half + 1) * 384], k == 0, k == 7, r=["w", hk],
                           w=[("ps", bank)])
                    CP("act" if half else "dve", vt[:, half * 3:(half + 1) * 3, 0:128],
                       ps[bank][:, 0:384].rearrange("p (a b) -> p a b", a=3), r=[("ps", bank)], w=[("v1t", t % 2)])
                row0 = b * 512 + t * 128
                DMA("sp", V1d[row0:row0 + 128, :], vt.rearrange("p a b -> p (a b)"), r=[("v1t", t % 2)], slot=f"v1t{t % 2}")
            if not isctx:
                for c in range(2):
                    bank = 2 + pc[0] % 2
                    pc[0] += 1
                    for k in range(8):
                        MM(ps[bank][:, :], w[:, k, 2304 + c * 128:2304 + (c + 1) * 128], hT[:, k, :], k == 0, k == 7,
                           r=["w", hk], w=[("ps", bank)])
                    CP("act", fT[:, c, :], ps[bank][:, :], r=[("ps", bank)], w=["fT"])
                for t in range(4):
                    bank = 6 + t % 2
                    for c in range(2):
                        MM(ps[bank][:, :], fT[:, c, t * 128:(t + 1) * 128], CS[:, c, :], c == 0, c == 1, r=["fT", "CS"],
                           w=[("ps", bank)])
                    CP("dve", fcst[t % 2][:], ps[bank][:, :], r=[("ps", bank)], w=[("fcst", t % 2)])
                    row0 = b * 512 + t * 128
                    DMA("sp", fcsd[row0:row0 + 128, :], fcst[t % 2][:], r=[("fcst", t % 2)], slot=f"fc{t % 2}")
        S.barrier()

    def attn():
        S.reset_phase()
        kT = S.alloc([128, 6, NKEY], BF16)
        V1 = S.alloc([128, NKT, 774], BF16)
        lamt = S.alloc([128, 256], F32)
        SUB = S.alloc([128, 128], F32)
        qb = [S.alloc([128, 6, 512], BF16) for _ in range(2)]
        ob = [S.alloc([128, 6, 512], BF16) for _ in range(2)]
        a1 = S.alloc([128, 128], F32)
        att = S.alloc([128, 128], F32)
        attb4 = S.alloc([128, 4, 128], BF16)
        junk = S.alloc([128, 128], BF16)
        sm = S.alloc([128, 16], F32)
        accS = S.alloc([128, 3, 387], F32)
        for h in range(6):
            DMA("sp", kT[:, h, :], kTd[h], w=["kT"], slot=f"kT{h % 2}")
        v1v = V1d.rearrange("(t p) c -> p t c", p=128)
        for q in range(2):
            DMA("sp", V1[:, q * 17:(q + 1) * 17, :], v1v[:, q * 17:(q + 1) * 17, :], w=["V1"], slot=f"V1{q}")
        DMA("sp", lamt[:], lam_d, w=["lamt"], slot="lam")
        DMA("sp", SUB[:], sub_d, w=["SUB"], slot="sub")
        TS("dve", SUB[:], SUB[:], 0.8, ALU.mult, r=["SUB"], w=["SUB"])
        TT("dve", lamt[:, 0:64], lamt[:, 0:64], lamt[:, 64:128], ALU.mult, r=["lamt"], w=["lamt"])
        TT("dve", lamt[:, 128:192], lamt[:, 128:192], lamt[:, 192:256], ALU.mult, r=["lamt"], w=["lamt"])
        S.op("dve", lambda e: e.reduce_sum(out=sm[:, 0:1], in_=lamt[:, 0:64], axis=AX.X), r=["lamt"], w=["sm"])
        S.op("dve", lambda e: e.reduce_sum(out=sm[:, 1:2], in_=lamt[:, 128:192], axis=AX.X), r=["lamt"], w=["sm"])
        ACT(sm[:, 0:2], sm[:, 0:2], AF.Exp, r=["sm"], w=["sm"])
        TT("dve", sm[:, 2:3], sm[:, 1:2], sm[:, 0:1], ALU.subtract, r=["sm"], w=["sm"])
        TS("dve", sm[:, 2:3], sm[:, 2:3], -0.2, ALU.add, r=["sm"], w=["sm"])
        nlam = sm[:, 2:3]
        accs = {}
        for m in range(2):
            for qs in range(4):
                j = m * 4 + qs
                accs[(m, qs)] = ps[4 + j // 3][:, (j % 3) * 129:(j % 3) * 129 + 129]
        PT2 = [S.alloc([128, 1024], BF16) for _ in range(2)]
        steps = [(h, m, kt) for h in range(6) for m in range(2) for kt in range(NKT)]
        npair = len(steps) // 2
        for g in range(8):
            qT = qb[g % 2]
            DMA("sp", qT[:], qTd[:, :, g * 512:(g + 1) * 512].rearrange("h p n -> p h n"), w=[("qT", g % 2)], slot=f"qT{g % 2}")
            oT = ob[g % 2]

            def QKEXP(p):
                pb = p % 2
                for u in range(2):
                    h, m, kt = steps[2 * p + u]
                    MM(pbig[pb][:, u * 512:(u + 1) * 512], kT[m * 64:(m + 1) * 64, h, kt * 128:(kt + 1) * 128],
                       qT[m * 64:(m + 1) * 64, h, :], True, True, r=["kT", ("qT", g % 2)], w=[("pq", pb)])
                ACT(PT2[pb][:], pbig[pb][:, :], AF.Exp, r=[("pq", pb)], w=[("PT", pb)])

            def AV(p):
                pb = p % 2
                for u in range(2):
                    h, m, kt = steps[2 * p + u]
                    for qs in range(4):
                        st_ = (kt == 0 and (m * 4 + qs) % 3 == 0)
                        S.op("pe", lambda e, o_=accs[(m, qs)], l_=PT2[pb][:, u * 512 + qs * 128:u * 512 + (qs + 1) * 128],
                             r_=V1[:, kt, h * 129:(h + 1) * 129], st_=st_, sp_=(kt == NKT - 1):
                             e.matmul(o_, l_, r_, start=st_, stop=sp_, skip_group_check=True),
                             r=[("PT", pb), "V1"], w=[("ps", 4 + (m * 4 + qs) // 3)])

            def EPI(h):
                for bk in range(3):
                    wdt = 387 if bk < 2 else 258
                    CP("act", accS[:, bk, 0:wdt], ps[4 + bk][:, 0:wdt], r=[("ps", 4 + bk)], w=["accS"])
                for qs in range(4):
                    j0, j1 = qs, 4 + qs
                    A0 = accS[:, j0 // 3, (j0 % 3) * 129:(j0 % 3) * 129 + 129]
                    A1 = accS[:, j1 // 3, (j1 % 3) * 129:(j1 % 3) * 129 + 129]
                    RECIP(sm[:, 4:5], A0[:, 128:129], r=["accS"], w=["sm4"])
                    RECIP(sm[:, 5:6], A1[:, 128:129], r=["accS"], w=["sm5"])
                    TT("dve", sm[:, 6:7], sm[:, 5:6], nlam, ALU.mult, r=["sm5", "sm"], w=["sm6"])
                    TS("dve", a1[:], A0[:, 0:128], sm[:, 4:5], ALU.mult, r=["accS", "sm4"], w=["a1"])
                    STT("dve", att[:], A1[:, 0:128], sm[:, 6:7], a1[:], ALU.mult, ALU.add, r=["accS", "sm6", "a1"],
                        w=["att"])
                    ACT(junk[:], att[:], AF.Square, r=["att"], w=["junkA", "sm7"], accum_out=sm[:, 7:8])
                    ACT(sm[:, 8:9], sm[:, 7:8], AF.Sqrt, r=["sm7"], w=["sm8"], bias=EPS, scale=1.0 / 128.0)
                    RECIP(sm[:, 9:10], sm[:, 8:9], r=["sm8"], w=["sm9"])
                    STT("dve", attb4[:, qs, :], att[:], sm[:, 9:10], SUB[:], ALU.mult, ALU.mult, r=["att", "sm9", "SUB"],
                        w=["attb"])

            def EPI_T(h):
                for qs in range(4):
                    TR(psb[7][:, qs * 128:(qs + 1) * 128], attb4[:, qs, :], identb[:], r=["attb", "identb"], w=[("ps", 7)])
                CP("act", oT[:, h, :], psb[7][:, 0:512], r=[("ps", 7)], w=[("oT", g % 2)])

            QKEXP(0)
            pend = None
            for p in range(npair):
                if p + 1 < npair:
                    QKEXP(p + 1)
                AV(p)
                if pend is not None and p == pend[1]:
                    EPI_T(pend[0])
                    pend = None
                hh, mm, kk = steps[2 * p + 1]
                if mm == 1 and kk == NKT - 1:
                    EPI(hh)
                    pend = (hh, p + 6)
                    if p == npair - 1:
                        EPI_T(hh)
                        pend = None
            DMA("sp", oTd[0:6, :, g * 512:(g + 1) * 512].rearrange("c p n -> p c n"), oT[:], r=[("oT", g % 2)], slot=f"oT{g % 2}")
        S.barrier()

    def fourier():
        S.reset_phase()
        fcs = S.alloc([128, 32, 512], BF16)
        cb = [S.alloc([128, 16, 512], BF16) for _ in range(2)]
        sb = [S.alloc([128, 16, 512], BF16) for _ in range(2)]
        oF = [S.alloc([128, 2, 512], BF16) for _ in range(2)]
        DMA("sp", fcs[:], fcsd.rearrange("(t p) c -> p t c", p=128), w=["fcs"], slot="fcs")
        cv = cos_d.rearrange("(t p) k -> p t k", p=128)
        sv = sin_d.rearrange("(t p) k -> p t k", p=128)
        n = 0
        for g in range(8):
            for half in range(2):
                cbb, sbb = cb[n % 2], sb[n % 2]
                DMA("sp", cbb[:], cv[:, half * 16:(half + 1) * 16, g * 512:(g + 1) * 512], w=[("cb", n % 2)], slot=f"cb{n % 2}")
                DMA("act", sbb[:], sv[:, half * 16:(half + 1) * 16, g * 512:(g + 1) * 512], w=[("sb", n % 2)], slot=f"sb{n % 2}")
                for c in range(2):
                    for t in range(16):
                        tt = half * 16 + t
                        MM(ps[c][:, :], fcs[:, tt, c * 128:(c + 1) * 128], cbb[:, t, :], tt == 0, False,
                           r=["fcs", ("cb", n % 2)], w=[("ps", c)])
                        MM(ps[c][:, :], fcs[:, tt, 256 + c * 128:256 + (c + 1) * 128], sbb[:, t, :], False, tt == 31,
                           r=["fcs", ("sb", n % 2)], w=[("ps", c)])
                n += 1
            for c in range(2):
                CP("dve" if c else "act", oF[g % 2][:, c, :], ps[c][:, :], r=[("ps", c)], w=[("oF", g % 2)])
            DMA("sp", oTd[6:8, :, g * 512:(g + 1) * 512].rearrange("c p n -> p c n"), oF[g % 2][:], r=[("oF", g % 2)],
                slot=f"oF{g % 2}")
        S.barrier()

    def outproj(l, wsrc, xin, xout):
        S.reset_phase()
        wo = S.alloc([128, 8, 1024], BF16)
        wr = S.alloc([128, 8, 16], F32)
        ob = [S.alloc([128, 8, 128], BF16) for _ in range(2)]
        xb = [S.alloc([128, 1024], F32) for _ in range(2)]
        yb = [S.alloc([128, 1024], F32) for _ in range(2)]
        x1b = [S.alloc([128, 1024], F32) for _ in range(2)]
        tmpb = [S.alloc([128, 1024], F32) for _ in range(2)]
        hfb = [S.alloc([128, 1024], F32) for _ in range(2)]
        hbb = [S.alloc([128, 1024], BF16) for _ in range(2)]
        hfT = [S.alloc([128, 8, 128], F32) for _ in range(2)]
        junk = S.alloc([128, 1024], BF16)
        stt = [S.alloc([128, 8], F32) for _ in range(2)]
        ex = S.alloc([128, 16], F32)
        wv = wsrc.rearrange("(k p) n -> p k n", p=128)
        for q in range(2):
            DMA("pool", wo[:, :, q * 512:(q + 1) * 512], wv[:, :, q * 512:(q + 1) * 512], w=["wo"], slot=f"wo{q}")
        DMA("sp", wr[:], rw_d[l].rearrange("(k p) e -> p k e", p=128), w=["wr"], slot="wr")
        for i in range(NT):
            i2 = i % 2
            DMA("sp", ob[i2][:], oTd[:, :, i * 128:(i + 1) * 128].rearrange("c p n -> p c n"), w=[("ob", i2)], slot=f"ob{i2}")
            DMA("sp", xb[i2][:], xin[i * 128:(i + 1) * 128, :], w=[("xb", i2)], slot=f"xb{i2}")
            for half in range(2):
                for c in range(8):
                    MM(ps[half][:, :], ob[i2][:, c, :], wo[:, c, half * 512:(half + 1) * 512], c == 0, c == 7,
                       r=[("ob", i2), "wo"], w=[("ps", half)])
                TT("dve", yb[i2][:, half * 512:(half + 1) * 512], ps[half][:, :], MOD[:, 2, half * 512:(half + 1) * 512],
                   ALU.mult, r=[("ps", half), "MOD"], w=[("yb", i2)])
            TT("pool", x1b[i2][:], yb[i2][:], xb[i2][:], ALU.add, r=[("yb", i2), ("xb", i2)], w=[("x1b", i2)])
            DMA("sp", xout[i * 128:(i + 1) * 128, :], x1b[i2][:], r=[("x1b", i2)], slot=f"x1o{i2}")
            modulate(x1b[i2][:], ("x1b", i2), MOD[:, 4, :], MOD[:, 3, :], hfb[i2][:], ("hfb", i2), stt[i2], ("st", i2),
                     tmpb[i2][:], ("tmp", i2), junk[:], "junk")
            CP("act", hbb[i2][:], hfb[i2][:], r=[("hfb", i2)], w=[("hbb", i2)])
            DMA("sp", hfd[i * 128:(i + 1) * 128, :], hbb[i2][:], r=[("hbb", i2)], slot=f"hfo{i2}")
            for k in range(8):
                bank = 2 + k // 4
                TR(ps[bank][:, (k % 4) * 128:(k % 4 + 1) * 128], hfb[i2][:, k * 128:(k + 1) * 128], identf[:],
                   r=[("hfb", i2), "identf"], w=[("ps", bank)])
            CP("act", hfT[i2][:, 0:4, :], ps[2][:, :].rearrange("p (k n) -> p k n", k=4), r=[("ps", 2)], w=[("hfT", i2)])
            CP("dve", hfT[i2][:, 4:8, :], ps[3][:, :].rearrange("p (k n) -> p k n", k=4), r=[("ps", 3)], w=[("hfT", i2)])
            for k in range(8):
                MM(ps[4][:, 0:16], hfT[i2][:, k, :], wr[:, k, :], k == 0, k == 7, r=[("hfT", i2), "wr"], w=[("ps", 4)])
            st = stt[i2]
            S.op("dve", lambda e, st=st: e.reduce_max(out=st[:, 4:5], in_=ps[4][:, 0:16], axis=AX.X), r=[("ps", 4)],
                 w=[("st5", i2)])
            TS("dve", st[:, 5:6], st[:, 4:5], -1.0, ALU.mult, r=[("st5", i2)], w=[("st6", i2)])
            ACT(ex[:], ps[4][:, 0:16], AF.Exp, r=[("ps", 4), ("st6", i2)], w=["ex", ("st7", i2)], bias=st[:, 5:6], scale=1.0,
                accum_out=st[:, 6:7])
            RECIP(st[:, 7:8], st[:, 6:7], r=[("st7", i2)], w=[("st8", i2)])
            TS("dve", AFF[:, i, :], ex[:], st[:, 7:8], ALU.mult, r=["ex", ("st8", i2)], w=["AFF"])
        S.barrier()

    def moe(l):
        S.reset_phase()
        slotm = S.alloc([128, 32, 16], F32)
        VALS = S.alloc([128, 32, 16, 5], BF16)
        VA = S.alloc([128, 512, 20], BF16)
        mark = S.sb_off
        affT = S.alloc([16, T], F32)
        junkb = S.alloc([16, T], BF16)
        maskT = S.alloc([16, T], BF16)
        bs = S.alloc([16, 8], F32)
        MASK = S.alloc([128, 512], F32)
        MASKb = S.alloc([128, 512], BF16)
        tots = S.alloc([128, 32, 16], F32)
        base = S.alloc([128, 32, 16], F32)
        r1 = S.alloc([128, 512], F32)
        gtmp = S.alloc([128, 512], BF16)
        zt = S.alloc([128, 2, 1024], F32)
        bm = S.alloc([128, 512], F32)
        ag = S.alloc([128, 512], F32)
        aoh = S.alloc([128, 512], BF16)
        AFFf = AFF.rearrange("p a b -> p (a b)")
        MSET("pool", zt[:], 0.0, w=["zt"])
        yv = ymoe.rearrange("(t p) d -> p t d", p=128)
        for q in range(16):
            DMA("sp", yv[:, q * 2:(q + 1) * 2, :], zt[:], r=["zt"], w=[("ymoe", 0)], slot="zy")
        for i in range(NT):
            bank = i // 4 % 2
            TR(ps[bank][0:16, (i % 4) * 128:(i % 4 + 1) * 128], AFF[:, i, :], identf[:], r=["AFF", "identf"], w=[("ps", bank)])
            if i % 4 == 3:
                CP("act", affT[:, (i // 4) * 512:(i // 4 + 1) * 512], ps[bank][0:16, :], r=[("ps", bank)], w=["affT"])
        MSET("dve", bs[:], 0.0, w=["bs"])
        for it in range(28):
            step = 2.0 ** -(it + 1)
            TS("dve", bs[:, 1:2], bs[:, 0:1], step, ALU.add, r=["bs"], w=["bs1"])
            TS("dve", junkb[:], affT[:], bs[:, 1:2], ALU.is_gt, 0.0, ALU.add, r=["affT", "bs1"], w=["junkb", "bs2"],
               accum=bs[:, 2:3])
            TS("dve", bs[:, 3:4], bs[:, 2:3], 511.5, ALU.is_gt, step, ALU.mult, r=["bs2"], w=["bs3"])
            TT("dve", bs[:, 0:1], bs[:, 0:1], bs[:, 3:4], ALU.add, r=["bs", "bs3"], w=["bs"])
        TS("dve", maskT[:], affT[:], bs[:, 0:1], ALU.is_gt, r=["affT", "bs"], w=["maskT"])
        for i in range(NT):
            TR(psb[2][:, i * 16:(i + 1) * 16], maskT[:, i * 128:(i + 1) * 128], identb[0:16, 0:16], r=["maskT", "identb"],
               w=[("ps", 2)])
        CP("act", MASKb[:], psb[2][:, 0:512], r=[("ps", 2)], w=["MASKb"])
        CP("dve", MASK[:], MASKb[:], r=["MASKb"], w=["MASK"])
        MM(ps[3][:, :], Ust[:], MASKb[:], True, True, r=["Ust", "MASKb"], w=[("ps", 3)])
        MM(ps[4][:, :], onesb[:], MASKb[:], True, True, r=["onesb", "MASKb"], w=[("ps", 4)])
        CP("act", tots.rearrange("p a b -> p (a b)"), ps[4][:, :], r=[("ps", 4)], w=["tots"])
        MSET("dve", base[:, 0, :], 0.0, w=["base"])
        for i in range(1, NT):
            TT("dve", base[:, i, :], base[:, i - 1, :], tots[:, i - 1, :], ALU.add, r=["base", "tots"], w=["base"])
        sf = slotm.rearrange("p a b -> p (a b)")
        TT("dve", sf, ps[3][:, :], base.rearrange("p a b -> p (a b)"), ALU.add, r=[("ps", 3), "base"], w=["slotm"])
        TS("dve", ag[:], sf, 128.0, ALU.is_ge, r=["slotm"], w=["ag"])
        STT("dve", ag[:], sf, 256.0, ag[:], ALU.is_ge, ALU.add, r=["slotm", "ag"], w=["ag"])
        STT("dve", ag[:], sf, 384.0, ag[:], ALU.is_ge, ALU.add, r=["slotm", "ag"], w=["ag"])
        STT("dve", bm[:], ag[:], -128.0, sf, ALU.mult, ALU.add, r=["slotm", "ag"], w=["bm"])
        STT("dve", sf, bm[:], 1.0, MASK[:], ALU.add, ALU.mult, r=["bm", "MASK"], w=["slotm"])
        TS("dve", sf, sf, -1.0, ALU.add, r=["slotm"], w=["slotm"])
        CP("pool", VALS[:, :, :, 0:2], tokhl[:], r=["tokhl"], w=["VALS"])
        Vg = lambda j: VALS[:, :, :, j].rearrange("p a b -> p (a b)")
        CP("dve", gtmp[:], AFFf, r=["AFF"], w=["gtmp"])
        CP("dve", VALS[:, :, :, 2], gtmp.rearrange("p (a b) -> p a b", a=32), r=["gtmp"], w=["VALS"])
        TT("dve", r1[:], AFFf, gtmp[:], ALU.subtract, r=["AFF", "gtmp"], w=["r1"])
        CP("dve", gtmp[:], r1[:], r=["r1"], w=["gtmp"])
        CP("dve", VALS[:, :, :, 3], gtmp.rearrange("p (a b) -> p a b", a=32), r=["gtmp"], w=["VALS"])
        TT("dve", r1[:], r1[:], gtmp[:], ALU.subtract, r=["r1", "gtmp"], w=["r1"])
        CP("dve", VALS[:, :, :, 4], r1.rearrange("p (a b) -> p a b", a=32), r=["r1"], w=["VALS"])
        VALf = VALS.rearrange("p a b c -> p (a b) c")
        for a_ in range(4):
            TS("dve", aoh[:], ag[:], float(a_), ALU.is_equal, r=["ag"], w=["aoh"])
            for v_ in range(5):
                TT("dve", VA[:, :, v_ * 4 + a_], VALf[:, :, v_], aoh[:], ALU.mult, r=["VALS", "aoh"], w=["VA"])

        S.barrier()
        S.sb_off = mark
        wgb = [S.alloc([128, 8, 1024], BF16) for _ in range(2)]
        wub = [S.alloc([128, 8, 1024], BF16) for _ in range(2)]
        wdb = [S.alloc([128, 8, 1024], BF16) for _ in range(2)]
        selb = [S.alloc([128, 128], BF16) for _ in range(4)]
        ivt = S.alloc([128, 5, 4], F32)
        tokf = S.alloc([128, 4], F32)
        idx = [S.alloc([128, 4], I32) for _ in range(2)]
        gg = [S.alloc([128, 4], F32) for _ in range(2)]
        xs = [S.alloc([128, 1024], BF16) for _ in range(4)]
        xsT = S.alloc([128, 8, 512], BF16)
        sg = [S.alloc([128, 512], F32) for _ in range(2)]
        aT = S.alloc([128, 8, 512], BF16)
        ysb = [S.alloc([128, 1024], F32) for _ in range(2)]
        yc = 0
        for ex_ in range(16):
            e2 = ex_ % 2
            for (buf, src, nm) in ((wgb, wg_d, "wg"), (wub, wu_d, "wu"), (wdb, wd_d, "wd")):
                sv = src[l, ex_].rearrange("(k p) n -> p k n", p=128)
                for q in range(2):
                    DMA("pool", buf[e2][:, :, q * 512:(q + 1) * 512], sv[:, :, q * 512:(q + 1) * 512], w=[(nm, e2)],
                        slot=f"{nm}{e2}{q}")
            for i in range(NT):
                sb_ = selb[i % 4]
                TS("dve", sb_[:], iota[:, 0:128], slotm[:, i, ex_:ex_ + 1], ALU.is_equal,
                   r=["iota", "slotm"], w=[("selb", i % 4)])
                MM(ps[5][:, 0:20], sb_[:], VA[:, i * 16 + ex_, :], i == 0, i == NT - 1, r=["VA", ("selb", i % 4)], w=[("ps", 5)])
            CP("dve", ivt.rearrange("p a b -> p (a b)"), ps[5][:, 0:20], r=[("ps", 5)], w=["ivt"])
            STT("dve", tokf[:], ivt[:, 0, :], 64.0, ivt[:, 1, :], ALU.mult, ALU.add, r=["ivt"], w=["tokf"])
            CP("dve", idx[e2][:], tokf[:], r=["tokf"], w=[("idx", e2)])
            TT("dve", gg[e2][:], ivt[:, 2, :], ivt[:, 3, :], ALU.add, r=["ivt"], w=[("gg", e2)])
            TT("dve", gg[e2][:], gg[e2][:], ivt[:, 4, :], ALU.add, r=["ivt", ("gg", e2)], w=[("gg", e2)])
            for grp in range(4):
                S.op("pool", lambda e, grp=grp, e2=e2: e.indirect_dma_start(
                    out=xs[grp][:], out_offset=None, in_=hfd,
                    in_offset=bass.IndirectOffsetOnAxis(ap=idx[e2][:, grp:grp + 1], axis=0)),
                    r=[("idx", e2)], w=[("xs", grp)], dma=f"xs{grp}")
                pT = psb[7]
                for k in range(8):
                    TR(pT[:, k * 128:(k + 1) * 128], xs[grp][:, k * 128:(k + 1) * 128], identb[:], r=[("xs", grp), "identb"],
                       w=[("ps", 7)])
                CP("act" if grp % 2 else "dve", xsT[:, :, grp * 128:(grp + 1) * 128],
                   pT[:, 0:1024].rearrange("p (k n) -> p k n", k=8), r=[("ps", 7)], w=["xsT"])
            for fc in range(8):
                for k in range(8):
                    MM(ps[0][:, :], wgb[e2][:, k, fc * 128:(fc + 1) * 128], xsT[:, k, :], k == 0, k == 7, r=[("wg", e2), "xsT"],
                       w=[("ps", 0)])
                for k in range(8):
                    MM(ps[1][:, :], wub[e2][:, k, fc * 128:(fc + 1) * 128], xsT[:, k, :], k == 0, k == 7, r=[("wu", e2), "xsT"],
                       w=[("ps", 1)])
                ACT(sg[fc % 2][:], ps[0][:, :], AF.Silu, r=[("ps", 0)], w=[("sg", fc % 2)])
                TT("dve", aT[:, fc, :], sg[fc % 2][:], ps[1][:, :], ALU.mult, r=[("sg", fc % 2), ("ps", 1)], w=["aT"])
            for grp in range(4):
                y2 = yc % 2
                yc += 1
                for half in range(2):
                    bank = 2 + half
                    for fc in range(8):
                        MM(ps[bank][:, :], aT[:, fc, grp * 128:(grp + 1) * 128], wdb[e2][:, fc, half * 512:(half + 1) * 512],
                           fc == 0, fc == 7, r=["aT", ("wd", e2)], w=[("ps", bank)])
                    if half:
                        ACT(ysb[y2][:, 512:1024], ps[bank][:, :], AF.Copy, r=[("ps", bank), ("gg", e2)], w=[("ysb", y2)],
                            scale=gg[e2][:, grp:grp + 1])
                    else:
                        TS("dve", ysb[y2][:, 0:512], ps[bank][:, :], gg[e2][:, grp:grp + 1], ALU.mult,
                           r=[("ps", bank), ("gg", e2)], w=[("ysb", y2)])
                S.op("pool", lambda e, grp=grp, e2=e2, y2=y2: e.indirect_dma_start(
                    out=ymoe, out_offset=bass.IndirectOffsetOnAxis(ap=idx[e2][:, grp:grp + 1], axis=0),
                    in_=ysb[y2][:], in_offset=None, compute_op=ALU.add),
                    r=[("idx", e2), ("ysb", y2), ("ymoe", ex_)], w=[("ymoe", ex_ + 1), ("ymoeg", grp)], dma=f"ys{y2}")
        S.barrier()

    def combine(src, dst):
        S.reset_phase()
        xb = [S.alloc([128, 1024], F32) for _ in range(2)]
        yb = [S.alloc([128, 1024], F32) for _ in range(2)]
        ob = [S.alloc([128, 1024], F32) for _ in range(2)]
        for i in range(NT):
            i2 = i % 2
            DMA("sp", xb[i2][:], src[i * 128:(i + 1) * 128, :], w=[("xb", i2)], slot=f"cx{i2}")
            DMA("act", yb[i2][:], ymoe[i * 128:(i + 1) * 128, :], w=[("yb", i2)], slot=f"cy{i2}")
            TT("dve", yb[i2][:], yb[i2][:], MOD[:, 5, :], ALU.mult, r=[("yb", i2), "MOD"], w=[("yb", i2)])
            TT("pool", ob[i2][:], yb[i2][:], xb[i2][:], ALU.add, r=[("yb", i2), ("xb", i2)], w=[("ob", i2)])
            DMA("sp", dst[i * 128:(i + 1) * 128, :], ob[i2][:], r=[("ob", i2)], slot=f"co{i2}")
        S.barrier()

    def conv():
        S.reset_phase()
        hT = S.alloc([128, 8, T], BF16)
        xb = [S.alloc([128, 1024], F32) for _ in range(2)]
        tmpb = [S.alloc([128, 1024], F32) for _ in range(2)]
        hb = [S.alloc([128, 1024], BF16) for _ in range(2)]
        junk = S.alloc([128, 1024], BF16)
        stt = [S.alloc([128, 4], F32) for _ in range(2)]
        cw = S.alloc([128, 8, 3], F32)
        w3 = [S.alloc([128, 8, 3, 128], BF16) for _ in range(2)]
        z = S.alloc([128, T + 2], F32)
        tt_ = S.alloc([128, T], F32)
        bgs = S.alloc([128, T], BF16)
        cgs = [S.alloc([128, 512], F32) for _ in range(2)]
        vT = [S.alloc([128, T], BF16) for _ in range(2)]
        DMA("sp", cw[:], convw_d, w=["cw"], slot="cw")
        MSET("pool", z[:], 0.0, w=["z"])
        for i in range(NT):
            i2 = i % 2
            DMA("sp", xb[i2][:], x2d[i * 128:(i + 1) * 128, :], w=[("xb", i2)], slot=f"xb{i2}")
            modulate(xb[i2][:], ("xb", i2), MOD[:, 1, :], MOD[:, 0, :], hb[i2][:], ("hb", i2), stt[i2], ("st", i2),
                     tmpb[i2][:], ("tmp", i2), junk[:], "junk")
            pT = psb[i % 2]
            for k in range(8):
                TR(pT[:, k * 128:(k + 1) * 128], hb[i2][:, k * 128:(k + 1) * 128], identb[:], r=[("hb", i2), "identb"],
                   w=[("ps", i % 2)])
            CP("act", hT[:, :, i * 128:(i + 1) * 128], pT[:, 0:1024].rearrange("p (k n) -> p k n", k=8), r=[("ps", i % 2)],
               w=["hT"])
        wv = cwin_d.rearrange("(k p) n -> p k n", p=128)
        for fc in range(8):
            f2 = fc % 2
            for j in range(3):
                DMA("pool", w3[f2][:, :, j, :], wv[:, :, j * 1024 + fc * 128:j * 1024 + (fc + 1) * 128], w=[("w3", f2)],
                    slot=f"w3{f2}{j}")
            for b in range(8):
                for j in range(3):
                    bank = 2 + j * 2 + b % 2
                    for k in range(8):
                        MM(ps[bank][:, :], w3[f2][:, k, j, :], hT[:, k, b * 512:(b + 1) * 512], k == 0, k == 7,
                           r=[("w3", f2), "hT"], w=[("ps", bank)])
                CP("act", bgs[:, b * 512:(b + 1) * 512], ps[2 + b % 2][:, :], r=[("ps", 2 + b % 2)], w=["bgs"])
                CP("act", cgs[b % 2][:], ps[4 + b % 2][:, :], r=[("ps", 4 + b % 2)], w=[("cgs", b % 2)])
                TT("dve", z[:, 1 + b * 512:1 + (b + 1) * 512], cgs[b % 2][:], ps[6 + b % 2][:, :], ALU.mult,
                   r=[("cgs", b % 2), ("ps", 6 + b % 2)], w=["z"])
            ACT(tt_[:], z[:, 0:T], AF.Copy, r=["z", "cw"], w=["tt"], scale=cw[:, fc, 0:1])
            STT("dve", tt_[:], z[:, 1:T + 1], cw[:, fc, 1:2], tt_[:], ALU.mult, ALU.add, r=["z", "cw", "tt"], w=["tt"])
            STT("dve", tt_[:], z[:, 2:T + 2], cw[:, fc, 2:3], tt_[:], ALU.mult, ALU.add, r=["z", "cw", "tt"], w=["tt"])
            TT("pool", vT[f2][:], bgs[:], tt_[:], ALU.mult, r=["bgs", "tt"], w=[("vT", f2)])
            DMA("sp", oTd[fc], vT[f2][:], r=[("vT", f2)], slot=f"vT{f2}")
        S.barrier()

    phases = [
        lambda: ada(0, True),
        proj0,
        attn,
        fourier,
        lambda: outproj(0, wout_d, x_d, x1d),
        lambda: moe(0),
        lambda: combine(x1d, x2d),
        lambda: ada(1, False),
        conv,
        lambda: outproj(1, cwout_d, x2d, x3d),
        lambda: moe(1),
        lambda: combine(x3d, out_d),
    ]
    for i, ph in enumerate(phases):
        if stop is not None and i >= stop:
            break
        ph()
    S.emit()
    return nc, S


import ml_dtypes

_CACHE = {}


def _consts():
    if "c" in _CACHE:
        return _CACHE["c"]
    bf = ml_dtypes.bfloat16
    cpk = np.zeros((128, 1152), np.float32)
    cpk[:, 0:128] = np.eye(128)
    bo = np.zeros((128, 128), np.float32)
    bo[0:64, 0:64] = 1.0
    bo[64:128, 64:128] = 1.0
    cpk[:, 128:256] = bo
    R = np.zeros((64, 64), np.float32)
    for j in range(64):
        q = j // 16
        if q % 2 == 0:
            R[j, j + 16] = -1.0
        else:
            R[j, j - 16] = 1.0
    R2 = np.zeros((128, 128), np.float32)
    R2[0:64, 0:64] = R
    R2[64:128, 64:128] = R
    cpk[:, 256:384] = R2.T
    cpk[:, 384:512] = np.triu(np.ones((128, 128), np.float32), 1)
    cpk[:, 512:640] = 1.0
    cpk[:, 640:1152] = np.arange(512, dtype=np.float32)[None, :]
    tok = (np.arange(32)[None, :] * 128 + np.arange(128)[:, None])
    tokhl = np.zeros((128, 32, 16, 2), np.float32)
    tokhl[:, :, :, 0] = (tok // 64)[:, :, None]
    tokhl[:, :, :, 1] = (tok % 64)[:, :, None]
    sel = np.zeros((2, 256), np.float32)
    sel[0, 0:128] = 1.0
    sel[1, 128:256] = 1.0
    n = 4096
    r = np.repeat(np.arange(n // 64, dtype=np.float32), 64)
    col = np.tile(np.arange(64, dtype=np.float32), n // 64)
    inv = (np.float32(10000.0) ** (-np.arange(16, dtype=np.float32) / np.float32(16))).astype(np.float32)
    ar = r[:, None] * inv
    ac = col[:, None] * inv
    ang = np.concatenate([ar, ar, ac, ac], axis=-1).astype(np.float32)
    cosT = np.cos(ang).astype(np.float32).T
    sinT = np.sin(ang).astype(np.float32).T
    rope = np.stack([np.concatenate([cosT, cosT], 0), np.concatenate([sinT, sinT], 0)]).astype(np.float32)
    cm = np.arange(64)[:, None] * np.arange(64)[None, :]
    cb = np.cos(2 * np.pi * (cm % 64) / 64.0) / 512.0
    sb = np.sin(2 * np.pi * (cm % 64) / 64.0) / 512.0
    csd = np.zeros((256, 512), np.float32)
    for g in range(4):
        csd[g * 64:(g + 1) * 64, g * 64:(g + 1) * 64] = cb
        csd[g * 64:(g + 1) * 64, 256 + g * 64:256 + (g + 1) * 64] = sb
    kn = (np.arange(n, dtype=np.int64)[:, None] * np.arange(n, dtype=np.int64)[None, :]) % n
    angp = kn.astype(np.float64) * (2 * np.pi / n)
    cosd = np.cos(angp).astype(bf)
    sind = (-np.sin(angp)).astype(bf)
    c = dict(cpk=cpk, tokhl=tokhl.reshape(128, 1024), sel=sel, rope=rope, csd=csd, cosd=cosd, sind=sind)
    _CACHE["c"] = c
    return c


def kernel(x, c, ctx, c_ctx, ada_w, ada_b, norm_mix, norm_ffn, attn_w_in, attn_q_norm, attn_k_norm,
           lam_q1, lam_k1, lam_q2, lam_k2, attn_subln, attn_w_out, conv_w_in, conv_w, conv_w_out,
           router_w, moe_w_gate, moe_w_up, moe_w_down, _stop=None, _debug=False, _ncores=4):
    f = lambda a: np.ascontiguousarray(np.asarray(a, dtype=np.float32))
    x, c, ctx, c_ctx = f(x), f(c), f(ctx), f(c_ctx)
    K = _consts()
    ck = ("nc", _stop, _debug)
    if ck not in _CACHE:
        _CACHE[ck] = build_nc(_stop, _debug)[0]
    nc = _CACHE[ck]
    shared = dict(K)
    shared["ada_w"] = f(ada_w)
    shared["adab2"] = f(np.repeat(f(ada_b)[:, None, :], 2, axis=1))
    shared["nmrep"] = f(np.repeat(f(norm_mix)[:, None, :], 128, axis=1))
    shared["nfrep"] = f(np.repeat(f(norm_ffn)[:, None, :], 128, axis=1))
    shared["attn_w_in"] = f(attn_w_in)[0]
    shared["attn_w_out"] = f(attn_w_out)[0]
    shared["conv_w_in"] = f(conv_w_in)[0]
    shared["conv_w_out"] = f(conv_w_out)[0]
    shared["qkcol"] = f(np.stack([np.tile(f(attn_q_norm)[0], 2), np.tile(f(attn_k_norm)[0], 2)], axis=1))
    lamrow = np.concatenate([f(lam_q1)[0], f(lam_k1)[0], f(lam_q2)[0], f(lam_k2)[0]])
    shared["lamrep"] = f(np.repeat(lamrow[None, :], 128, axis=0))
    shared["sublnrep"] = f(np.repeat(f(attn_subln)[0][None, :], 128, axis=0))
    shared["convw"] = f(f(conv_w)[0].T.reshape(8, 128, 3).transpose(1, 0, 2))
    shared["router_w"] = f(router_w)
    shared["moe_w_gate"] = f(moe_w_gate)
    shared["moe_w_up"] = f(moe_w_up)
    shared["moe_w_down"] = f(moe_w_down)
    in_maps = []
    for b in range(_ncores):
        m = dict(shared)
        m["x"] = x[b]
        m["ctx"] = ctx[b]
        c2 = np.stack([c[b].reshape(8, 128).T, c_ctx.reshape(8, 128).T], axis=-1)
        m["c2"] = f(c2)
        in_maps.append(m)
    res = run_bass_kernel_spmd(nc, in_maps, core_ids=list(range(_ncores)))
    _CACHE["res"] = res
    if _debug:
        return res
    return np.stack([np.asarray(res.results[b]["out"], dtype=np.float32) for b in range(4)], axis=0)
```

```python
from concourse.bass_utils import run_bass_kernel_spmd
import numpy as np
import concourse.bass as bass
import concourse.mybir as mybir

F32 = mybir.dt.float32
BF16 = mybir.dt.bfloat16
I32 = mybir.dt.int32
ALU = mybir.AluOpType
AF = mybir.ActivationFunctionType
AX = mybir.AxisListType
ENG = ("pe", "act", "dve", "pool", "sp")


class Op:
    __slots__ = ("eng", "fn", "deps", "sig", "sem", "val", "dma", "waits", "semkey")


class Sched:
    def __init__(self, nc):
        self.nc = nc
        self.streams = {e: [] for e in ENG}
        self.lw = {}
        self.lr = {}
        self.pending_dma = []
        self.last_real = {}
        self.slot_sem = {}
        self.slot_cnt = {}
        self.nsem = 0
        self.sb_off = 0
        self.sb_base = 0
        self.uid = 0

    def alloc(self, shape, dtype, name="t"):
        if not hasattr(self, "views"):
            big = self.nc.alloc_sbuf_tensor("arena", [128, 103 * 1024], BF16)
            self.views = {BF16: big, F32: big.bitcast(F32), I32: big.bitcast(I32)}
        esz = mybir.dt.size(dtype)
        n = int(np.prod(shape[1:]))
        off = (self.sb_off + 63) // 64 * 64
        self.sb_off = off + n * esz
        assert self.sb_off <= 206 * 1024, (name, self.sb_off)
        ap = self.views[dtype][0:shape[0], off // esz: off // esz + n]
        if len(shape) == 3:
            ap = ap.rearrange("p (a b) -> p a b", a=shape[1])
        elif len(shape) == 4:
            ap = ap.rearrange("p (a b c) -> p a b c", a=shape[1], b=shape[2])
        return ap

    def mark_persistent(self):
        self.sb_base = self.sb_off

    def reset_phase(self):
        self.sb_off = self.sb_base

    def newsem(self, name):
        self.nsem += 1
        return self.nc.alloc_semaphore(f"{name}_{self.nsem}")

    def op(self, eng, fn, r=(), w=(), dma=None):
        o = Op()
        o.eng, o.fn, o.sig, o.dma = eng, fn, False, dma
        o.sem = None
        o.val = 0
        deps = {}
        for k in r:
            for d in self.lw.get(k, ()):
                deps[id(d)] = d
        for k in w:
            for d in self.lw.get(k, ()):
                if d.dma or dma or d.eng != eng:
                    deps[id(d)] = d
            for d in self.lr.get(k, ()):
                if d.dma or dma or d.eng != eng:
                    deps[id(d)] = d
        o.deps = list(deps.values())
        for d in o.deps:
            d.sig = True
        for k in w:
            self.lw[k] = [o]
            self.lr[k] = []
        for k in r:
            lst = self.lr.setdefault(k, [])
            if not dma:
                lst[:] = [x for x in lst if x.dma or x.eng != eng]
            lst.append(o)
        self.streams[eng].append(o)
        if dma:
            o.sig = True
            self.pending_dma.append(o)
        elif fn is not None:
            self.last_real[eng] = o
        return o

    def barrier(self):
        lasts = list(self.last_real.values()) + list(self.pending_dma)
        for d in lasts:
            d.sig = True
        for e in ENG:
            o = Op()
            o.eng, o.fn, o.sig, o.dma, o.sem, o.val = e, None, False, None, None, 0
            o.deps = [d for d in lasts if d.dma or d.eng != e]
            self.streams[e].append(o)
        self.lw, self.lr, self.pending_dma = {}, {}, []

    def finalize(self):
        for e in ENG:
            cnt = 0
            cur = None
            for o in self.streams[e]:
                if o.dma:
                    key = (e, o.dma)
                    if key not in self.slot_sem:
                        self.slot_sem[key] = self.newsem("d")
                        self.slot_cnt[key] = 0
                    self.slot_cnt[key] += 16
                    o.sem, o.val, o.semkey = self.slot_sem[key], self.slot_cnt[key], ("d",) + key
                elif o.sig:
                    if cur is None or cnt >= 30000:
                        cur = self.newsem("e" + e)
                        curkey = ("e", e, self.nsem)
                        cnt = 0
                    cnt += 1
                    o.sem, o.val, o.semkey = cur, cnt, curkey
        nw = 0
        for e in ENG:
            known = {}
            for o in self.streams[e]:
                need = {}
                for d in o.deps:
                    assert d.sem is not None
                    if known.get(d.semkey, 0) < d.val:
                        if need.get(d.semkey, (None, 0))[1] < d.val:
                            need[d.semkey] = (d.sem, d.val)
                for k, (sm, v) in need.items():
                    known[k] = v
                o.waits = list(need.values())
                nw += len(o.waits)
        self.nwaits = nw

    def replay(self, eng, e):
        for o in self.streams[eng]:
            for sm, v in o.waits:
                e.wait_ge(sm, v)
            if o.fn is not None:
                ins = o.fn(e)
                if o.sig:
                    ins.then_inc(o.sem, 16 if o.dma else 1)

    def emit(self):
        self.barrier()
        self.finalize()
        with self.nc.Block() as blk:
            @blk.tensor
            def _(e):
                self.replay("pe", e)

            @blk.scalar
            def _(e):
                self.replay("act", e)

            @blk.vector
            def _(e):
                self.replay("dve", e)

            @blk.gpsimd
            def _(e):
                self.replay("pool", e)

            @blk.sync
            def _(e):
                self.replay("sp", e)


EPS = 1e-6
T = 4096
NT = 32
NKEY = 4352
NKT = 34
DEBUG = False


def build_nc(stop=None, debug=False):
    nc = bass.Bass("TRN2", target_bir_lowering=False)
    S = Sched(nc)

    def din(name, shape, dt=F32):
        return nc.dram_tensor(name, list(shape), dt, kind="ExternalInput").ap()

    def dscr(name, shape, dt, out=False):
        return nc.dram_tensor(name, list(shape), dt, kind="ExternalOutput" if debug else "Internal").ap()

    x_d = din("x", [T, 1024])
    ctx_d = din("ctx", [256, 1024])
    c2_d = din("c2", [128, 8, 2])
    adaw_d = din("ada_w", [2, 1024, 6144])
    adab_d = din("adab2", [2, 2, 6144])
    nm_d = din("nmrep", [2, 128, 1024])
    nf_d = din("nfrep", [2, 128, 1024])
    win_d = din("attn_w_in", [1024, 2560])
    wout_d = din("attn_w_out", [1024, 1024])
    cwin_d = din("conv_w_in", [1024, 3072])
    cwout_d = din("conv_w_out", [1024, 1024])
    qk_d = din("qkcol", [128, 2])
    lam_d = din("lamrep", [128, 256])
    sub_d = din("sublnrep", [128, 128])
    convw_d = din("convw", [128, 8, 3])
    rw_d = din("router_w", [2, 1024, 16])
    wg_d = din("moe_w_gate", [2, 16, 1024, 1024])
    wu_d = din("moe_w_up", [2, 16, 1024, 1024])
    wd_d = din("moe_w_down", [2, 16, 1024, 1024])
    cpk_d = din("cpk", [128, 1152])
    tok_d = din("tokhl", [128, 1024])
    sel_d = din("sel", [2, 256])
    rope_d = din("rope", [2, 128, T])
    cs_d = din("csd", [256, 512])
    cos_d = din("cosd", [T, T], BF16)
    sin_d = din("sind", [T, T], BF16)
    out_d = nc.dram_tensor("out", [T, 1024], F32, kind="ExternalOutput").ap()

    kTd = dscr("kTd", [6, 128, NKEY], BF16)
    qTd = dscr("qTd", [6, 128, T], BF16)
    V1d = dscr("V1d", [NKEY, 774], BF16)
    fcsd = dscr("fcsd", [T, 512], BF16)
    oTd = dscr("oTd", [8, 128, T], BF16)
    x1d = dscr("x1d", [T, 1024], F32, DEBUG)
    x2d = dscr("x2d", [T, 1024], F32, DEBUG)
    x3d = dscr("x3d", [T, 1024], F32, DEBUG)
    hfd = dscr("hfd", [T, 1024], BF16)
    ymoe = dscr("ymoe", [T, 1024], F32)

    pbig = [nc.alloc_psum_tensor(f"pq{j}", [128, 1024], F32) for j in range(4)]
    pbigb = [p.bitcast(BF16) for p in pbig]
    ps = [pbig[i // 2][:, (i % 2) * 512:(i % 2 + 1) * 512] for i in range(8)]
    psb = [pbigb[i // 2][:, (i % 2) * 1024:(i % 2 + 1) * 1024] for i in range(8)]

    def DMA(eng, out, in_, r=(), w=(), slot=None):
        S.op(eng, lambda e: e.dma_start(out=out, in_=in_), r=r, w=w, dma=slot)

    def MM(out, lhsT, rhs, start, stop, r=(), w=()):
        S.op("pe", lambda e: e.matmul(out, lhsT, rhs, start=start, stop=stop), r=r, w=w)

    def TR(out, in_, ident, r=(), w=()):
        S.op("pe", lambda e: e.transpose(out, in_, ident), r=r, w=w)

    def ACT(out, in_, func, r=(), w=(), **kw):
        S.op("act", lambda e: e.activation(out=out, in_=in_, func=func, **kw), r=r, w=w)

    def TT(eng, out, in0, in1, op, r=(), w=()):
        S.op(eng, lambda e: e.tensor_tensor(out=out, in0=in0, in1=in1, op=op), r=r, w=w)

    def TS(eng, out, in0, s1, op0, s2=None, op1=None, r=(), w=(), accum=None):
        if op1 is None:
            S.op(eng, lambda e: e.tensor_single_scalar(out=out, in_=in0, scalar=s1, op=op0), r=r, w=w)
        else:
            S.op(eng, lambda e: e.tensor_scalar(out=out, in0=in0, scalar1=s1, scalar2=s2, op0=op0, op1=op1,
                                                accum_out=accum), r=r, w=w)

    def STT(eng, out, in0, scalar, in1, op0, op1, r=(), w=()):
        S.op(eng, lambda e: e.scalar_tensor_tensor(out=out, in0=in0, scalar=scalar, in1=in1, op0=op0, op1=op1),
             r=r, w=w)

    def CP(eng, out, in_, r=(), w=()):
        if eng == "act":
            ACT(out, in_, AF.Copy, r=r, w=w)
        else:
            S.op(eng, lambda e: e.tensor_copy(out=out, in_=in_), r=r, w=w)

    def RECIP(out, in_, r=(), w=()):
        S.op("dve", lambda e: e.reciprocal(out=out, in_=in_), r=r, w=w)

    def MSET(eng, ap, v, w=()):
        S.op(eng, lambda e: e.memset(ap, v), w=w)

    identf = S.alloc([128, 128], F32)
    identb = S.alloc([128, 128], BF16)
    bones = S.alloc([128, 128], BF16)
    RT = S.alloc([128, 128], BF16)
    Ust = S.alloc([128, 128], BF16)
    onesb = S.alloc([128, 128], BF16)
    iota = S.alloc([128, 512], F32)
    tokhl = S.alloc([128, 32, 16, 2], BF16)
    sel = S.alloc([2, 256], F32)
    MOD = S.alloc([128, 6, 1024], F32)
    CMOD = S.alloc([128, 2, 1024], F32)
    AFF = S.alloc([128, 32, 16], F32)
    small = S.alloc([128, 64], F32)
    DMA("sp", identf[:], cpk_d[:, 0:128], w=["identf"], slot="c0")
    DMA("sp", iota[:], cpk_d[:, 640:1152], w=["iota"], slot="c1")
    DMA("sp", sel[:], sel_d, w=["sel"], slot="c2")
    DMA("pool", identb[:], cpk_d[:, 0:128], w=["identb"], slot="c3")
    DMA("pool", bones[:], cpk_d[:, 128:256], w=["bones"], slot="c4")
    DMA("pool", RT[:], cpk_d[:, 256:384], w=["RT"], slot="c5")
    DMA("pool", Ust[:], cpk_d[:, 384:512], w=["Ust"], slot="c6")
    DMA("pool", onesb[:], cpk_d[:, 512:640], w=["onesb"], slot="c7")
    DMA("pool", tokhl.rearrange("p a b c -> p (a b c)"), tok_d, w=["tokhl"], slot="c8")
    S.mark_persistent()
    S.barrier()

    def ada(l, with_ctx):
        S.reset_phase()
        sc = S.alloc([128, 8, 2], F32)
        m2 = S.alloc([2, 6144], F32)
        ab = S.alloc([2, 6144], F32)
        wb = [S.alloc([128, 8, 512], F32) for _ in range(2)]
        nmt = S.alloc([128, 1024], F32)
        nft = S.alloc([128, 1024], F32)
        DMA("sp", sc[:], c2_d, w=["sc"], slot="a0")
        DMA("sp", ab[:], adab_d[l], w=["ab"], slot="a1")
        DMA("sp", nmt[:], nm_d[l], w=["nmt"], slot="a2")
        DMA("sp", nft[:], nf_d[l], w=["nft"], slot="a3")
        ACT(sc[:], sc[:], AF.Silu, r=["sc"], w=["sc"])
        wv = adaw_d[l].rearrange("(k p) n -> p k n", p=128)
        for cg in range(12):
            wt = wb[cg % 2]
            DMA("sp", wt[:], wv[:, :, cg * 512:(cg + 1) * 512], w=[("wb", cg % 2)], slot=f"aw{cg % 2}")
            pp = ps[cg % 2]
            for k in range(8):
                MM(pp[0:2, :], sc[:, k, :], wt[:, k, :], k == 0, k == 7, r=["sc", ("wb", cg % 2)], w=[("ps", cg % 2)])
            TT("dve", m2[:, cg * 512:(cg + 1) * 512], pp[0:2, :], ab[:, cg * 512:(cg + 1) * 512], ALU.add,
               r=[("ps", cg % 2), "ab"], w=["m2"])
        for j in range(12):
            pp = ps[2 + j % 2]
            MM(pp[:, :], sel[0:2, 0:128], m2[0:2, j * 512:(j + 1) * 512], True, True, r=["sel", "m2"], w=[("ps", 2 + j % 2)])
            CP("act" if j % 2 else "dve", MOD[:, j // 2, (j % 2) * 512:(j % 2 + 1) * 512], pp[:, :],
               r=[("ps", 2 + j % 2)], w=["MOD"])
            if with_ctx and j < 4:
                pq = ps[4 + j % 2]
                MM(pq[:, :], sel[0:2, 128:256], m2[0:2, j * 512:(j + 1) * 512], True, True, r=["sel", "m2"],
                   w=[("ps", 4 + j % 2)])
                CP("act" if j % 2 else "dve", CMOD[:, j // 2, (j % 2) * 512:(j % 2 + 1) * 512], pq[:, :],
                   r=[("ps", 4 + j % 2)], w=["CMOD"])
        STT("dve", MOD[:, 1, :], MOD[:, 1, :], 1.0, nmt[:], ALU.add, ALU.mult, r=["MOD", "nmt"], w=["MOD"])
        TS("dve", MOD[:, 1, :], MOD[:, 1, :], 32.0, ALU.mult, r=["MOD"], w=["MOD"])
        STT("dve", MOD[:, 4, :], MOD[:, 4, :], 1.0, nft[:], ALU.add, ALU.mult, r=["MOD", "nft"], w=["MOD"])
        TS("dve", MOD[:, 4, :], MOD[:, 4, :], 32.0, ALU.mult, r=["MOD"], w=["MOD"])
        if with_ctx:
            STT("dve", CMOD[:, 1, :], CMOD[:, 1, :], 1.0, nmt[:], ALU.add, ALU.mult, r=["CMOD", "nmt"], w=["CMOD"])
            TS("dve", CMOD[:, 1, :], CMOD[:, 1, :], 32.0, ALU.mult, r=["CMOD"], w=["CMOD"])
        S.barrier()

    def modulate(xt, xkey, G32, SH, out, okey, st, skey, tmp, tkey, junk, jkey):
        ACT(junk, xt, AF.Square, r=[xkey], w=[jkey, skey], accum_out=st[:, 0:1])
        ACT(st[:, 1:2], st[:, 0:1], AF.Sqrt, r=[skey], w=[skey], bias=1024.0 * EPS, scale=1.0)
        RECIP(st[:, 2:3], st[:, 1:2], r=[skey], w=[skey])
        STT("dve", tmp, xt, st[:, 2:3], G32, ALU.mult, ALU.mult, r=[xkey, skey, "MOD", "CMOD"], w=[tkey])
        TT("pool", out, tmp, SH, ALU.add, r=[tkey, "MOD", "CMOD"], w=[okey])

    def proj0():
        S.reset_phase()
        w = S.alloc([128, 8, 2560], BF16)
        cosT = S.alloc([128, T], F32)
        sinT = S.alloc([128, T], F32)
        qk = S.alloc([128, 2], F32)
        CS = S.alloc([128, 2, 512], BF16)
        xb = [S.alloc([128, 1024], F32) for _ in range(2)]
        tmpb = [S.alloc([128, 1024], F32) for _ in range(2)]
        hb = [S.alloc([128, 1024], BF16) for _ in range(2)]
        junk = S.alloc([128, 1024], BF16)
        stt = [S.alloc([128, 4], F32) for _ in range(2)]
        hTb = [S.alloc([128, 8, 512], BF16) for _ in range(2)]
        sqb = S.alloc([128, 512], BF16)
        rs = S.alloc([128, 512], F32)
        knb = [S.alloc([128, 512], BF16) for _ in range(2)]
        t1 = S.alloc([128, 512], F32)
        t2 = S.alloc([128, 512], F32)
        kout = [S.alloc([128, 512], BF16) for _ in range(2)]
        v1t = [S.alloc([128, 6, 129], BF16) for _ in range(2)]
        fT = S.alloc([128, 2, 512], BF16)
        fcst = [S.alloc([128, 512], BF16) for _ in range(2)]
        wv = win_d.rearrange("(k p) n -> p k n", p=128)
        for q in range(4):
            DMA("pool", w[:, :, q * 640:(q + 1) * 640], wv[:, :, q * 640:(q + 1) * 640], w=["w"], slot=f"pw{q}")
        DMA("sp", cosT[:], rope_d[0], w=["cos"], slot="p0")
        DMA("sp", sinT[:], rope_d[1], w=["sin"], slot="p1")
        DMA("sp", qk[:], qk_d, w=["qk"], slot="p2")
        DMA("pool", CS[:], cs_d.rearrange("(c p) n -> p c n", p=128), w=["CS"], slot="p3")
        TS("dve", qk[:, 1:2], qk[:, 1:2], 8.0, ALU.mult, r=["qk"], w=["qk"])
        for i in range(2):
            MSET("pool", v1t[i][:], 1.0, w=[("v1t", i)])
        nrc = [0]

        def normrope(pp, pkey, nt, gain, rope, pos0, dst):
            c = nrc[0]
            nrc[0] += 1
            kn = knb[c % 2]
            ko = kout[c % 2]
            ACT(sqb[:, :nt], pp[:, :nt], AF.Square, r=[pkey], w=["sqb"])
            MM(ps[4][:, :nt], bones[:], sqb[:, :nt], True, True, r=["bones", "sqb"], w=[("ps", 4)])
            ACT(rs[:, :nt], ps[4][:, :nt], AF.Sqrt, r=[("ps", 4)], w=["rs"], bias=64.0 * EPS, scale=1.0)
            RECIP(rs[:, :nt], rs[:, :nt], r=["rs"], w=["rs"])
            STT("dve", kn[:, :nt], pp[:, :nt], gain, rs[:, :nt], ALU.mult, ALU.mult, r=[pkey, "qk", "rs"],
                w=[("kn", c % 2)])
            if rope:
                MM(ps[5][:, :nt], RT[:], kn[:, :nt], True, True, r=["RT", ("kn", c % 2)], w=[("ps", 5)])
                TT("pool", t1[:, :nt], kn[:, :nt], cosT[:, pos0:pos0 + nt], ALU.mult, r=[("kn", c % 2), "cos"], w=["t1"])
                TT("dve", t2[:, :nt], ps[5][:, :nt], sinT[:, pos0:pos0 + nt], ALU.mult, r=[("ps", 5), "sin"], w=["t2"])
                TT("pool", ko[:, :nt], t1[:, :nt], t2[:, :nt], ALU.add, r=["t1", "t2"], w=[("ko", c % 2)])
                DMA("sp", dst, ko[:, :nt], r=[("ko", c % 2)], slot=f"ko{c % 2}")
            else:
                DMA("sp", dst, kn[:, :nt], r=[("kn", c % 2)], slot=f"kn{c % 2}")

        pc = [0]
        for b in range(9):
            isctx = b == 8
            ntile = 2 if isctx else 4
            nt = ntile * 128
            hT = hTb[b % 2]
            hk = ("hT", b % 2)
            for t in range(ntile):
                i2 = t % 2
                src = ctx_d[t * 128:(t + 1) * 128, :] if isctx else x_d[b * 512 + t * 128: b * 512 + (t + 1) * 128, :]
                DMA("sp", xb[i2][:], src, w=[("xb", i2)], slot=f"xb{i2}")
                G = CMOD[:, 1, :] if isctx else MOD[:, 1, :]
                SHh = CMOD[:, 0, :] if isctx else MOD[:, 0, :]
                modulate(xb[i2][:], ("xb", i2), G, SHh, hb[i2][:], ("hb", i2), stt[i2], ("st", i2),
                         tmpb[i2][:], ("tmp", i2), junk[:], "junk")
                pT = psb[t % 2]
                for k in range(8):
                    TR(pT[:, k * 128:(k + 1) * 128], hb[i2][:, k * 128:(k + 1) * 128], identb[:],
                       r=[("hb", i2), "identb"], w=[("ps", t % 2)])
                CP("act", hT[:, :, t * 128:(t + 1) * 128], pT[:, 0:1024].rearrange("p (k n) -> p k n", k=8),
                   r=[("ps", t % 2)], w=[hk])
            for h in range(6):
                for which in ((0, 1) if not isctx else (1,)):
                    bank = 2 + pc[0] % 2
                    pc[0] += 1
                    col0 = (768 if which == 1 else 0) + h * 128
                    for k in range(8):
                        MM(ps[bank][:, :nt], w[:, k, col0:col0 + 128], hT[:, k, :nt], k == 0, k == 7,
                           r=["w", hk], w=[("ps", bank)])
                    dst = (kTd if which == 1 else qTd)[h, :, b * 512:b * 512 + nt]
                    normrope(ps[bank], ("ps", bank), nt, qk[:, which:which + 1], not isctx, b * 512 if not isctx else 0, dst)
            for t in range(ntile):
                vt = v1t[t % 2]
                for half in range(2):
                    bank = 6 + half
                    for k in range(8):
                        MM(ps[bank][:, 0:384], hT[:, k, t * 128:(t + 1) * 128],
                           w[:, k, 1536 + half * 384:1536 + (half + 1) * 384], k == 0, k == 7, r=["w", hk],
                           w=[("ps", bank)])
                    CP("act" if half else "dve", vt[:, half * 3:(half + 1) * 3, 0:128],
                       ps[bank][:, 0:384].rearrange("p (a b) -> p a b", a=3), r=[("ps", bank)], w=[("v1t", t % 2)])
                row0 = b * 512 + t * 128
                DMA("sp", V1d[row0:row0 + 128, :], vt.rearrange("p a b -> p (a b)"), r=[("v1t", t % 2)], slot=f"v1t{t % 2}")
            if not isctx:
                for c in range(2):
                    bank = 2 + pc[0] % 2
                    pc[0] += 1
                    for k in range(8):
                        MM(ps[bank][:, :], w[:, k, 2304 + c * 128:2304 + (c + 1) * 128], hT[:, k, :], k == 0, k == 7,
                           r=["w", hk], w=[("ps", bank)])
                    CP("act", fT[:, c, :], ps[bank][:, :], r=[("ps", bank)], w=["fT"])
                for t in range(4):
                    bank = 6 + t % 2
                    for c in range(2):
                        MM(ps[bank][:, :], fT[:, c, t * 128:(t + 1) * 128], CS[:, c, :], c == 0, c == 1, r=["fT", "CS"],
                           w=[("ps", bank)])
                    CP("dve", fcst[t % 2][:], ps[bank][:, :], r=[("ps", bank)], w=[("fcst", t % 2)])
                    row0 = b * 512 + t * 128
                    DMA("sp", fcsd[row0:row0 + 128, :], fcst[t % 2][:], r=[("fcst", t % 2)], slot=f"fc{t % 2}")
        S.barrier()

    def attn():
        S.reset_phase()
        kT = S.alloc([128, 6, NKEY], BF16)
        V1 = S.alloc([128, NKT, 774], BF16)
        lamt = S.alloc([128, 256], F32)
        SUB = S.alloc([128, 128], F32)
        qb = [S.alloc([128, 2, 6, 512], BF16) for _ in range(2)]
        ob = [S.alloc([128, 6, 512], BF16) for _ in range(2)]
        for i in range(2):
            MSET("pool", qb[i].rearrange("p a b c -> p (a b c)"), 0.0, w=[("qT", i)])
        a1 = S.alloc([128, 128], F32)
        att = S.alloc([128, 128], F32)
        attb4 = S.alloc([128, 4, 128], BF16)
        junk = S.alloc([128, 128], BF16)
        sm = S.alloc([128, 16], F32)
        accS = S.alloc([128, 3, 387], F32)
        for h in range(6):
            DMA("sp", kT[:, h, :], kTd[h], w=["kT"], slot=f"kT{h % 2}")
        v1v = V1d.rearrange("(t p) c -> p t c", p=128)
        for q in range(2):
            DMA("sp", V1[:, q * 17:(q + 1) * 17, :], v1v[:, q * 17:(q + 1) * 17, :], w=["V1"], slot=f"V1{q}")
        DMA("sp", lamt[:], lam_d, w=["lamt"], slot="lam")
        DMA("sp", SUB[:], sub_d, w=["SUB"], slot="sub")
        TS("dve", SUB[:], SUB[:], 0.8, ALU.mult, r=["SUB"], w=["SUB"])
        TT("dve", lamt[:, 0:64], lamt[:, 0:64], lamt[:, 64:128], ALU.mult, r=["lamt"], w=["lamt"])
        TT("dve", lamt[:, 128:192], lamt[:, 128:192], lamt[:, 192:256], ALU.mult, r=["lamt"], w=["lamt"])
        S.op("dve", lambda e: e.reduce_sum(out=sm[:, 0:1], in_=lamt[:, 0:64], axis=AX.X), r=["lamt"], w=["sm"])
        S.op("dve", lambda e: e.reduce_sum(out=sm[:, 1:2], in_=lamt[:, 128:192], axis=AX.X), r=["lamt"], w=["sm"])
        ACT(sm[:, 0:2], sm[:, 0:2], AF.Exp, r=["sm"], w=["sm"])
        TT("dve", sm[:, 2:3], sm[:, 1:2], sm[:, 0:1], ALU.subtract, r=["sm"], w=["sm"])
        TS("dve", sm[:, 2:3], sm[:, 2:3], -0.2, ALU.add, r=["sm"], w=["sm"])
        nlam = sm[:, 2:3]
        accs = {}
        for m in range(2):
            for qs in range(4):
                j = m * 4 + qs
                accs[(m, qs)] = ps[4 + j // 3][:, (j % 3) * 129:(j % 3) * 129 + 129]
        PT2 = [S.alloc([128, 1024], BF16) for _ in range(2)]
        steps = [(h, m, kt) for h in range(6) for m in range(2) for kt in range(NKT)]
        npair = len(steps) // 2
        for g in range(8):
            qT = qb[g % 2]
            for m_ in range(2):
                DMA("sp", qT[m_ * 64:(m_ + 1) * 64, m_, :, :],
                    qTd[:, m_ * 64:(m_ + 1) * 64, g * 512:(g + 1) * 512].rearrange("h p n -> p h n"), r=[("qT", g % 2)],
                    w=[("qT", g % 2)], slot=f"qT{g % 2}")
            oT = ob[g % 2]

            def QKEXP(p):
                pb = p % 2
                for u in range(2):
                    h, m, kt = steps[2 * p + u]
                    MM(pbig[pb][:, u * 512:(u + 1) * 512], kT[:, h, kt * 128:(kt + 1) * 128],
                       qT[:, m, h, :], True, True, r=["kT", ("qT", g % 2)], w=[("pq", pb)])
                ACT(PT2[pb][:], pbig[pb][:, :], AF.Exp, r=[("pq", pb)], w=[("PT", pb)])

            def AV(p):
                pb = p % 2
                for u in range(2):
                    h, m, kt = steps[2 * p + u]
                    for qs in range(4):
                        st_ = (kt == 0 and (m * 4 + qs) % 3 == 0)
                        S.op("pe", lambda e, o_=accs[(m, qs)], l_=PT2[pb][:, u * 512 + qs * 128:u * 512 + (qs + 1) * 128],
                             r_=V1[:, kt, h * 129:(h + 1) * 129], st_=st_, sp_=(kt == NKT - 1):
                             e.matmul(o_, l_, r_, start=st_, stop=sp_, skip_group_check=True),
                             r=[("PT", pb), "V1"], w=[("ps", 4 + (m * 4 + qs) // 3)])

            def EPI(h):
                for bk in range(3):
                    wdt = 387 if bk < 2 else 258
                    CP("act", accS[:, bk, 0:wdt], ps[4 + bk][:, 0:wdt], r=[("ps", 4 + bk)], w=["accS"])
                for qs in range(4):
                    j0, j1 = qs, 4 + qs
                    A0 = accS[:, j0 // 3, (j0 % 3) * 129:(j0 % 3) * 129 + 129]
                    A1 = accS[:, j1 // 3, (j1 % 3) * 129:(j1 % 3) * 129 + 129]
                    RECIP(sm[:, 4:5], A0[:, 128:129], r=["accS"], w=["sm4"])
                    RECIP(sm[:, 5:6], A1[:, 128:129], r=["accS"], w=["sm5"])
                    TT("dve", sm[:, 6:7], sm[:, 5:6], nlam, ALU.mult, r=["sm5", "sm"], w=["sm6"])
                    TS("dve", a1[:], A0[:, 0:128], sm[:, 4:5], ALU.mult, r=["accS", "sm4"], w=["a1"])
                    STT("dve", att[:], A1[:, 0:128], sm[:, 6:7], a1[:], ALU.mult, ALU.add, r=["accS", "sm6", "a1"],
                        w=["att"])
                    ACT(junk[:], att[:], AF.Square, r=["att"], w=["junkA", "sm7"], accum_out=sm[:, 7:8])
                    ACT(sm[:, 8:9], sm[:, 7:8], AF.Sqrt, r=["sm7"], w=["sm8"], bias=EPS, scale=1.0 / 128.0)
                    RECIP(sm[:, 9:10], sm[:, 8:9], r=["sm8"], w=["sm9"])
                    STT("dve", attb4[:, qs, :], att[:], sm[:, 9:10], SUB[:], ALU.mult, ALU.mult, r=["att", "sm9", "SUB"],
                        w=["attb"])

            def EPI_T(h):
                for qs in range(4):
                    TR(psb[7][:, qs * 128:(qs + 1) * 128], attb4[:, qs, :], identb[:], r=["attb", "identb"], w=[("ps", 7)])
                CP("act", oT[:, h, :], psb[7][:, 0:512], r=[("ps", 7)], w=[("oT", g % 2)])

            QKEXP(0)
            pend = None
            for p in range(npair):
                if p + 1 < npair:
                    QKEXP(p + 1)
                AV(p)
                if pend is not None and p == pend[1]:
                    EPI_T(pend[0])
                    pend = None
                hh, mm, kk = steps[2 * p + 1]
                if mm == 1 and kk == NKT - 1:
                    EPI(hh)
                    pend = (hh, p + 6)
                    if p == npair - 1:
                        EPI_T(hh)
                        pend = None
            DMA("sp", oTd[0:6, :, g * 512:(g + 1) * 512].rearrange("c p n -> p c n"), oT[:], r=[("oT", g % 2)], slot=f"oT{g % 2}")
        S.barrier()

    def fourier():
        S.reset_phase()
        fcs = S.alloc([128, 32, 512], BF16)
        cb = [S.alloc([128, 16, 512], BF16) for _ in range(2)]
        sb = [S.alloc([128, 16, 512], BF16) for _ in range(2)]
        oF = [S.alloc([128, 2, 512], BF16) for _ in range(2)]
        DMA("sp", fcs[:], fcsd.rearrange("(t p) c -> p t c", p=128), w=["fcs"], slot="fcs")
        cv = cos_d.rearrange("(t p) k -> p t k", p=128)
        sv = sin_d.rearrange("(t p) k -> p t k", p=128)
        n = 0
        for g in range(8):
            for half in range(2):
                cbb, sbb = cb[n % 2], sb[n % 2]
                DMA("sp", cbb[:], cv[:, half * 16:(half + 1) * 16, g * 512:(g + 1) * 512], w=[("cb", n % 2)], slot=f"cb{n % 2}")
                DMA("act", sbb[:], sv[:, half * 16:(half + 1) * 16, g * 512:(g + 1) * 512], w=[("sb", n % 2)], slot=f"sb{n % 2}")
                for c in range(2):
                    for t in range(16):
                        tt = half * 16 + t
                        MM(ps[c][:, :], fcs[:, tt, c * 128:(c + 1) * 128], cbb[:, t, :], tt == 0, False,
                           r=["fcs", ("cb", n % 2)], w=[("ps", c)])
                        MM(ps[c][:, :], fcs[:, tt, 256 + c * 128:256 + (c + 1) * 128], sbb[:, t, :], False, tt == 31,
                           r=["fcs", ("sb", n % 2)], w=[("ps", c)])
                n += 1
            for c in range(2):
                CP("dve" if c else "act", oF[g % 2][:, c, :], ps[c][:, :], r=[("ps", c)], w=[("oF", g % 2)])
            DMA("sp", oTd[6:8, :, g * 512:(g + 1) * 512].rearrange("c p n -> p c n"), oF[g % 2][:], r=[("oF", g % 2)],
                slot=f"oF{g % 2}")
        S.barrier()

    def outproj(l, wsrc, xin, xout):
        S.reset_phase()
        wo = S.alloc([128, 8, 1024], BF16)
        wr = S.alloc([128, 8, 16], F32)
        ob = [S.alloc([128, 8, 128], BF16) for _ in range(2)]
        xb = [S.alloc([128, 1024], F32) for _ in range(2)]
        yb = [S.alloc([128, 1024], F32) for _ in range(2)]
        x1b = [S.alloc([128, 1024], F32) for _ in range(2)]
        tmpb = [S.alloc([128, 1024], F32) for _ in range(2)]
        hfb = [S.alloc([128, 1024], F32) for _ in range(2)]
        hbb = [S.alloc([128, 1024], BF16) for _ in range(2)]
        hfT = [S.alloc([128, 8, 128], F32) for _ in range(2)]
        junk = S.alloc([128, 1024], BF16)
        stt = [S.alloc([128, 8], F32) for _ in range(2)]
        ex = S.alloc([128, 16], F32)
        wv = wsrc.rearrange("(k p) n -> p k n", p=128)
        for q in range(2):
            DMA("pool", wo[:, :, q * 512:(q + 1) * 512], wv[:, :, q * 512:(q + 1) * 512], w=["wo"], slot=f"wo{q}")
        DMA("sp", wr[:], rw_d[l].rearrange("(k p) e -> p k e", p=128), w=["wr"], slot="wr")
        for i in range(NT):
            i2 = i % 2
            DMA("sp", ob[i2][:], oTd[:, :, i * 128:(i + 1) * 128].rearrange("c p n -> p c n"), w=[("ob", i2)], slot=f"ob{i2}")
            DMA("sp", xb[i2][:], xin[i * 128:(i + 1) * 128, :], w=[("xb", i2)], slot=f"xb{i2}")
            for half in range(2):
                for c in range(8):
                    MM(ps[half][:, :], ob[i2][:, c, :], wo[:, c, half * 512:(half + 1) * 512], c == 0, c == 7,
                       r=[("ob", i2), "wo"], w=[("ps", half)])
                TT("dve", yb[i2][:, half * 512:(half + 1) * 512], ps[half][:, :], MOD[:, 2, half * 512:(half + 1) * 512],
                   ALU.mult, r=[("ps", half), "MOD"], w=[("yb", i2)])
            TT("pool", x1b[i2][:], yb[i2][:], xb[i2][:], ALU.add, r=[("yb", i2), ("xb", i2)], w=[("x1b", i2)])
            DMA("sp", xout[i * 128:(i + 1) * 128, :], x1b[i2][:], r=[("x1b", i2)], slot=f"x1o{i2}")
            modulate(x1b[i2][:], ("x1b", i2), MOD[:, 4, :], MOD[:, 3, :], hfb[i2][:], ("hfb", i2), stt[i2], ("st", i2),
                     tmpb[i2][:], ("tmp", i2), junk[:], "junk")
            CP("act", hbb[i2][:], hfb[i2][:], r=[("hfb", i2)], w=[("hbb", i2)])
            DMA("sp", hfd[i * 128:(i + 1) * 128, :], hbb[i2][:], r=[("hbb", i2)], slot=f"hfo{i2}")
            for k in range(8):
                bank = 2 + k // 4
                TR(ps[bank][:, (k % 4) * 128:(k % 4 + 1) * 128], hfb[i2][:, k * 128:(k + 1) * 128], identf[:],
                   r=[("hfb", i2), "identf"], w=[("ps", bank)])
            CP("act", hfT[i2][:, 0:4, :], ps[2][:, :].rearrange("p (k n) -> p k n", k=4), r=[("ps", 2)], w=[("hfT", i2)])
            CP("dve", hfT[i2][:, 4:8, :], ps[3][:, :].rearrange("p (k n) -> p k n", k=4), r=[("ps", 3)], w=[("hfT", i2)])
            for k in range(8):
                MM(ps[4][:, 0:16], hfT[i2][:, k, :], wr[:, k, :], k == 0, k == 7, r=[("hfT", i2), "wr"], w=[("ps", 4)])
            st = stt[i2]
            S.op("dve", lambda e, st=st: e.reduce_max(out=st[:, 4:5], in_=ps[4][:, 0:16], axis=AX.X), r=[("ps", 4)],
                 w=[("st5", i2)])
            TS("dve", st[:, 5:6], st[:, 4:5], -1.0, ALU.mult, r=[("st5", i2)], w=[("st6", i2)])
            ACT(ex[:], ps[4][:, 0:16], AF.Exp, r=[("ps", 4), ("st6", i2)], w=["ex", ("st7", i2)], bias=st[:, 5:6], scale=1.0,
                accum_out=st[:, 6:7])
            RECIP(st[:, 7:8], st[:, 6:7], r=[("st7", i2)], w=[("st8", i2)])
            TS("dve", AFF[:, i, :], ex[:], st[:, 7:8], ALU.mult, r=["ex", ("st8", i2)], w=["AFF"])
        S.barrier()

    def moe(l):
        S.reset_phase()
        slotm = S.alloc([128, 32, 16], F32)
        VALS = S.alloc([128, 32, 16, 5], BF16)
        VA = S.alloc([128, 512, 20], BF16)
        mark = S.sb_off
        affT = S.alloc([16, T], F32)
        junkb = S.alloc([16, T], BF16)
        maskT = S.alloc([16, T], BF16)
        bs = S.alloc([16, 8], F32)
        MASK = S.alloc([128, 512], F32)
        MASKb = S.alloc([128, 512], BF16)
        tots = S.alloc([128, 32, 16], F32)
        base = S.alloc([128, 32, 16], F32)
        r1 = S.alloc([128, 512], F32)
        gtmp = S.alloc([128, 512], BF16)
        zt = S.alloc([128, 2, 1024], F32)
        bm = S.alloc([128, 512], F32)
        ag = S.alloc([128, 512], F32)
        aoh = S.alloc([128, 512], BF16)
        AFFf = AFF.rearrange("p a b -> p (a b)")
        MSET("pool", zt[:], 0.0, w=["zt"])
        yv = ymoe.rearrange("(t p) d -> p t d", p=128)
        for q in range(16):
            DMA("sp", yv[:, q * 2:(q + 1) * 2, :], zt[:], r=["zt"], w=[("ymoe", 0)], slot="zy")
        for i in range(NT):
            bank = i // 4 % 2
            TR(ps[bank][0:16, (i % 4) * 128:(i % 4 + 1) * 128], AFF[:, i, :], identf[:], r=["AFF", "identf"], w=[("ps", bank)])
            if i % 4 == 3:
                CP("act", affT[:, (i // 4) * 512:(i // 4 + 1) * 512], ps[bank][0:16, :], r=[("ps", bank)], w=["affT"])
        MSET("dve", bs[:], 0.0, w=["bs"])
        for it in range(28):
            step = 2.0 ** -(it + 1)
            TS("dve", bs[:, 1:2], bs[:, 0:1], step, ALU.add, r=["bs"], w=["bs1"])
            TS("dve", junkb[:], affT[:], bs[:, 1:2], ALU.is_gt, 0.0, ALU.add, r=["affT", "bs1"], w=["junkb", "bs2"],
               accum=bs[:, 2:3])
            TS("dve", bs[:, 3:4], bs[:, 2:3], 511.5, ALU.is_gt, step, ALU.mult, r=["bs2"], w=["bs3"])
            TT("dve", bs[:, 0:1], bs[:, 0:1], bs[:, 3:4], ALU.add, r=["bs", "bs3"], w=["bs"])
        TS("dve", maskT[:], affT[:], bs[:, 0:1], ALU.is_gt, r=["affT", "bs"], w=["maskT"])
        for i in range(NT):
            TR(psb[2][:, i * 16:(i + 1) * 16], maskT[:, i * 128:(i + 1) * 128], identb[0:16, 0:16], r=["maskT", "identb"],
               w=[("ps", 2)])
        CP("act", MASKb[:], psb[2][:, 0:512], r=[("ps", 2)], w=["MASKb"])
        CP("dve", MASK[:], MASKb[:], r=["MASKb"], w=["MASK"])
        MM(ps[3][:, :], Ust[:], MASKb[:], True, True, r=["Ust", "MASKb"], w=[("ps", 3)])
        MM(ps[4][:, :], onesb[:], MASKb[:], True, True, r=["onesb", "MASKb"], w=[("ps", 4)])
        CP("act", tots.rearrange("p a b -> p (a b)"), ps[4][:, :], r=[("ps", 4)], w=["tots"])
        MSET("dve", base[:, 0, :], 0.0, w=["base"])
        for i in range(1, NT):
            TT("dve", base[:, i, :], base[:, i - 1, :], tots[:, i - 1, :], ALU.add, r=["base", "tots"], w=["base"])
        sf = slotm.rearrange("p a b -> p (a b)")
        TT("dve", sf, ps[3][:, :], base.rearrange("p a b -> p (a b)"), ALU.add, r=[("ps", 3), "base"], w=["slotm"])
        TS("dve", ag[:], sf, 128.0, ALU.is_ge, r=["slotm"], w=["ag"])
        STT("dve", ag[:], sf, 256.0, ag[:], ALU.is_ge, ALU.add, r=["slotm", "ag"], w=["ag"])
        STT("dve", ag[:], sf, 384.0, ag[:], ALU.is_ge, ALU.add, r=["slotm", "ag"], w=["ag"])
        STT("dve", bm[:], ag[:], -128.0, sf, ALU.mult, ALU.add, r=["slotm", "ag"], w=["bm"])
        STT("dve", sf, bm[:], 1.0, MASK[:], ALU.add, ALU.mult, r=["bm", "MASK"], w=["slotm"])
        TS("dve", sf, sf, -1.0, ALU.add, r=["slotm"], w=["slotm"])
        CP("pool", VALS[:, :, :, 0:2], tokhl[:], r=["tokhl"], w=["VALS"])
        Vg = lambda j: VALS[:, :, :, j].rearrange("p a b -> p (a b)")
        CP("dve", gtmp[:], AFFf, r=["AFF"], w=["gtmp"])
        CP("dve", VALS[:, :, :, 2], gtmp.rearrange("p (a b) -> p a b", a=32), r=["gtmp"], w=["VALS"])
        TT("dve", r1[:], AFFf, gtmp[:], ALU.subtract, r=["AFF", "gtmp"], w=["r1"])
        CP("dve", gtmp[:], r1[:], r=["r1"], w=["gtmp"])
        CP("dve", VALS[:, :, :, 3], gtmp.rearrange("p (a b) -> p a b", a=32), r=["gtmp"], w=["VALS"])
        TT("dve", r1[:], r1[:], gtmp[:], ALU.subtract, r=["r1", "gtmp"], w=["r1"])
        CP("dve", VALS[:, :, :, 4], r1.rearrange("p (a b) -> p a b", a=32), r=["r1"], w=["VALS"])
        VALf = VALS.rearrange("p a b c -> p (a b) c")
        for a_ in range(4):
            TS("dve", aoh[:], ag[:], float(a_), ALU.is_equal, r=["ag"], w=["aoh"])
            for v_ in range(5):
                TT("dve", VA[:, :, v_ * 4 + a_], VALf[:, :, v_], aoh[:], ALU.mult, r=["VALS", "aoh"], w=["VA"])

        S.barrier()
        S.sb_off = mark
        wgb = [S.alloc([128, 8, 1024], BF16) for _ in range(2)]
        wub = [S.alloc([128, 8, 1024], BF16) for _ in range(2)]
        wdb = [S.alloc([128, 8, 1024], BF16) for _ in range(2)]
        selb = [S.alloc([128, 128], BF16) for _ in range(4)]
        ivt = S.alloc([128, 5, 4], F32)
        tokf = S.alloc([128, 4], F32)
        idx = [S.alloc([128, 4], I32) for _ in range(2)]
        gg = [S.alloc([128, 4], F32) for _ in range(2)]
        xs = [S.alloc([128, 1024], BF16) for _ in range(4)]
        xsT = S.alloc([128, 8, 512], BF16)
        sg = [S.alloc([128, 512], F32) for _ in range(2)]
        aT = S.alloc([128, 8, 512], BF16)
        ysb = [S.alloc([128, 1024], F32) for _ in range(2)]
        yc = [0]

        def W(ex_):
            e2 = ex_ % 2
            for (buf, src, nm) in ((wgb, wg_d, "wg"), (wub, wu_d, "wu"), (wdb, wd_d, "wd")):
                sv = src[l, ex_].rearrange("(k p) n -> p k n", p=128)
                for q in range(2):
                    DMA("pool", buf[e2][:, :, q * 512:(q + 1) * 512], sv[:, :, q * 512:(q + 1) * 512], w=[(nm, e2)],
                        slot=f"{nm}{e2}{q}")

        def IDX(ex_):
            e2 = ex_ % 2
            for i in range(NT):
                sb_ = selb[i % 4]
                TS("dve", sb_[:], iota[:, 0:128], slotm[:, i, ex_:ex_ + 1], ALU.is_equal,
                   r=["iota", "slotm"], w=[("selb", i % 4)])
                MM(ps[5][:, 0:20], sb_[:], VA[:, i * 16 + ex_, :], i == 0, i == NT - 1, r=["VA", ("selb", i % 4)], w=[("ps", 5)])
            CP("dve", ivt.rearrange("p a b -> p (a b)"), ps[5][:, 0:20], r=[("ps", 5)], w=["ivt"])
            STT("dve", tokf[:], ivt[:, 0, :], 64.0, ivt[:, 1, :], ALU.mult, ALU.add, r=["ivt"], w=["tokf"])
            CP("dve", idx[e2][:], tokf[:], r=["tokf"], w=[("idx", e2)])
            TT("dve", gg[e2][:], ivt[:, 2, :], ivt[:, 3, :], ALU.add, r=["ivt"], w=[("gg", e2)])
            TT("dve", gg[e2][:], gg[e2][:], ivt[:, 4, :], ALU.add, r=["ivt", ("gg", e2)], w=[("gg", e2)])

        def G(ex_):
            e2 = ex_ % 2
            for grp in range(4):
                S.op("pool", lambda e, grp=grp, e2=e2: e.indirect_dma_start(
                    out=xs[grp][:], out_offset=None, in_=hfd,
                    in_offset=bass.IndirectOffsetOnAxis(ap=idx[e2][:, grp:grp + 1], axis=0)),
                    r=[("idx", e2)], w=[("xs", grp)], dma=f"xs{grp}")

        def XT(ex_):
            for grp in range(4):
                pT = psb[6 + grp % 2]
                for k in range(8):
                    TR(pT[:, k * 128:(k + 1) * 128], xs[grp][:, k * 128:(k + 1) * 128], identb[:], r=[("xs", grp), "identb"],
                       w=[("ps", 6 + grp % 2)])
                CP("act" if grp % 2 else "dve", xsT[:, :, grp * 128:(grp + 1) * 128],
                   pT[:, 0:1024].rearrange("p (k n) -> p k n", k=8), r=[("ps", 6 + grp % 2)], w=["xsT"])

        def FFN(ex_):
            e2 = ex_ % 2
            for fc in range(8):
                for k in range(8):
                    MM(ps[0][:, :], wgb[e2][:, k, fc * 128:(fc + 1) * 128], xsT[:, k, :], k == 0, k == 7, r=[("wg", e2), "xsT"],
                       w=[("ps", 0)])
                for k in range(8):
                    MM(ps[1][:, :], wub[e2][:, k, fc * 128:(fc + 1) * 128], xsT[:, k, :], k == 0, k == 7, r=[("wu", e2), "xsT"],
                       w=[("ps", 1)])
                ACT(sg[fc % 2][:], ps[0][:, :], AF.Silu, r=[("ps", 0)], w=[("sg", fc % 2)])
                TT("dve", aT[:, fc, :], sg[fc % 2][:], ps[1][:, :], ALU.mult, r=[("sg", fc % 2), ("ps", 1)], w=["aT"])
            for grp in range(4):
                y2 = yc[0] % 2
                yc[0] += 1
                for half in range(2):
                    bank = 2 + half
                    for fc in range(8):
                        MM(ps[bank][:, :], aT[:, fc, grp * 128:(grp + 1) * 128], wdb[e2][:, fc, half * 512:(half + 1) * 512],
                           fc == 0, fc == 7, r=["aT", ("wd", e2)], w=[("ps", bank)])
                    if half:
                        ACT(ysb[y2][:, 512:1024], ps[bank][:, :], AF.Copy, r=[("ps", bank), ("gg", e2)], w=[("ysb", y2)],
                            scale=gg[e2][:, grp:grp + 1])
                    else:
                        TS("dve", ysb[y2][:, 0:512], ps[bank][:, :], gg[e2][:, grp:grp + 1], ALU.mult,
                           r=[("ps", bank), ("gg", e2)], w=[("ysb", y2)])
                S.op("pool", lambda e, grp=grp, e2=e2, y2=y2: e.indirect_dma_start(
                    out=ymoe, out_offset=bass.IndirectOffsetOnAxis(ap=idx[e2][:, grp:grp + 1], axis=0),
                    in_=ysb[y2][:], in_offset=None, compute_op=ALU.add),
                    r=[("idx", e2), ("ysb", y2), ("ymoe", ex_)], w=[("ymoe", ex_ + 1), ("ymoeg", grp)], dma=f"ys{y2}")

        NEXP = 16
        IDX(0)
        G(0)
        W(0)
        W(1)
        for ex_ in range(NEXP):
            if ex_ + 1 < NEXP:
                IDX(ex_ + 1)
            XT(ex_)
            if ex_ + 1 < NEXP:
                G(ex_ + 1)
            FFN(ex_)
            if ex_ + 2 < NEXP:
                W(ex_ + 2)
        S.barrier()

    def combine(src, dst):
        S.reset_phase()
        xb = [S.alloc([128, 1024], F32) for _ in range(2)]
        yb = [S.alloc([128, 1024], F32) for _ in range(2)]
        ob = [S.alloc([128, 1024], F32) for _ in range(2)]
        for i in range(NT):
            i2 = i % 2
            DMA("sp", xb[i2][:], src[i * 128:(i + 1) * 128, :], w=[("xb", i2)], slot=f"cx{i2}")
            DMA("act", yb[i2][:], ymoe[i * 128:(i + 1) * 128, :], w=[("yb", i2)], slot=f"cy{i2}")
            TT("dve", yb[i2][:], yb[i2][:], MOD[:, 5, :], ALU.mult, r=[("yb", i2), "MOD"], w=[("yb", i2)])
            TT("pool", ob[i2][:], yb[i2][:], xb[i2][:], ALU.add, r=[("yb", i2), ("xb", i2)], w=[("ob", i2)])
            DMA("sp", dst[i * 128:(i + 1) * 128, :], ob[i2][:], r=[("ob", i2)], slot=f"co{i2}")
        S.barrier()

    def conv():
        S.reset_phase()
        hT = S.alloc([128, 8, T], BF16)
        xb = [S.alloc([128, 1024], F32) for _ in range(2)]
        tmpb = [S.alloc([128, 1024], F32) for _ in range(2)]
        hb = [S.alloc([128, 1024], BF16) for _ in range(2)]
        junk = S.alloc([128, 1024], BF16)
        stt = [S.alloc([128, 4], F32) for _ in range(2)]
        cw = S.alloc([128, 8, 3], F32)
        w3 = [S.alloc([128, 8, 3, 128], BF16) for _ in range(2)]
        z = S.alloc([128, T + 2], F32)
        tt_ = S.alloc([128, T], F32)
        bgs = S.alloc([128, T], BF16)
        cgs = [S.alloc([128, 512], F32) for _ in range(2)]
        vT = [S.alloc([128, T], BF16) for _ in range(2)]
        DMA("sp", cw[:], convw_d, w=["cw"], slot="cw")
        MSET("pool", z[:], 0.0, w=["z"])
        for i in range(NT):
            i2 = i % 2
            DMA("sp", xb[i2][:], x2d[i * 128:(i + 1) * 128, :], w=[("xb", i2)], slot=f"xb{i2}")
            modulate(xb[i2][:], ("xb", i2), MOD[:, 1, :], MOD[:, 0, :], hb[i2][:], ("hb", i2), stt[i2], ("st", i2),
                     tmpb[i2][:], ("tmp", i2), junk[:], "junk")
            pT = psb[i % 2]
            for k in range(8):
                TR(pT[:, k * 128:(k + 1) * 128], hb[i2][:, k * 128:(k + 1) * 128], identb[:], r=[("hb", i2), "identb"],
                   w=[("ps", i % 2)])
            CP("act", hT[:, :, i * 128:(i + 1) * 128], pT[:, 0:1024].rearrange("p (k n) -> p k n", k=8), r=[("ps", i % 2)],
               w=["hT"])
        wv = cwin_d.rearrange("(k p) n -> p k n", p=128)
        for fc in range(8):
            f2 = fc % 2
            for j in range(3):
                DMA("pool", w3[f2][:, :, j, :], wv[:, :, j * 1024 + fc * 128:j * 1024 + (fc + 1) * 128], w=[("w3", f2)],
                    slot=f"w3{f2}{j}")
            for b in range(8):
                for j in range(3):
                    bank = 2 + j * 2 + b % 2
                    for k in range(8):
                        MM(ps[bank][:, :], w3[f2][:, k, j, :], hT[:, k, b * 512:(b + 1) * 512], k == 0, k == 7,
                           r=[("w3", f2), "hT"], w=[("ps", bank)])
                CP("act", bgs[:, b * 512:(b + 1) * 512], ps[2 + b % 2][:, :], r=[("ps", 2 + b % 2)], w=["bgs"])
                CP("act", cgs[b % 2][:], ps[4 + b % 2][:, :], r=[("ps", 4 + b % 2)], w=[("cgs", b % 2)])
                TT("dve", z[:, 1 + b * 512:1 + (b + 1) * 512], cgs[b % 2][:], ps[6 + b % 2][:, :], ALU.mult,
                   r=[("cgs", b % 2), ("ps", 6 + b % 2)], w=["z"])
            ACT(tt_[:], z[:, 0:T], AF.Copy, r=["z", "cw"], w=["tt"], scale=cw[:, fc, 0:1])
            STT("dve", tt_[:], z[:, 1:T + 1], cw[:, fc, 1:2], tt_[:], ALU.mult, ALU.add, r=["z", "cw", "tt"], w=["tt"])
            STT("dve", tt_[:], z[:, 2:T + 2], cw[:, fc, 2:3], tt_[:], ALU.mult, ALU.add, r=["z", "cw", "tt"], w=["tt"])
            TT("pool", vT[f2][:], bgs[:], tt_[:], ALU.mult, r=["bgs", "tt"], w=[("vT", f2)])
            DMA("sp", oTd[fc], vT[f2][:], r=[("vT", f2)], slot=f"vT{f2}")
        S.barrier()

    phases = [
        lambda: ada(0, True),
        proj0,
        attn,
        fourier,
        lambda: outproj(0, wout_d, x_d, x1d),
        lambda: moe(0),
        lambda: combine(x1d, x2d),
        lambda: ada(1, False),
        conv,
        lambda: outproj(1, cwout_d, x2d, x3d),
        lambda: moe(1),
        lambda: combine(x3d, out_d),
    ]
    for i, ph in enumerate(phases):
        if stop is not None and i >= stop:
            break
        ph()
    S.emit()
    return nc, S


import ml_dtypes

_CACHE = {}


def _consts():
    if "c" in _CACHE:
        return _CACHE["c"]
    bf = ml_dtypes.bfloat16
    cpk = np.zeros((128, 1152), np.float32)
    cpk[:, 0:128] = np.eye(128)
    bo = np.zeros((128, 128), np.float32)
    bo[0:64, 0:64] = 1.0
    bo[64:128, 64:128] = 1.0
    cpk[:, 128:256] = bo
    R = np.zeros((64, 64), np.float32)
    for j in range(64):
        q = j // 16
        if q % 2 == 0:
            R[j, j + 16] = -1.0
        else:
            R[j, j - 16] = 1.0
    R2 = np.zeros((128, 128), np.float32)
    R2[0:64, 0:64] = R
    R2[64:128, 64:128] = R
    cpk[:, 256:384] = R2.T
    cpk[:, 384:512] = np.triu(np.ones((128, 128), np.float32), 1)
    cpk[:, 512:640] = 1.0
    cpk[:, 640:1152] = np.arange(512, dtype=np.float32)[None, :]
    tok = (np.arange(32)[None, :] * 128 + np.arange(128)[:, None])
    tokhl = np.zeros((128, 32, 16, 2), np.float32)
    tokhl[:, :, :, 0] = (tok // 64)[:, :, None]
    tokhl[:, :, :, 1] = (tok % 64)[:, :, None]
    sel = np.zeros((2, 256), np.float32)
    sel[0, 0:128] = 1.0
    sel[1, 128:256] = 1.0
    n = 4096
    r = np.repeat(np.arange(n // 64, dtype=np.float32), 64)
    col = np.tile(np.arange(64, dtype=np.float32), n // 64)
    inv = (np.float32(10000.0) ** (-np.arange(16, dtype=np.float32) / np.float32(16))).astype(np.float32)
    ar = r[:, None] * inv
    ac = col[:, None] * inv
    ang = np.concatenate([ar, ar, ac, ac], axis=-1).astype(np.float32)
    cosT = np.cos(ang).astype(np.float32).T
    sinT = np.sin(ang).astype(np.float32).T
    rope = np.stack([np.concatenate([cosT, cosT], 0), np.concatenate([sinT, sinT], 0)]).astype(np.float32)
    cm = np.arange(64)[:, None] * np.arange(64)[None, :]
    cb = np.cos(2 * np.pi * (cm % 64) / 64.0) / 512.0
    sb = np.sin(2 * np.pi * (cm % 64) / 64.0) / 512.0
    csd = np.zeros((256, 512), np.float32)
    for g in range(4):
        csd[g * 64:(g + 1) * 64, g * 64:(g + 1) * 64] = cb
        csd[g * 64:(g + 1) * 64, 256 + g * 64:256 + (g + 1) * 64] = sb
    kn = (np.arange(n, dtype=np.int64)[:, None] * np.arange(n, dtype=np.int64)[None, :]) % n
    angp = kn.astype(np.float64) * (2 * np.pi / n)
    cosd = np.cos(angp).astype(bf)
    sind = (-np.sin(angp)).astype(bf)
    c = dict(cpk=cpk, tokhl=tokhl.reshape(128, 1024), sel=sel, rope=rope, csd=csd, cosd=cosd, sind=sind)
    _CACHE["c"] = c
    return c


def kernel(x, c, ctx, c_ctx, ada_w, ada_b, norm_mix, norm_ffn, attn_w_in, attn_q_norm, attn_k_norm,
           lam_q1, lam_k1, lam_q2, lam_k2, attn_subln, attn_w_out, conv_w_in, conv_w, conv_w_out,
           router_w, moe_w_gate, moe_w_up, moe_w_down, _stop=None, _debug=False, _ncores=4):
    f = lambda a: np.ascontiguousarray(np.asarray(a, dtype=np.float32))
    x, c, ctx, c_ctx = f(x), f(c), f(ctx), f(c_ctx)
    K = _consts()
    ck = ("nc", _stop, _debug)
    if ck not in _CACHE:
        _CACHE[ck] = build_nc(_stop, _debug)[0]
    nc = _CACHE[ck]
    shared = dict(K)
    shared["ada_w"] = f(ada_w)
    shared["adab2"] = f(np.repeat(f(ada_b)[:, None, :], 2, axis=1))
    shared["nmrep"] = f(np.repeat(f(norm_mix)[:, None, :], 128, axis=1))
    shared["nfrep"] = f(np.repeat(f(norm_ffn)[:, None, :], 128, axis=1))
    shared["attn_w_in"] = f(attn_w_in)[0]
    shared["attn_w_out"] = f(attn_w_out)[0]
    shared["conv_w_in"] = f(conv_w_in)[0]
    shared["conv_w_out"] = f(conv_w_out)[0]
    shared["qkcol"] = f(np.stack([np.tile(f(attn_q_norm)[0], 2), np.tile(f(attn_k_norm)[0], 2)], axis=1))
    lamrow = np.concatenate([f(lam_q1)[0], f(lam_k1)[0], f(lam_q2)[0], f(lam_k2)[0]])
    shared["lamrep"] = f(np.repeat(lamrow[None, :], 128, axis=0))
    shared["sublnrep"] = f(np.repeat(f(attn_subln)[0][None, :], 128, axis=0))
    shared["convw"] = f(f(conv_w)[0].T.reshape(8, 128, 3).transpose(1, 0, 2))
    shared["router_w"] = f(router_w)
    shared["moe_w_gate"] = f(moe_w_gate)
    shared["moe_w_up"] = f(moe_w_up)
    shared["moe_w_down"] = f(moe_w_down)
    in_maps = []
    for b in range(_ncores):
        m = dict(shared)
        m["x"] = x[b]
        m["ctx"] = ctx[b]
        c2 = np.stack([c[b].reshape(8, 128).T, c_ctx.reshape(8, 128).T], axis=-1)
        m["c2"] = f(c2)
        in_maps.append(m)
    res = run_bass_kernel_spmd(nc, in_maps, core_ids=list(range(_ncores)))
    _CACHE["res"] = res
    if _debug:
        return res
    return np.stack([np.asarray(res.results[b]["out"], dtype=np.float32) for b in range(4)], axis=0)
```

```python
from concourse.bass_utils import run_bass_kernel_spmd
import numpy as np
import concourse.bass as bass
import concourse.mybir as mybir

F32 = mybir.dt.float32
BF16 = mybir.dt.bfloat16
I32 = mybir.dt.int32
ALU = mybir.AluOpType
AF = mybir.ActivationFunctionType
AX = mybir.AxisListType
ENG = ("pe", "act", "dve", "pool", "sp")


class Op:
    __slots__ = ("eng", "fn", "deps", "sig", "sem", "val", "dma", "waits", "semkey", "inc")


class Sched:
    def __init__(self, nc):
        self.nc = nc
        self.streams = {e: [] for e in ENG}
        self.lw = {}
        self.lr = {}
        self.pending_dma = []
        self.last_real = {}
        self.slot_sem = {}
        self.slot_cnt = {}
        self.nsem = 0
        self.sb_off = 0
        self.sb_base = 0
        self.uid = 0

    def alloc(self, shape, dtype, name="t"):
        if not hasattr(self, "views"):
            big = self.nc.alloc_sbuf_tensor("arena", [128, 103 * 1024], BF16)
            self.views = {BF16: big, F32: big.bitcast(F32), I32: big.bitcast(I32)}
        esz = mybir.dt.size(dtype)
        n = int(np.prod(shape[1:]))
        off = (self.sb_off + 63) // 64 * 64
        self.sb_off = off + n * esz
        assert self.sb_off <= 206 * 1024, (name, self.sb_off)
        ap = self.views[dtype][0:shape[0], off // esz: off // esz + n]
        if len(shape) == 3:
            ap = ap.rearrange("p (a b) -> p a b", a=shape[1])
        elif len(shape) == 4:
            ap = ap.rearrange("p (a b c) -> p a b c", a=shape[1], b=shape[2])
        return ap

    def mark_persistent(self):
        self.sb_base = self.sb_off

    def reset_phase(self):
        self.sb_off = self.sb_base

    def newsem(self, name):
        self.nsem += 1
        return self.nc.alloc_semaphore(f"{name}_{self.nsem}")

    def op(self, eng, fn, r=(), w=(), dma=None, inc=16):
        o = Op()
        o.eng, o.fn, o.sig, o.dma = eng, fn, False, dma
        o.inc = inc if dma else 1
        o.sem = None
        o.val = 0
        deps = {}
        for k in r:
            for d in self.lw.get(k, ()):
                deps[id(d)] = d
        for k in w:
            for d in self.lw.get(k, ()):
                if d.dma or dma or d.eng != eng:
                    deps[id(d)] = d
            for d in self.lr.get(k, ()):
                if d.dma or dma or d.eng != eng:
                    deps[id(d)] = d
        o.deps = list(deps.values())
        for d in o.deps:
            d.sig = True
        for k in w:
            self.lw[k] = [o]
            self.lr[k] = []
        for k in r:
            lst = self.lr.setdefault(k, [])
            if not dma:
                lst[:] = [x for x in lst if x.dma or x.eng != eng]
            lst.append(o)
        self.streams[eng].append(o)
        if dma:
            o.sig = True
            self.pending_dma.append(o)
        elif fn is not None:
            self.last_real[eng] = o
        return o

    def barrier(self):
        lasts = list(self.last_real.values()) + list(self.pending_dma)
        for d in lasts:
            d.sig = True
        for e in ENG:
            o = Op()
            o.eng, o.fn, o.sig, o.dma, o.sem, o.val, o.inc = e, None, False, None, None, 0, 1
            o.deps = [d for d in lasts if d.dma or d.eng != e]
            self.streams[e].append(o)
        self.lw, self.lr, self.pending_dma = {}, {}, []

    def finalize(self):
        for e in ENG:
            cnt = 0
            cur = None
            for o in self.streams[e]:
                if o.dma:
                    key = (e, o.dma)
                    if key not in self.slot_sem:
                        self.slot_sem[key] = self.newsem("d")
                        self.slot_cnt[key] = 0
                    self.slot_cnt[key] += o.inc
                    o.sem, o.val, o.semkey = self.slot_sem[key], self.slot_cnt[key], ("d",) + key
                elif o.sig:
                    if cur is None or cnt >= 30000:
                        cur = self.newsem("e" + e)
                        curkey = ("e", e, self.nsem)
                        cnt = 0
                    cnt += 1
                    o.sem, o.val, o.semkey = cur, cnt, curkey
        nw = 0
        for e in ENG:
            known = {}
            for o in self.streams[e]:
                need = {}
                for d in o.deps:
                    assert d.sem is not None
                    if known.get(d.semkey, 0) < d.val:
                        if need.get(d.semkey, (None, 0))[1] < d.val:
                            need[d.semkey] = (d.sem, d.val)
                for k, (sm, v) in need.items():
                    known[k] = v
                o.waits = list(need.values())
                nw += len(o.waits)
        self.nwaits = nw

    def replay(self, eng, e):
        for o in self.streams[eng]:
            for sm, v in o.waits:
                e.wait_ge(sm, v)
            if o.fn is not None:
                ins = o.fn(e)
                if o.sig:
                    ins.then_inc(o.sem, o.inc)

    def emit(self):
        self.barrier()
        self.finalize()
        with self.nc.Block() as blk:
            @blk.tensor
            def _(e):
                self.replay("pe", e)

            @blk.scalar
            def _(e):
                self.replay("act", e)

            @blk.vector
            def _(e):
                self.replay("dve", e)

            @blk.gpsimd
            def _(e):
                self.replay("pool", e)

            @blk.sync
            def _(e):
                self.replay("sp", e)


EPS = 1e-6
T = 4096
NT = 32
NKEY = 4352
NKT = 34
DEBUG = False


def build_nc(stop=None, debug=False, ncores=8):
    nc = bass.Bass("TRN2", target_bir_lowering=False)
    S = Sched(nc)

    def din(name, shape, dt=F32):
        return nc.dram_tensor(name, list(shape), dt, kind="ExternalInput").ap()

    def dscr(name, shape, dt, out=False):
        return nc.dram_tensor(name, list(shape), dt, kind="ExternalOutput" if debug else "Internal").ap()

    x_d = din("x", [T, 1024])
    xloc_d = din("xloc", [T, 1024])
    ctx_d = din("ctx", [256, 1024])
    c2_d = din("c2", [128, 8, 2])
    adaw_d = din("ada_w", [2, 1024, 6144])
    adab_d = din("adab2", [2, 2, 6144])
    nm_d = din("nmrep", [2, 128, 1024])
    nf_d = din("nfrep", [2, 128, 1024])
    win_d = din("attn_w_in", [1024, 2560])
    wout_d = din("attn_w_out", [1024, 1024])
    cwin_d = din("conv_w_in", [1024, 3072])
    cwout_d = din("conv_w_out", [1024, 1024])
    qk_d = din("qkcol", [128, 2])
    lam_d = din("lamrep", [128, 256])
    sub_d = din("sublnrep", [128, 128])
    convw_d = din("convw", [128, 8, 3])
    rw_d = din("router_w", [2, 1024, 16])
    wg_d = din("moe_w_gate", [2, 8, 1024, 1024])
    wu_d = din("moe_w_up", [2, 8, 1024, 1024])
    wd_d = din("moe_w_down", [2, 8, 1024, 1024])
    cpk_d = din("cpk", [128, 1152])
    tok_d = din("tokhl", [128, 1024])
    sel_d = din("sel", [2, 256])
    rope_d = din("rope", [2, 128, T])
    cs_d = din("csd", [256, 512])
    cos_d = din("cosd", [T, 2048], BF16)
    sin_d = din("sind", [T, 2048], BF16)
    out_d = nc.dram_tensor("out", [T, 1024], F32, kind="ExternalOutput").ap()

    kTd = dscr("kTd", [6, 128, NKEY], BF16)
    qTd = dscr("qTd", [6, 128, 2048], BF16)
    V1d = dscr("V1d", [NKEY, 774], BF16)
    fcsd = dscr("fcsd", [T, 512], BF16)
    oTd = dscr("oTd", [8, 128, T], BF16)
    x1d = dscr("x1d", [T, 1024], F32, DEBUG)
    x2d = dscr("x2d", [T, 1024], F32, DEBUG)
    x3d = dscr("x3d", [T, 1024], F32, DEBUG)
    hfd = dscr("hfd", [T, 1024], BF16)
    oTh = nc.dram_tensor("oTh", [1024, 2048], BF16).ap()
    oTg = nc.dram_tensor("oTg", [2048, 2048], BF16).ap()
    ymoe = nc.dram_tensor("ymoe", [T, 1024], F32).ap()
    ysum = nc.dram_tensor("ysum", [T, 1024], F32).ap()
    RG = [[2 * i, 2 * i + 1] for i in range(ncores // 2)]
    oThv = oTh.rearrange("(c p) n -> c p n", p=128)
    oTgv = oTg.rearrange("(c r p) n -> r c p n", r=2, c=8)

    def CC(kind, alu, src, dst, r, w, name):
        S.op("pool", lambda e: e.collective_compute(kind, alu, replica_groups=RG, ins=[src.opt()], outs=[dst.opt()]),
             r=r, w=w, dma=name, inc=1)

    pbig = [nc.alloc_psum_tensor(f"pq{j}", [128, 1024], F32) for j in range(4)]
    pbigb = [p.bitcast(BF16) for p in pbig]
    ps = [pbig[i // 2][:, (i % 2) * 512:(i % 2 + 1) * 512] for i in range(8)]
    psb = [pbigb[i // 2][:, (i % 2) * 1024:(i % 2 + 1) * 1024] for i in range(8)]

    def DMA(eng, out, in_, r=(), w=(), slot=None):
        S.op(eng, lambda e: e.dma_start(out=out, in_=in_), r=r, w=w, dma=slot)

    def MM(out, lhsT, rhs, start, stop, r=(), w=()):
        S.op("pe", lambda e: e.matmul(out, lhsT, rhs, start=start, stop=stop), r=r, w=w)

    def TR(out, in_, ident, r=(), w=()):
        S.op("pe", lambda e: e.transpose(out, in_, ident), r=r, w=w)

    def ACT(out, in_, func, r=(), w=(), **kw):
        S.op("act", lambda e: e.activation(out=out, in_=in_, func=func, **kw), r=r, w=w)

    def TT(eng, out, in0, in1, op, r=(), w=()):
        S.op(eng, lambda e: e.tensor_tensor(out=out, in0=in0, in1=in1, op=op), r=r, w=w)

    def TS(eng, out, in0, s1, op0, s2=None, op1=None, r=(), w=(), accum=None):
        if op1 is None:
            S.op(eng, lambda e: e.tensor_single_scalar(out=out, in_=in0, scalar=s1, op=op0), r=r, w=w)
        else:
            S.op(eng, lambda e: e.tensor_scalar(out=out, in0=in0, scalar1=s1, scalar2=s2, op0=op0, op1=op1,
                                                accum_out=accum), r=r, w=w)

    def STT(eng, out, in0, scalar, in1, op0, op1, r=(), w=()):
        S.op(eng, lambda e: e.scalar_tensor_tensor(out=out, in0=in0, scalar=scalar, in1=in1, op0=op0, op1=op1),
             r=r, w=w)

    def CP(eng, out, in_, r=(), w=()):
        if eng == "act":
            ACT(out, in_, AF.Copy, r=r, w=w)
        else:
            S.op(eng, lambda e: e.tensor_copy(out=out, in_=in_), r=r, w=w)

    def RECIP(out, in_, r=(), w=()):
        S.op("dve", lambda e: e.reciprocal(out=out, in_=in_), r=r, w=w)

    def MSET(eng, ap, v, w=()):
        S.op(eng, lambda e: e.memset(ap, v), w=w)

    identf = S.alloc([128, 128], F32)
    identb = S.alloc([128, 128], BF16)
    bones = S.alloc([128, 128], BF16)
    RT = S.alloc([128, 128], BF16)
    Ust = S.alloc([128, 128], BF16)
    onesb = S.alloc([128, 128], BF16)
    iota = S.alloc([128, 512], F32)
    tokhl = S.alloc([128, 32, 16, 2], BF16)
    sel = S.alloc([2, 256], F32)
    MOD = S.alloc([128, 6, 1024], F32)
    CMOD = S.alloc([128, 2, 1024], F32)
    AFF = S.alloc([128, 32, 16], F32)
    small = S.alloc([128, 64], F32)
    DMA("sp", identf[:], cpk_d[:, 0:128], w=["identf"], slot="c0")
    DMA("sp", iota[:], cpk_d[:, 640:1152], w=["iota"], slot="c1")
    DMA("sp", sel[:], sel_d, w=["sel"], slot="c2")
    DMA("pool", identb[:], cpk_d[:, 0:128], w=["identb"], slot="c3")
    DMA("pool", bones[:], cpk_d[:, 128:256], w=["bones"], slot="c4")
    DMA("pool", RT[:], cpk_d[:, 256:384], w=["RT"], slot="c5")
    DMA("pool", Ust[:], cpk_d[:, 384:512], w=["Ust"], slot="c6")
    DMA("pool", onesb[:], cpk_d[:, 512:640], w=["onesb"], slot="c7")
    DMA("pool", tokhl.rearrange("p a b c -> p (a b c)"), tok_d, w=["tokhl"], slot="c8")
    S.mark_persistent()
    S.barrier()

    def ada(l, with_ctx):
        S.reset_phase()
        sc = S.alloc([128, 8, 2], F32)
        m2 = S.alloc([2, 6144], F32)
        ab = S.alloc([2, 6144], F32)
        wb = [S.alloc([128, 8, 512], F32) for _ in range(2)]
        nmt = S.alloc([128, 1024], F32)
        nft = S.alloc([128, 1024], F32)
        DMA("sp", sc[:], c2_d, w=["sc"], slot="a0")
        DMA("sp", ab[:], adab_d[l], w=["ab"], slot="a1")
        DMA("sp", nmt[:], nm_d[l], w=["nmt"], slot="a2")
        DMA("sp", nft[:], nf_d[l], w=["nft"], slot="a3")
        ACT(sc[:], sc[:], AF.Silu, r=["sc"], w=["sc"])
        wv = adaw_d[l].rearrange("(k p) n -> p k n", p=128)
        for cg in range(12):
            wt = wb[cg % 2]
            DMA("sp", wt[:], wv[:, :, cg * 512:(cg + 1) * 512], w=[("wb", cg % 2)], slot=f"aw{cg % 2}")
            pp = ps[cg % 2]
            for k in range(8):
                MM(pp[0:2, :], sc[:, k, :], wt[:, k, :], k == 0, k == 7, r=["sc", ("wb", cg % 2)], w=[("ps", cg % 2)])
            TT("dve", m2[:, cg * 512:(cg + 1) * 512], pp[0:2, :], ab[:, cg * 512:(cg + 1) * 512], ALU.add,
               r=[("ps", cg % 2), "ab"], w=["m2"])
        for j in range(12):
            pp = ps[2 + j % 2]
            MM(pp[:, :], sel[0:2, 0:128], m2[0:2, j * 512:(j + 1) * 512], True, True, r=["sel", "m2"], w=[("ps", 2 + j % 2)])
            CP("act" if j % 2 else "dve", MOD[:, j // 2, (j % 2) * 512:(j % 2 + 1) * 512], pp[:, :],
               r=[("ps", 2 + j % 2)], w=["MOD"])
            if with_ctx and j < 4:
                pq = ps[4 + j % 2]
                MM(pq[:, :], sel[0:2, 128:256], m2[0:2, j * 512:(j + 1) * 512], True, True, r=["sel", "m2"],
                   w=[("ps", 4 + j % 2)])
                CP("act" if j % 2 else "dve", CMOD[:, j // 2, (j % 2) * 512:(j % 2 + 1) * 512], pq[:, :],
                   r=[("ps", 4 + j % 2)], w=["CMOD"])
        STT("dve", MOD[:, 1, :], MOD[:, 1, :], 1.0, nmt[:], ALU.add, ALU.mult, r=["MOD", "nmt"], w=["MOD"])
        TS("dve", MOD[:, 1, :], MOD[:, 1, :], 32.0, ALU.mult, r=["MOD"], w=["MOD"])
        STT("dve", MOD[:, 4, :], MOD[:, 4, :], 1.0, nft[:], ALU.add, ALU.mult, r=["MOD", "nft"], w=["MOD"])
        TS("dve", MOD[:, 4, :], MOD[:, 4, :], 32.0, ALU.mult, r=["MOD"], w=["MOD"])
        if with_ctx:
            STT("dve", CMOD[:, 1, :], CMOD[:, 1, :], 1.0, nmt[:], ALU.add, ALU.mult, r=["CMOD", "nmt"], w=["CMOD"])
            TS("dve", CMOD[:, 1, :], CMOD[:, 1, :], 32.0, ALU.mult, r=["CMOD"], w=["CMOD"])
        S.barrier()

    def modulate(xt, xkey, G32, SH, out, okey, st, skey, tmp, tkey, junk, jkey):
        ACT(junk, xt, AF.Square, r=[xkey], w=[jkey, skey], accum_out=st[:, 0:1])
        ACT(st[:, 1:2], st[:, 0:1], AF.Sqrt, r=[skey], w=[skey], bias=1024.0 * EPS, scale=1.0)
        RECIP(st[:, 2:3], st[:, 1:2], r=[skey], w=[skey])
        STT("dve", tmp, xt, st[:, 2:3], G32, ALU.mult, ALU.mult, r=[xkey, skey, "MOD", "CMOD"], w=[tkey])
        TT("pool", out, tmp, SH, ALU.add, r=[tkey, "MOD", "CMOD"], w=[okey])

    def proj0():
        S.reset_phase()
        w = S.alloc([128, 8, 2560], BF16)
        cosT = S.alloc([128, T], F32)
        sinT = S.alloc([128, T], F32)
        qk = S.alloc([128, 2], F32)
        CS = S.alloc([128, 2, 512], BF16)
        xb = [S.alloc([128, 1024], F32) for _ in range(2)]
        tmpb = [S.alloc([128, 1024], F32) for _ in range(2)]
        hb = [S.alloc([128, 1024], BF16) for _ in range(2)]
        junk = S.alloc([128, 1024], BF16)
        stt = [S.alloc([128, 4], F32) for _ in range(2)]
        hTb = [S.alloc([128, 8, 512], BF16) for _ in range(2)]
        sqb = S.alloc([128, 512], BF16)
        rs = S.alloc([128, 512], F32)
        knb = [S.alloc([128, 512], BF16) for _ in range(2)]
        t1 = S.alloc([128, 512], F32)
        t2 = S.alloc([128, 512], F32)
        kout = [S.alloc([128, 512], BF16) for _ in range(2)]
        v1t = [S.alloc([128, 6, 129], BF16) for _ in range(2)]
        fT = S.alloc([128, 2, 512], BF16)
        fcst = [S.alloc([128, 512], BF16) for _ in range(2)]
        wv = win_d.rearrange("(k p) n -> p k n", p=128)
        for q in range(4):
            DMA("pool", w[:, :, q * 640:(q + 1) * 640], wv[:, :, q * 640:(q + 1) * 640], w=["w"], slot=f"pw{q}")
        DMA("sp", cosT[:], rope_d[0], w=["cos"], slot="p0")
        DMA("sp", sinT[:], rope_d[1], w=["sin"], slot="p1")
        DMA("sp", qk[:], qk_d, w=["qk"], slot="p2")
        DMA("pool", CS[:], cs_d.rearrange("(c p) n -> p c n", p=128), w=["CS"], slot="p3")
        TS("dve", qk[:, 1:2], qk[:, 1:2], 8.0, ALU.mult, r=["qk"], w=["qk"])
        for i in range(2):
            MSET("pool", v1t[i][:], 1.0, w=[("v1t", i)])
        nrc = [0]

        def normrope(pp, pkey, nt, gain, rope, pos0, dst):
            c = nrc[0]
            nrc[0] += 1
            kn = knb[c % 2]
            ko = kout[c % 2]
            ACT(sqb[:, :nt], pp[:, :nt], AF.Square, r=[pkey], w=["sqb"])
            MM(ps[4][:, :nt], bones[:], sqb[:, :nt], True, True, r=["bones", "sqb"], w=[("ps", 4)])
            ACT(rs[:, :nt], ps[4][:, :nt], AF.Sqrt, r=[("ps", 4)], w=["rs"], bias=64.0 * EPS, scale=1.0)
            RECIP(rs[:, :nt], rs[:, :nt], r=["rs"], w=["rs"])
            STT("dve", kn[:, :nt], pp[:, :nt], gain, rs[:, :nt], ALU.mult, ALU.mult, r=[pkey, "qk", "rs"],
                w=[("kn", c % 2)])
            if rope:
                MM(ps[5][:, :nt], RT[:], kn[:, :nt], True, True, r=["RT", ("kn", c % 2)], w=[("ps", 5)])
                TT("pool", t1[:, :nt], kn[:, :nt], cosT[:, pos0:pos0 + nt], ALU.mult, r=[("kn", c % 2), "cos"], w=["t1"])
                TT("dve", t2[:, :nt], ps[5][:, :nt], sinT[:, pos0:pos0 + nt], ALU.mult, r=[("ps", 5), "sin"], w=["t2"])
                TT("pool", ko[:, :nt], t1[:, :nt], t2[:, :nt], ALU.add, r=["t1", "t2"], w=[("ko", c % 2)])
                DMA("sp", dst, ko[:, :nt], r=[("ko", c % 2)], slot=f"ko{c % 2}")
            else:
                DMA("sp", dst, kn[:, :nt], r=[("kn", c % 2)], slot=f"kn{c % 2}")

        pc = [0]
        for b in range(9):
            isctx = b == 8
            ntile = 2 if isctx else 4
            nt = ntile * 128
            hT = hTb[b % 2]
            hk = ("hT", b % 2)
            for t in range(ntile):
                i2 = t % 2
                src = ctx_d[t * 128:(t + 1) * 128, :] if isctx else xloc_d[b * 512 + t * 128: b * 512 + (t + 1) * 128, :]
                DMA("sp", xb[i2][:], src, w=[("xb", i2)], slot=f"xb{i2}")
                G = CMOD[:, 1, :] if isctx else MOD[:, 1, :]
                SHh = CMOD[:, 0, :] if isctx else MOD[:, 0, :]
                modulate(xb[i2][:], ("xb", i2), G, SHh, hb[i2][:], ("hb", i2), stt[i2], ("st", i2),
                         tmpb[i2][:], ("tmp", i2), junk[:], "junk")
                pT = psb[t % 2]
                for k in range(8):
                    TR(pT[:, k * 128:(k + 1) * 128], hb[i2][:, k * 128:(k + 1) * 128], identb[:],
                       r=[("hb", i2), "identb"], w=[("ps", t % 2)])
                CP("act", hT[:, :, t * 128:(t + 1) * 128], pT[:, 0:1024].rearrange("p (k n) -> p k n", k=8),
                   r=[("ps", t % 2)], w=[hk])
            for h in range(6):
                for which in ((0, 1) if b < 4 else (1,)):
                    bank = 2 + pc[0] % 2
                    pc[0] += 1
                    col0 = (768 if which == 1 else 0) + h * 128
                    for k in range(8):
                        MM(ps[bank][:, :nt], w[:, k, col0:col0 + 128], hT[:, k, :nt], k == 0, k == 7,
                           r=["w", hk], w=[("ps", bank)])
                    dst = (kTd if which == 1 else qTd)[h, :, b * 512:b * 512 + nt]
                    normrope(ps[bank], ("ps", bank), nt, qk[:, which:which + 1], not isctx, b * 512 if not isctx else 0, dst)
            for t in range(ntile):
                vt = v1t[t % 2]
                for half in range(2):
                    bank = 6 + half
                    for k in range(8):
                        MM(ps[bank][:, 0:384], hT[:, k, t * 128:(t + 1) * 128],
                           w[:, k, 1536 + half * 384:1536 + (half + 1) * 384], k == 0, k == 7, r=["w", hk],
                           w=[("ps", bank)])
                    CP("act" if half else "dve", vt[:, half * 3:(half + 1) * 3, 0:128],
                       ps[bank][:, 0:384].rearrange("p (a b) -> p a b", a=3), r=[("ps", bank)], w=[("v1t", t % 2)])
                row0 = b * 512 + t * 128
                DMA("sp", V1d[row0:row0 + 128, :], vt.rearrange("p a b -> p (a b)"), r=[("v1t", t % 2)], slot=f"v1t{t % 2}")
            if not isctx:
                for c in range(2):
                    bank = 2 + pc[0] % 2
                    pc[0] += 1
                    for k in range(8):
                        MM(ps[bank][:, :], w[:, k, 2304 + c * 128:2304 + (c + 1) * 128], hT[:, k, :], k == 0, k == 7,
                           r=["w", hk], w=[("ps", bank)])
                    CP("act", fT[:, c, :], ps[bank][:, :], r=[("ps", bank)], w=["fT"])
                for t in range(4):
                    bank = 6 + t % 2
                    for c in range(2):
                        MM(ps[bank][:, :], fT[:, c, t * 128:(t + 1) * 128], CS[:, c, :], c == 0, c == 1, r=["fT", "CS"],
                           w=[("ps", bank)])
                    CP("dve", fcst[t % 2][:], ps[bank][:, :], r=[("ps", bank)], w=[("fcst", t % 2)])
                    row0 = b * 512 + t * 128
                    DMA("sp", fcsd[row0:row0 + 128, :], fcst[t % 2][:], r=[("fcst", t % 2)], slot=f"fc{t % 2}")
        S.barrier()

    def attn():
        S.reset_phase()
        kT = S.alloc([128, 6, NKEY], BF16)
        V1 = S.alloc([128, NKT, 774], BF16)
        lamt = S.alloc([128, 256], F32)
        SUB = S.alloc([128, 128], F32)
        qb = [S.alloc([128, 2, 6, 512], BF16) for _ in range(2)]
        ob = [S.alloc([128, 6, 512], BF16) for _ in range(2)]
        for i in range(2):
            MSET("pool", qb[i].rearrange("p a b c -> p (a b c)"), 0.0, w=[("qT", i)])
        a1 = S.alloc([128, 128], F32)
        att = S.alloc([128, 128], F32)
        attb4 = S.alloc([128, 4, 128], BF16)
        junk = S.alloc([128, 128], BF16)
        sm = S.alloc([128, 16], F32)
        accS = S.alloc([128, 3, 387], F32)
        for h in range(6):
            DMA("sp", kT[:, h, :], kTd[h], w=["kT"], slot=f"kT{h % 2}")
        v1v = V1d.rearrange("(t p) c -> p t c", p=128)
        for q in range(2):
            DMA("sp", V1[:, q * 17:(q + 1) * 17, :], v1v[:, q * 17:(q + 1) * 17, :], w=["V1"], slot=f"V1{q}")
        DMA("sp", lamt[:], lam_d, w=["lamt"], slot="lam")
        DMA("sp", SUB[:], sub_d, w=["SUB"], slot="sub")
        TS("dve", SUB[:], SUB[:], 0.8, ALU.mult, r=["SUB"], w=["SUB"])
        TT("dve", lamt[:, 0:64], lamt[:, 0:64], lamt[:, 64:128], ALU.mult, r=["lamt"], w=["lamt"])
        TT("dve", lamt[:, 128:192], lamt[:, 128:192], lamt[:, 192:256], ALU.mult, r=["lamt"], w=["lamt"])
        S.op("dve", lambda e: e.reduce_sum(out=sm[:, 0:1], in_=lamt[:, 0:64], axis=AX.X), r=["lamt"], w=["sm"])
        S.op("dve", lambda e: e.reduce_sum(out=sm[:, 1:2], in_=lamt[:, 128:192], axis=AX.X), r=["lamt"], w=["sm"])
        ACT(sm[:, 0:2], sm[:, 0:2], AF.Exp, r=["sm"], w=["sm"])
        TT("dve", sm[:, 2:3], sm[:, 1:2], sm[:, 0:1], ALU.subtract, r=["sm"], w=["sm"])
        TS("dve", sm[:, 2:3], sm[:, 2:3], -0.2, ALU.add, r=["sm"], w=["sm"])
        nlam = sm[:, 2:3]
        accs = {}
        for m in range(2):
            for qs in range(4):
                j = m * 4 + qs
                accs[(m, qs)] = ps[4 + j // 3][:, (j % 3) * 129:(j % 3) * 129 + 129]
        PT2 = [S.alloc([128, 1024], BF16) for _ in range(2)]
        steps = [(h, m, kt) for h in range(6) for m in range(2) for kt in range(NKT)]
        npair = len(steps) // 2
        for g in range(4):
            qT = qb[g % 2]
            for m_ in range(2):
                DMA("sp", qT[m_ * 64:(m_ + 1) * 64, m_, :, :],
                    qTd[:, m_ * 64:(m_ + 1) * 64, g * 512:(g + 1) * 512].rearrange("h p n -> p h n"), r=[("qT", g % 2)],
                    w=[("qT", g % 2)], slot=f"qT{g % 2}")
            oT = ob[g % 2]

            def QKEXP(p):
                pb = p % 2
                for u in range(2):
                    h, m, kt = steps[2 * p + u]
                    MM(pbig[pb][:, u * 512:(u + 1) * 512], kT[:, h, kt * 128:(kt + 1) * 128],
                       qT[:, m, h, :], True, True, r=["kT", ("qT", g % 2)], w=[("pq", pb)])
                ACT(PT2[pb][:], pbig[pb][:, :], AF.Exp, r=[("pq", pb)], w=[("PT", pb)])

            def AV(p):
                pb = p % 2
                for u in range(2):
                    h, m, kt = steps[2 * p + u]
                    for qs in range(4):
                        st_ = (kt == 0 and (m * 4 + qs) % 3 == 0)
                        S.op("pe", lambda e, o_=accs[(m, qs)], l_=PT2[pb][:, u * 512 + qs * 128:u * 512 + (qs + 1) * 128],
                             r_=V1[:, kt, h * 129:(h + 1) * 129], st_=st_, sp_=(kt == NKT - 1):
                             e.matmul(o_, l_, r_, start=st_, stop=sp_, skip_group_check=True),
                             r=[("PT", pb), "V1"], w=[("ps", 4 + (m * 4 + qs) // 3)])

            def EPI(h):
                for bk in range(3):
                    wdt = 387 if bk < 2 else 258
                    CP("act", accS[:, bk, 0:wdt], ps[4 + bk][:, 0:wdt], r=[("ps", 4 + bk)], w=["accS"])
                for qs in range(4):
                    j0, j1 = qs, 4 + qs
                    A0 = accS[:, j0 // 3, (j0 % 3) * 129:(j0 % 3) * 129 + 129]
                    A1 = accS[:, j1 // 3, (j1 % 3) * 129:(j1 % 3) * 129 + 129]
                    RECIP(sm[:, 4:5], A0[:, 128:129], r=["accS"], w=["sm4"])
                    RECIP(sm[:, 5:6], A1[:, 128:129], r=["accS"], w=["sm5"])
                    TT("dve", sm[:, 6:7], sm[:, 5:6], nlam, ALU.mult, r=["sm5", "sm"], w=["sm6"])
                    TS("dve", a1[:], A0[:, 0:128], sm[:, 4:5], ALU.mult, r=["accS", "sm4"], w=["a1"])
                    STT("dve", att[:], A1[:, 0:128], sm[:, 6:7], a1[:], ALU.mult, ALU.add, r=["accS", "sm6", "a1"],
                        w=["att"])
                    ACT(junk[:], att[:], AF.Square, r=["att"], w=["junkA", "sm7"], accum_out=sm[:, 7:8])
                    ACT(sm[:, 8:9], sm[:, 7:8], AF.Sqrt, r=["sm7"], w=["sm8"], bias=EPS, scale=1.0 / 128.0)
                    RECIP(sm[:, 9:10], sm[:, 8:9], r=["sm8"], w=["sm9"])
                    STT("dve", attb4[:, qs, :], att[:], sm[:, 9:10], SUB[:], ALU.mult, ALU.mult, r=["att", "sm9", "SUB"],
                        w=["attb"])

            def EPI_T(h):
                for qs in range(4):
                    TR(psb[7][:, qs * 128:(qs + 1) * 128], attb4[:, qs, :], identb[:], r=["attb", "identb"], w=[("ps", 7)])
                CP("act", oT[:, h, :], psb[7][:, 0:512], r=[("ps", 7)], w=[("oT", g % 2)])

            QKEXP(0)
            pend = None
            for p in range(npair):
                if p + 1 < npair:
                    QKEXP(p + 1)
                AV(p)
                if pend is not None and p == pend[1]:
                    EPI_T(pend[0])
                    pend = None
                hh, mm, kk = steps[2 * p + 1]
                if mm == 1 and kk == NKT - 1:
                    EPI(hh)
                    pend = (hh, p + 6)
                    if p == npair - 1:
                        EPI_T(hh)
                        pend = None
            DMA("sp", oThv[0:6, :, g * 512:(g + 1) * 512].rearrange("c p n -> p c n"), oT[:], r=[("oT", g % 2)], slot=f"oT{g % 2}")
        S.barrier()

    def fourier():
        S.reset_phase()
        fcs = S.alloc([128, 32, 512], BF16)
        cb = [S.alloc([128, 16, 512], BF16) for _ in range(2)]
        sb = [S.alloc([128, 16, 512], BF16) for _ in range(2)]
        oF = [S.alloc([128, 2, 512], BF16) for _ in range(2)]
        DMA("sp", fcs[:], fcsd.rearrange("(t p) c -> p t c", p=128), w=["fcs"], slot="fcs")
        cv = cos_d.rearrange("(t p) k -> p t k", p=128)
        sv = sin_d.rearrange("(t p) k -> p t k", p=128)
        n = 0
        for g in range(4):
            for half in range(2):
                cbb, sbb = cb[n % 2], sb[n % 2]
                DMA("sp", cbb[:], cv[:, half * 16:(half + 1) * 16, g * 512:(g + 1) * 512], w=[("cb", n % 2)], slot=f"cb{n % 2}")
                DMA("act", sbb[:], sv[:, half * 16:(half + 1) * 16, g * 512:(g + 1) * 512], w=[("sb", n % 2)], slot=f"sb{n % 2}")
                for c in range(2):
                    for t in range(16):
                        tt = half * 16 + t
                        MM(ps[c][:, :], fcs[:, tt, c * 128:(c + 1) * 128], cbb[:, t, :], tt == 0, False,
                           r=["fcs", ("cb", n % 2)], w=[("ps", c)])
                        MM(ps[c][:, :], fcs[:, tt, 256 + c * 128:256 + (c + 1) * 128], sbb[:, t, :], False, tt == 31,
                           r=["fcs", ("sb", n % 2)], w=[("ps", c)])
                n += 1
            for c in range(2):
                CP("dve" if c else "act", oF[g % 2][:, c, :], ps[c][:, :], r=[("ps", c)], w=[("oF", g % 2)])
            DMA("sp", oThv[6:8, :, g * 512:(g + 1) * 512].rearrange("c p n -> p c n"), oF[g % 2][:], r=[("oF", g % 2)],
                slot=f"oF{g % 2}")
        S.barrier()
        for c_ in range(8):
            CC("AllGather", ALU.bypass, oTh[c_ * 128:(c_ + 1) * 128, :], oTg[c_ * 256:(c_ + 1) * 256, :], [], [], "ccg")
        S.barrier()

    def outproj(l, wsrc, xin, xout, gathered):
        S.reset_phase()
        wo = S.alloc([128, 8, 1024], BF16)
        wr = S.alloc([128, 8, 16], F32)
        ob = [S.alloc([128, 8, 128], BF16) for _ in range(2)]
        xb = [S.alloc([128, 1024], F32) for _ in range(2)]
        yb = [S.alloc([128, 1024], F32) for _ in range(2)]
        x1b = [S.alloc([128, 1024], F32) for _ in range(2)]
        tmpb = [S.alloc([128, 1024], F32) for _ in range(2)]
        hfb = [S.alloc([128, 1024], F32) for _ in range(2)]
        hbb = [S.alloc([128, 1024], BF16) for _ in range(2)]
        hfT = [S.alloc([128, 8, 128], F32) for _ in range(2)]
        junk = S.alloc([128, 1024], BF16)
        stt = [S.alloc([128, 8], F32) for _ in range(2)]
        ex = S.alloc([128, 16], F32)
        wv = wsrc.rearrange("(k p) n -> p k n", p=128)
        for q in range(2):
            DMA("pool", wo[:, :, q * 512:(q + 1) * 512], wv[:, :, q * 512:(q + 1) * 512], w=["wo"], slot=f"wo{q}")
        DMA("sp", wr[:], rw_d[l].rearrange("(k p) e -> p k e", p=128), w=["wr"], slot="wr")
        for i in range(NT):
            i2 = i % 2
            osrc = (oTgv[i // 16][:, :, (i % 16) * 128:(i % 16 + 1) * 128] if gathered else oTd[:, :, i * 128:(i + 1) * 128])
            DMA("sp", ob[i2][:], osrc.rearrange("c p n -> p c n"), w=[("ob", i2)], slot=f"ob{i2}")
            DMA("sp", xb[i2][:], xin[i * 128:(i + 1) * 128, :], w=[("xb", i2)], slot=f"xb{i2}")
            for half in range(2):
                for c in range(8):
                    MM(ps[half][:, :], ob[i2][:, c, :], wo[:, c, half * 512:(half + 1) * 512], c == 0, c == 7,
                       r=[("ob", i2), "wo"], w=[("ps", half)])
                TT("dve", yb[i2][:, half * 512:(half + 1) * 512], ps[half][:, :], MOD[:, 2, half * 512:(half + 1) * 512],
                   ALU.mult, r=[("ps", half), "MOD"], w=[("yb", i2)])
            TT("pool", x1b[i2][:], yb[i2][:], xb[i2][:], ALU.add, r=[("yb", i2), ("xb", i2)], w=[("x1b", i2)])
            DMA("sp", xout[i * 128:(i + 1) * 128, :], x1b[i2][:], r=[("x1b", i2)], slot=f"x1o{i2}")
            modulate(x1b[i2][:], ("x1b", i2), MOD[:, 4, :], MOD[:, 3, :], hfb[i2][:], ("hfb", i2), stt[i2], ("st", i2),
                     tmpb[i2][:], ("tmp", i2), junk[:], "junk")
            CP("act", hbb[i2][:], hfb[i2][:], r=[("hfb", i2)], w=[("hbb", i2)])
            DMA("sp", hfd[i * 128:(i + 1) * 128, :], hbb[i2][:], r=[("hbb", i2)], slot=f"hfo{i2}")
            for k in range(8):
                bank = 2 + k // 4
                TR(ps[bank][:, (k % 4) * 128:(k % 4 + 1) * 128], hfb[i2][:, k * 128:(k + 1) * 128], identf[:],
                   r=[("hfb", i2), "identf"], w=[("ps", bank)])
            CP("act", hfT[i2][:, 0:4, :], ps[2][:, :].rearrange("p (k n) -> p k n", k=4), r=[("ps", 2)], w=[("hfT", i2)])
            CP("dve", hfT[i2][:, 4:8, :], ps[3][:, :].rearrange("p (k n) -> p k n", k=4), r=[("ps", 3)], w=[("hfT", i2)])
            for k in range(8):
                MM(ps[4][:, 0:16], hfT[i2][:, k, :], wr[:, k, :], k == 0, k == 7, r=[("hfT", i2), "wr"], w=[("ps", 4)])
            st = stt[i2]
            S.op("dve", lambda e, st=st: e.reduce_max(out=st[:, 4:5], in_=ps[4][:, 0:16], axis=AX.X), r=[("ps", 4)],
                 w=[("st5", i2)])
            TS("dve", st[:, 5:6], st[:, 4:5], -1.0, ALU.mult, r=[("st5", i2)], w=[("st6", i2)])
            ACT(ex[:], ps[4][:, 0:16], AF.Exp, r=[("ps", 4), ("st6", i2)], w=["ex", ("st7", i2)], bias=st[:, 5:6], scale=1.0,
                accum_out=st[:, 6:7])
            RECIP(st[:, 7:8], st[:, 6:7], r=[("st7", i2)], w=[("st8", i2)])
            TS("dve", AFF[:, i, :], ex[:], st[:, 7:8], ALU.mult, r=["ex", ("st8", i2)], w=["AFF"])
        S.barrier()

    def moe(l):
        S.reset_phase()
        slotm = S.alloc([128, 32, 16], F32)
        VALS = S.alloc([128, 32, 16, 5], BF16)
        VA = S.alloc([128, 512, 20], BF16)
        mark = S.sb_off
        affT = S.alloc([16, T], F32)
        junkb = S.alloc([16, T], BF16)
        maskT = S.alloc([16, T], BF16)
        bs = S.alloc([16, 8], F32)
        MASK = S.alloc([128, 512], F32)
        MASKb = S.alloc([128, 512], BF16)
        tots = S.alloc([128, 32, 16], F32)
        base = S.alloc([128, 32, 16], F32)
        r1 = S.alloc([128, 512], F32)
        gtmp = S.alloc([128, 512], BF16)
        zt = S.alloc([128, 2, 1024], F32)
        bm = S.alloc([128, 512], F32)
        ag = S.alloc([128, 512], F32)
        aoh = S.alloc([128, 512], BF16)
        AFFf = AFF.rearrange("p a b -> p (a b)")
        MSET("pool", zt[:], 0.0, w=["zt"])
        yv = ymoe.rearrange("(t p) d -> p t d", p=128)
        for q in range(16):
            DMA("sp", yv[:, q * 2:(q + 1) * 2, :], zt[:], r=["zt"], w=[("ymoe", 0)], slot="zy")
        for i in range(NT):
            bank = i // 4 % 2
            TR(ps[bank][0:16, (i % 4) * 128:(i % 4 + 1) * 128], AFF[:, i, :], identf[:], r=["AFF", "identf"], w=[("ps", bank)])
            if i % 4 == 3:
                CP("act", affT[:, (i // 4) * 512:(i // 4 + 1) * 512], ps[bank][0:16, :], r=[("ps", bank)], w=["affT"])
        MSET("dve", bs[:], 0.0, w=["bs"])
        for it in range(28):
            step = 2.0 ** -(it + 1)
            TS("dve", bs[:, 1:2], bs[:, 0:1], step, ALU.add, r=["bs"], w=["bs1"])
            TS("dve", junkb[:], affT[:], bs[:, 1:2], ALU.is_gt, 0.0, ALU.add, r=["affT", "bs1"], w=["junkb", "bs2"],
               accum=bs[:, 2:3])
            TS("dve", bs[:, 3:4], bs[:, 2:3], 511.5, ALU.is_gt, step, ALU.mult, r=["bs2"], w=["bs3"])
            TT("dve", bs[:, 0:1], bs[:, 0:1], bs[:, 3:4], ALU.add, r=["bs", "bs3"], w=["bs"])
        TS("dve", maskT[:], affT[:], bs[:, 0:1], ALU.is_gt, r=["affT", "bs"], w=["maskT"])
        for i in range(NT):
            TR(psb[2][:, i * 16:(i + 1) * 16], maskT[:, i * 128:(i + 1) * 128], identb[0:16, 0:16], r=["maskT", "identb"],
               w=[("ps", 2)])
        CP("act", MASKb[:], psb[2][:, 0:512], r=[("ps", 2)], w=["MASKb"])
        CP("dve", MASK[:], MASKb[:], r=["MASKb"], w=["MASK"])
        MM(ps[3][:, :], Ust[:], MASKb[:], True, True, r=["Ust", "MASKb"], w=[("ps", 3)])
        MM(ps[4][:, :], onesb[:], MASKb[:], True, True, r=["onesb", "MASKb"], w=[("ps", 4)])
        CP("act", tots.rearrange("p a b -> p (a b)"), ps[4][:, :], r=[("ps", 4)], w=["tots"])
        MSET("dve", base[:, 0, :], 0.0, w=["base"])
        for i in range(1, NT):
            TT("dve", base[:, i, :], base[:, i - 1, :], tots[:, i - 1, :], ALU.add, r=["base", "tots"], w=["base"])
        sf = slotm.rearrange("p a b -> p (a b)")
        TT("dve", sf, ps[3][:, :], base.rearrange("p a b -> p (a b)"), ALU.add, r=[("ps", 3), "base"], w=["slotm"])
        TS("dve", ag[:], sf, 128.0, ALU.is_ge, r=["slotm"], w=["ag"])
        STT("dve", ag[:], sf, 256.0, ag[:], ALU.is_ge, ALU.add, r=["slotm", "ag"], w=["ag"])
        STT("dve", ag[:], sf, 384.0, ag[:], ALU.is_ge, ALU.add, r=["slotm", "ag"], w=["ag"])
        STT("dve", bm[:], ag[:], -128.0, sf, ALU.mult, ALU.add, r=["slotm", "ag"], w=["bm"])
        STT("dve", sf, bm[:], 1.0, MASK[:], ALU.add, ALU.mult, r=["bm", "MASK"], w=["slotm"])
        TS("dve", sf, sf, -1.0, ALU.add, r=["slotm"], w=["slotm"])
        CP("pool", VALS[:, :, :, 0:2], tokhl[:], r=["tokhl"], w=["VALS"])
        Vg = lambda j: VALS[:, :, :, j].rearrange("p a b -> p (a b)")
        CP("dve", gtmp[:], AFFf, r=["AFF"], w=["gtmp"])
        CP("dve", VALS[:, :, :, 2], gtmp.rearrange("p (a b) -> p a b", a=32), r=["gtmp"], w=["VALS"])
        TT("dve", r1[:], AFFf, gtmp[:], ALU.subtract, r=["AFF", "gtmp"], w=["r1"])
        CP("dve", gtmp[:], r1[:], r=["r1"], w=["gtmp"])
        CP("dve", VALS[:, :, :, 3], gtmp.rearrange("p (a b) -> p a b", a=32), r=["gtmp"], w=["VALS"])
        TT("dve", r1[:], r1[:], gtmp[:], ALU.subtract, r=["r1", "gtmp"], w=["r1"])
        CP("dve", VALS[:, :, :, 4], r1.rearrange("p (a b) -> p a b", a=32), r=["r1"], w=["VALS"])
        VALf = VALS.rearrange("p a b c -> p (a b) c")
        for a_ in range(4):
            TS("dve", aoh[:], ag[:], float(a_), ALU.is_equal, r=["ag"], w=["aoh"])
            for v_ in range(5):
                TT("dve", VA[:, :, v_ * 4 + a_], VALf[:, :, v_], aoh[:], ALU.mult, r=["VALS", "aoh"], w=["VA"])

        S.barrier()
        S.sb_off = mark
        wgb = [S.alloc([128, 8, 1024], BF16) for _ in range(2)]
        wub = [S.alloc([128, 8, 1024], BF16) for _ in range(2)]
        wdb = [S.alloc([128, 8, 1024], BF16) for _ in range(2)]
        selb = [S.alloc([128, 128], BF16) for _ in range(4)]
        ivt = S.alloc([128, 5, 4], F32)
        tokf = S.alloc([128, 4], F32)
        idx = [S.alloc([128, 4], I32) for _ in range(2)]
        gg = [S.alloc([128, 4], F32) for _ in range(2)]
        xs = [S.alloc([128, 1024], BF16) for _ in range(4)]
        xsT = S.alloc([128, 8, 512], BF16)
        sg = [S.alloc([128, 512], F32) for _ in range(2)]
        aT = S.alloc([128, 8, 512], BF16)
        ysb = [S.alloc([128, 1024], F32) for _ in range(2)]
        yc = [0]

        def W(ex_):
            e2 = ex_ % 2
            for (buf, src, nm) in ((wgb, wg_d, "wg"), (wub, wu_d, "wu"), (wdb, wd_d, "wd")):
                sv = src[l, ex_].rearrange("(k p) n -> p k n", p=128)
                for q in range(2):
                    DMA("pool", buf[e2][:, :, q * 512:(q + 1) * 512], sv[:, :, q * 512:(q + 1) * 512], w=[(nm, e2)],
                        slot=f"{nm}{e2}{q}")

        def IDX(ex_):
            e2 = ex_ % 2
            for i in range(NT):
                sb_ = selb[i % 4]
                TS("dve", sb_[:], iota[:, 0:128], slotm[:, i, ex_:ex_ + 1], ALU.is_equal,
                   r=["iota", "slotm"], w=[("selb", i % 4)])
                MM(ps[5][:, 0:20], sb_[:], VA[:, i * 16 + ex_, :], i == 0, i == NT - 1, r=["VA", ("selb", i % 4)], w=[("ps", 5)])
            CP("dve", ivt.rearrange("p a b -> p (a b)"), ps[5][:, 0:20], r=[("ps", 5)], w=["ivt"])
            STT("dve", tokf[:], ivt[:, 0, :], 64.0, ivt[:, 1, :], ALU.mult, ALU.add, r=["ivt"], w=["tokf"])
            CP("dve", idx[e2][:], tokf[:], r=["tokf"], w=[("idx", e2)])
            TT("dve", gg[e2][:], ivt[:, 2, :], ivt[:, 3, :], ALU.add, r=["ivt"], w=[("gg", e2)])
            TT("dve", gg[e2][:], gg[e2][:], ivt[:, 4, :], ALU.add, r=["ivt", ("gg", e2)], w=[("gg", e2)])

        def G(ex_):
            e2 = ex_ % 2
            for grp in range(4):
                S.op("pool", lambda e, grp=grp, e2=e2: e.indirect_dma_start(
                    out=xs[grp][:], out_offset=None, in_=hfd,
                    in_offset=bass.IndirectOffsetOnAxis(ap=idx[e2][:, grp:grp + 1], axis=0)),
                    r=[("idx", e2)], w=[("xs", grp)], dma=f"xs{grp}")

        def XT(ex_):
            for grp in range(4):
                pT = psb[6 + grp % 2]
                for k in range(8):
                    TR(pT[:, k * 128:(k + 1) * 128], xs[grp][:, k * 128:(k + 1) * 128], identb[:], r=[("xs", grp), "identb"],
                       w=[("ps", 6 + grp % 2)])
                CP("act" if grp % 2 else "dve", xsT[:, :, grp * 128:(grp + 1) * 128],
                   pT[:, 0:1024].rearrange("p (k n) -> p k n", k=8), r=[("ps", 6 + grp % 2)], w=["xsT"])

        def FFN(ex_):
            e2 = ex_ % 2
            for fc in range(8):
                for k in range(8):
                    MM(ps[0][:, :], wgb[e2][:, k, fc * 128:(fc + 1) * 128], xsT[:, k, :], k == 0, k == 7, r=[("wg", e2), "xsT"],
                       w=[("ps", 0)])
                for k in range(8):
                    MM(ps[1][:, :], wub[e2][:, k, fc * 128:(fc + 1) * 128], xsT[:, k, :], k == 0, k == 7, r=[("wu", e2), "xsT"],
                       w=[("ps", 1)])
                ACT(sg[fc % 2][:], ps[0][:, :], AF.Silu, r=[("ps", 0)], w=[("sg", fc % 2)])
                TT("dve", aT[:, fc, :], sg[fc % 2][:], ps[1][:, :], ALU.mult, r=[("sg", fc % 2), ("ps", 1)], w=["aT"])
            for grp in range(4):
                y2 = yc[0] % 2
                yc[0] += 1
                for half in range(2):
                    bank = 2 + half
                    for fc in range(8):
                        MM(ps[bank][:, :], aT[:, fc, grp * 128:(grp + 1) * 128], wdb[e2][:, fc, half * 512:(half + 1) * 512],
                           fc == 0, fc == 7, r=["aT", ("wd", e2)], w=[("ps", bank)])
                    if half:
                        ACT(ysb[y2][:, 512:1024], ps[bank][:, :], AF.Copy, r=[("ps", bank), ("gg", e2)], w=[("ysb", y2)],
                            scale=gg[e2][:, grp:grp + 1])
                    else:
                        TS("dve", ysb[y2][:, 0:512], ps[bank][:, :], gg[e2][:, grp:grp + 1], ALU.mult,
                           r=[("ps", bank), ("gg", e2)], w=[("ysb", y2)])
                S.op("pool", lambda e, grp=grp, e2=e2, y2=y2: e.indirect_dma_start(
                    out=ymoe, out_offset=bass.IndirectOffsetOnAxis(ap=idx[e2][:, grp:grp + 1], axis=0),
                    in_=ysb[y2][:], in_offset=None, compute_op=ALU.add),
                    r=[("idx", e2), ("ysb", y2), ("ymoe", ex_)], w=[("ymoe", ex_ + 1), ("ymoeg", grp)], dma=f"ys{y2}")

        NEXP = 8
        IDX(0)
        G(0)
        W(0)
        W(1)
        for ex_ in range(NEXP):
            if ex_ + 1 < NEXP:
                IDX(ex_ + 1)
            XT(ex_)
            if ex_ + 1 < NEXP:
                G(ex_ + 1)
            FFN(ex_)
            if ex_ + 2 < NEXP:
                W(ex_ + 2)
        S.barrier()
        for c_ in range(4):
            CC("AllReduce", ALU.add, ymoe[c_ * 1024:(c_ + 1) * 1024, :], ysum[c_ * 1024:(c_ + 1) * 1024, :], [], [], "ccr")
        S.barrier()

    def combine(src, dst):
        S.reset_phase()
        xb = [S.alloc([128, 1024], F32) for _ in range(2)]
        yb = [S.alloc([128, 1024], F32) for _ in range(2)]
        ob = [S.alloc([128, 1024], F32) for _ in range(2)]
        for i in range(NT):
            i2 = i % 2
            DMA("sp", xb[i2][:], src[i * 128:(i + 1) * 128, :], w=[("xb", i2)], slot=f"cx{i2}")
            DMA("act", yb[i2][:], ysum[i * 128:(i + 1) * 128, :], w=[("yb", i2)], slot=f"cy{i2}")
            TT("dve", yb[i2][:], yb[i2][:], MOD[:, 5, :], ALU.mult, r=[("yb", i2), "MOD"], w=[("yb", i2)])
            TT("pool", ob[i2][:], yb[i2][:], xb[i2][:], ALU.add, r=[("yb", i2), ("xb", i2)], w=[("ob", i2)])
            DMA("sp", dst[i * 128:(i + 1) * 128, :], ob[i2][:], r=[("ob", i2)], slot=f"co{i2}")
        S.barrier()

    def conv():
        S.reset_phase()
        hT = S.alloc([128, 8, T], BF16)
        xb = [S.alloc([128, 1024], F32) for _ in range(2)]
        tmpb = [S.alloc([128, 1024], F32) for _ in range(2)]
        hb = [S.alloc([128, 1024], BF16) for _ in range(2)]
        junk = S.alloc([128, 1024], BF16)
        stt = [S.alloc([128, 4], F32) for _ in range(2)]
        cw = S.alloc([128, 8, 3], F32)
        w3 = [S.alloc([128, 8, 3, 128], BF16) for _ in range(2)]
        z = S.alloc([128, T + 2], F32)
        tt_ = S.alloc([128, T], F32)
        bgs = S.alloc([128, T], BF16)
        cgs = [S.alloc([128, 512], F32) for _ in range(2)]
        vT = [S.alloc([128, T], BF16) for _ in range(2)]
        DMA("sp", cw[:], convw_d, w=["cw"], slot="cw")
        MSET("pool", z[:], 0.0, w=["z"])
        for i in range(NT):
            i2 = i % 2
            DMA("sp", xb[i2][:], x2d[i * 128:(i + 1) * 128, :], w=[("xb", i2)], slot=f"xb{i2}")
            modulate(xb[i2][:], ("xb", i2), MOD[:, 1, :], MOD[:, 0, :], hb[i2][:], ("hb", i2), stt[i2], ("st", i2),
                     tmpb[i2][:], ("tmp", i2), junk[:], "junk")
            pT = psb[i % 2]
            for k in range(8):
                TR(pT[:, k * 128:(k + 1) * 128], hb[i2][:, k * 128:(k + 1) * 128], identb[:], r=[("hb", i2), "identb"],
                   w=[("ps", i % 2)])
            CP("act", hT[:, :, i * 128:(i + 1) * 128], pT[:, 0:1024].rearrange("p (k n) -> p k n", k=8), r=[("ps", i % 2)],
               w=["hT"])
        wv = cwin_d.rearrange("(k p) n -> p k n", p=128)
        for fc in range(8):
            f2 = fc % 2
            for j in range(3):
                DMA("pool", w3[f2][:, :, j, :], wv[:, :, j * 1024 + fc * 128:j * 1024 + (fc + 1) * 128], w=[("w3", f2)],
                    slot=f"w3{f2}{j}")
            for b in range(8):
                for j in range(3):
                    bank = 2 + j * 2 + b % 2
                    for k in range(8):
                        MM(ps[bank][:, :], w3[f2][:, k, j, :], hT[:, k, b * 512:(b + 1) * 512], k == 0, k == 7,
                           r=[("w3", f2), "hT"], w=[("ps", bank)])
                CP("act", bgs[:, b * 512:(b + 1) * 512], ps[2 + b % 2][:, :], r=[("ps", 2 + b % 2)], w=["bgs"])
                CP("act", cgs[b % 2][:], ps[4 + b % 2][:, :], r=[("ps", 4 + b % 2)], w=[("cgs", b % 2)])
                TT("dve", z[:, 1 + b * 512:1 + (b + 1) * 512], cgs[b % 2][:], ps[6 + b % 2][:, :], ALU.mult,
                   r=[("cgs", b % 2), ("ps", 6 + b % 2)], w=["z"])
            ACT(tt_[:], z[:, 0:T], AF.Copy, r=["z", "cw"], w=["tt"], scale=cw[:, fc, 0:1])
            STT("dve", tt_[:], z[:, 1:T + 1], cw[:, fc, 1:2], tt_[:], ALU.mult, ALU.add, r=["z", "cw", "tt"], w=["tt"])
            STT("dve", tt_[:], z[:, 2:T + 2], cw[:, fc, 2:3], tt_[:], ALU.mult, ALU.add, r=["z", "cw", "tt"], w=["tt"])
            TT("pool", vT[f2][:], bgs[:], tt_[:], ALU.mult, r=["bgs", "tt"], w=[("vT", f2)])
            DMA("sp", oTd[fc], vT[f2][:], r=[("vT", f2)], slot=f"vT{f2}")
        S.barrier()

    phases = [
        lambda: ada(0, True),
        proj0,
        attn,
        fourier,
        lambda: outproj(0, wout_d, x_d, x1d, True),
        lambda: moe(0),
        lambda: combine(x1d, x2d),
        lambda: ada(1, False),
        conv,
        lambda: outproj(1, cwout_d, x2d, x3d, False),
        lambda: moe(1),
        lambda: combine(x3d, out_d),
    ]
    for i, ph in enumerate(phases):
        if stop is not None and i >= stop:
            break
        ph()
    S.emit()
    return nc, S


import ml_dtypes

_CACHE = {}


def _consts():
    if "c" in _CACHE:
        return _CACHE["c"]
    bf = ml_dtypes.bfloat16
    cpk = np.zeros((128, 1152), np.float32)
    cpk[:, 0:128] = np.eye(128)
    bo = np.zeros((128, 128), np.float32)
    bo[0:64, 0:64] = 1.0
    bo[64:128, 64:128] = 1.0
    cpk[:, 128:256] = bo
    R = np.zeros((64, 64), np.float32)
    for j in range(64):
        q = j // 16
        if q % 2 == 0:
            R[j, j + 16] = -1.0
        else:
            R[j, j - 16] = 1.0
    R2 = np.zeros((128, 128), np.float32)
    R2[0:64, 0:64] = R
    R2[64:128, 64:128] = R
    cpk[:, 256:384] = R2.T
    cpk[:, 384:512] = np.triu(np.ones((128, 128), np.float32), 1)
    cpk[:, 512:640] = 1.0
    cpk[:, 640:1152] = np.arange(512, dtype=np.float32)[None, :]
    tok = (np.arange(32)[None, :] * 128 + np.arange(128)[:, None])
    tokhl = np.zeros((128, 32, 16, 2), np.float32)
    tokhl[:, :, :, 0] = (tok // 64)[:, :, None]
    tokhl[:, :, :, 1] = (tok % 64)[:, :, None]
    sel = np.zeros((2, 256), np.float32)
    sel[0, 0:128] = 1.0
    sel[1, 128:256] = 1.0
    n = 4096
    r = np.repeat(np.arange(n // 64, dtype=np.float32), 64)
    col = np.tile(np.arange(64, dtype=np.float32), n // 64)
    inv = (np.float32(10000.0) ** (-np.arange(16, dtype=np.float32) / np.float32(16))).astype(np.float32)
    ar = r[:, None] * inv
    ac = col[:, None] * inv
    ang = np.concatenate([ar, ar, ac, ac], axis=-1).astype(np.float32)
    cosT = np.cos(ang).astype(np.float32).T
    sinT = np.sin(ang).astype(np.float32).T
    rope = np.stack([np.concatenate([cosT, cosT], 0), np.concatenate([sinT, sinT], 0)]).astype(np.float32)
    cm = np.arange(64)[:, None] * np.arange(64)[None, :]
    cb = np.cos(2 * np.pi * (cm % 64) / 64.0) / 512.0
    sb = np.sin(2 * np.pi * (cm % 64) / 64.0) / 512.0
    csd = np.zeros((256, 512), np.float32)
    for g in range(4):
        csd[g * 64:(g + 1) * 64, g * 64:(g + 1) * 64] = cb
        csd[g * 64:(g + 1) * 64, 256 + g * 64:256 + (g + 1) * 64] = sb
    kn = (np.arange(n, dtype=np.int64)[:, None] * np.arange(n, dtype=np.int64)[None, :]) % n
    angp = kn.astype(np.float64) * (2 * np.pi / n)
    cosd = np.cos(angp).astype(bf)
    sind = (-np.sin(angp)).astype(bf)
    perms = [np.arange(n), np.concatenate([np.arange(2048, 4096), np.arange(0, 2048)])]
    cos_r = [np.ascontiguousarray(cosd[p][:, p[:2048]]) for p in perms]
    sin_r = [np.ascontiguousarray(sind[p][:, p[:2048]]) for p in perms]
    rope_r = [np.ascontiguousarray(rope[:, :, p]) for p in perms]
    c = dict(cpk=cpk, tokhl=tokhl.reshape(128, 1024), sel=sel, csd=csd, perms=perms, cos_r=cos_r, sin_r=sin_r, rope_r=rope_r)
    _CACHE["c"] = c
    return c


def kernel(x, c, ctx, c_ctx, ada_w, ada_b, norm_mix, norm_ffn, attn_w_in, attn_q_norm, attn_k_norm,
           lam_q1, lam_k1, lam_q2, lam_k2, attn_subln, attn_w_out, conv_w_in, conv_w, conv_w_out,
           router_w, moe_w_gate, moe_w_up, moe_w_down, _stop=None, _debug=False, _ncores=8):
    f = lambda a: np.ascontiguousarray(np.asarray(a, dtype=np.float32))
    x, c, ctx, c_ctx = f(x), f(c), f(ctx), f(c_ctx)
    K = _consts()
    ck = ("nc", _stop, _debug, _ncores)
    if ck not in _CACHE:
        _CACHE[ck] = build_nc(_stop, _debug, _ncores)[0]
    nc = _CACHE[ck]
    shared = {k: K[k] for k in ("cpk", "tokhl", "sel", "csd")}
    shared["ada_w"] = f(ada_w)
    shared["adab2"] = f(np.repeat(f(ada_b)[:, None, :], 2, axis=1))
    shared["nmrep"] = f(np.repeat(f(norm_mix)[:, None, :], 128, axis=1))
    shared["nfrep"] = f(np.repeat(f(norm_ffn)[:, None, :], 128, axis=1))
    shared["attn_w_in"] = f(attn_w_in)[0]
    shared["attn_w_out"] = f(attn_w_out)[0]
    shared["conv_w_in"] = f(conv_w_in)[0]
    shared["conv_w_out"] = f(conv_w_out)[0]
    shared["qkcol"] = f(np.stack([np.tile(f(attn_q_norm)[0], 2), np.tile(f(attn_k_norm)[0], 2)], axis=1))
    lamrow = np.concatenate([f(lam_q1)[0], f(lam_k1)[0], f(lam_q2)[0], f(lam_k2)[0]])
    shared["lamrep"] = f(np.repeat(lamrow[None, :], 128, axis=0))
    shared["sublnrep"] = f(np.repeat(f(attn_subln)[0][None, :], 128, axis=0))
    shared["convw"] = f(f(conv_w)[0].T.reshape(8, 128, 3).transpose(1, 0, 2))
    rw, wg, wu, wd = f(router_w), f(moe_w_gate), f(moe_w_up), f(moe_w_down)
    eperm = [np.arange(16), np.concatenate([np.arange(8, 16), np.arange(0, 8)])]
    per_rank = []
    for r in range(2):
        per_rank.append(dict(
            router_w=f(rw[:, :, eperm[r]]), moe_w_gate=f(wg[:, r * 8:(r + 1) * 8]), moe_w_up=f(wu[:, r * 8:(r + 1) * 8]),
            moe_w_down=f(wd[:, r * 8:(r + 1) * 8]), cosd=K["cos_r"][r], sind=K["sin_r"][r], rope=K["rope_r"][r]))
    in_maps = []
    for core in range(_ncores):
        b, r = core // 2, core % 2
        m = dict(shared)
        m.update(per_rank[r])
        m["x"] = x[b]
        m["xloc"] = f(x[b][K["perms"][r]])
        m["ctx"] = ctx[b]
        c2 = np.stack([c[b].reshape(8, 128).T, c_ctx.reshape(8, 128).T], axis=-1)
        m["c2"] = f(c2)
        in_maps.append(m)
    res = run_bass_kernel_spmd(nc, in_maps, core_ids=list(range(_ncores)))
    _CACHE["res"] = res
    if _debug:
        return res
    return np.stack([np.asarray(res.results[2 * b]["out"], dtype=np.float32) for b in range(4)], axis=0)
```

```python
from concourse.bass_utils import run_bass_kernel_spmd
import numpy as np
import concourse.bass as bass
import concourse.mybir as mybir

F32 = mybir.dt.float32
BF16 = mybir.dt.bfloat16
I32 = mybir.dt.int32
ALU = mybir.AluOpType
AF = mybir.ActivationFunctionType
AX = mybir.AxisListType
ENG = ("pe", "act", "dve", "pool", "sp")


class Op:
    __slots__ = ("eng", "fn", "deps", "sig", "sem", "val", "dma", "waits", "semkey", "inc")


class Sched:
    def __init__(self, nc):
        self.nc = nc
        self.streams = {e: [] for e in ENG}
        self.lw = {}
        self.lr = {}
        self.pending_dma = []
        self.last_real = {}
        self.slot_sem = {}
        self.slot_cnt = {}
        self.nsem = 0
        self.sb_off = 0
        self.sb_base = 0
        self.uid = 0

    def alloc(self, shape, dtype, name="t"):
        if not hasattr(self, "views"):
            big = self.nc.alloc_sbuf_tensor("arena", [128, 103 * 1024], BF16)
            self.views = {BF16: big, F32: big.bitcast(F32), I32: big.bitcast(I32)}
        esz = mybir.dt.size(dtype)
        n = int(np.prod(shape[1:]))
        off = (self.sb_off + 63) // 64 * 64
        self.sb_off = off + n * esz
        assert self.sb_off <= 206 * 1024, (name, self.sb_off)
        ap = self.views[dtype][0:shape[0], off // esz: off // esz + n]
        if len(shape) == 3:
            ap = ap.rearrange("p (a b) -> p a b", a=shape[1])
        elif len(shape) == 4:
            ap = ap.rearrange("p (a b c) -> p a b c", a=shape[1], b=shape[2])
        return ap

    def mark_persistent(self):
        self.sb_base = self.sb_off

    def reset_phase(self):
        self.sb_off = self.sb_base

    def newsem(self, name):
        self.nsem += 1
        return self.nc.alloc_semaphore(f"{name}_{self.nsem}")

    def op(self, eng, fn, r=(), w=(), dma=None, inc=16):
        o = Op()
        o.eng, o.fn, o.sig, o.dma = eng, fn, False, dma
        o.inc = inc if dma else 1
        o.sem = None
        o.val = 0
        deps = {}
        for k in r:
            for d in self.lw.get(k, ()):
                deps[id(d)] = d
        for k in w:
            for d in self.lw.get(k, ()):
                if d.dma or dma or d.eng != eng:
                    deps[id(d)] = d
            for d in self.lr.get(k, ()):
                if d.dma or dma or d.eng != eng:
                    deps[id(d)] = d
        o.deps = list(deps.values())
        for d in o.deps:
            d.sig = True
        for k in w:
            self.lw[k] = [o]
            self.lr[k] = []
        for k in r:
            lst = self.lr.setdefault(k, [])
            if not dma:
                lst[:] = [x for x in lst if x.dma or x.eng != eng]
            lst.append(o)
        self.streams[eng].append(o)
        if dma:
            o.sig = True
            self.pending_dma.append(o)
        elif fn is not None:
            self.last_real[eng] = o
        return o

    def barrier(self):
        lasts = list(self.last_real.values()) + list(self.pending_dma)
        for d in lasts:
            d.sig = True
        for e in ENG:
            o = Op()
            o.eng, o.fn, o.sig, o.dma, o.sem, o.val, o.inc = e, None, False, None, None, 0, 1
            o.deps = [d for d in lasts if d.dma or d.eng != e]
            self.streams[e].append(o)
        self.lw, self.lr, self.pending_dma = {}, {}, []

    def finalize(self):
        for e in ENG:
            cnt = 0
            cur = None
            for o in self.streams[e]:
                if o.dma:
                    key = (e, o.dma)
                    if key not in self.slot_sem:
                        self.slot_sem[key] = self.newsem("d")
                        self.slot_cnt[key] = 0
                    self.slot_cnt[key] += o.inc
                    o.sem, o.val, o.semkey = self.slot_sem[key], self.slot_cnt[key], ("d",) + key
                elif o.sig:
                    if cur is None or cnt >= 30000:
                        cur = self.newsem("e" + e)
                        curkey = ("e", e, self.nsem)
                        cnt = 0
                    cnt += 1
                    o.sem, o.val, o.semkey = cur, cnt, curkey
        nw = 0
        for e in ENG:
            known = {}
            for o in self.streams[e]:
                need = {}
                for d in o.deps:
                    assert d.sem is not None
                    if known.get(d.semkey, 0) < d.val:
                        if need.get(d.semkey, (None, 0))[1] < d.val:
                            need[d.semkey] = (d.sem, d.val)
                for k, (sm, v) in need.items():
                    known[k] = v
                o.waits = list(need.values())
                nw += len(o.waits)
        self.nwaits = nw

    def replay(self, eng, e):
        for o in self.streams[eng]:
            for sm, v in o.waits:
                e.wait_ge(sm, v)
            if o.fn is not None:
                ins = o.fn(e)
                if o.sig:
                    ins.then_inc(o.sem, o.inc)

    def emit(self):
        self.barrier()
        self.finalize()
        with self.nc.Block() as blk:
            @blk.tensor
            def _(e):
                self.replay("pe", e)

            @blk.scalar
            def _(e):
                self.replay("act", e)

            @blk.vector
            def _(e):
                self.replay("dve", e)

            @blk.gpsimd
            def _(e):
                self.replay("pool", e)

            @blk.sync
            def _(e):
                self.replay("sp", e)


EPS = 1e-6
T = 4096
NT = 32
NKEY = 4352
NKT = 34
DEBUG = False


def build_nc(stop=None, debug=False, ncores=8):
    nc = bass.Bass("TRN2", target_bir_lowering=False)
    S = Sched(nc)

    def din(name, shape, dt=F32):
        return nc.dram_tensor(name, list(shape), dt, kind="ExternalInput").ap()

    def dscr(name, shape, dt, out=False):
        return nc.dram_tensor(name, list(shape), dt, kind="ExternalOutput" if debug else "Internal").ap()

    x_d = din("x", [T, 1024])
    xloc_d = din("xloc", [T, 1024])
    ctx_d = din("ctx", [256, 1024])
    c2_d = din("c2", [128, 8, 2])
    adaw_d = din("ada_w", [2, 1024, 6144])
    adab_d = din("adab2", [2, 2, 6144])
    nm_d = din("nmrep", [2, 128, 1024])
    nf_d = din("nfrep", [2, 128, 1024])
    win_d = din("attn_w_in", [1024, 2560])
    wout_d = din("attn_w_out", [1024, 1024])
    cwin_d = din("conv_w_in", [1024, 3072])
    cwout_d = din("conv_w_out", [1024, 1024])
    qk_d = din("qkcol", [128, 2])
    lam_d = din("lamrep", [128, 256])
    sub_d = din("sublnrep", [128, 128])
    convw_d = din("convw", [128, 8, 3])
    rw_d = din("router_w", [2, 1024, 16])
    wg_d = din("moe_w_gate", [2, 8, 1024, 1024])
    wu_d = din("moe_w_up", [2, 8, 1024, 1024])
    wd_d = din("moe_w_down", [2, 8, 1024, 1024])
    cpk_d = din("cpk", [128, 1152])
    tok_d = din("tokhl", [128, 1024])
    sel_d = din("sel", [2, 256])
    rope_d = din("rope", [2, 128, T])
    cs_d = din("csd", [256, 512])
    cos_d = din("cosd", [T, 2048], BF16)
    sin_d = din("sind", [T, 2048], BF16)
    out_d = nc.dram_tensor("out", [T, 1024], F32, kind="ExternalOutput").ap()

    kTd = dscr("kTd", [6, 128, NKEY], BF16)
    qTd = dscr("qTd", [6, 128, 2048], BF16)
    V1d = dscr("V1d", [NKEY, 774], BF16)
    fcsd = dscr("fcsd", [T, 512], BF16)
    oTd = dscr("oTd", [8, 128, T], BF16)
    x1d = dscr("x1d", [T, 1024], F32, DEBUG)
    x2d = dscr("x2d", [T, 1024], F32, DEBUG)
    x3d = dscr("x3d", [T, 1024], F32, DEBUG)
    hfd = dscr("hfd", [T, 1024], BF16)
    oTh = nc.dram_tensor("oTh", [1024, 2048], BF16).ap()
    oTg = nc.dram_tensor("oTg", [2048, 2048], BF16).ap()
    ymoe = nc.dram_tensor("ymoe", [T, 1024], F32).ap()
    ysum = nc.dram_tensor("ysum", [T, 1024], F32).ap()
    RG = [[2 * i, 2 * i + 1] for i in range(ncores // 2)]
    oThv = oTh.rearrange("(c p) n -> c p n", p=128)
    oTgv = oTg.rearrange("(c r p) n -> r c p n", r=2, c=8)

    def CC(kind, alu, src, dst, r, w, name):
        S.op("pool", lambda e: e.collective_compute(kind, alu, replica_groups=RG, ins=[src.opt()], outs=[dst.opt()]),
             r=r, w=w, dma=name, inc=1)

    pbig = [nc.alloc_psum_tensor(f"pq{j}", [128, 1024], F32) for j in range(4)]
    pbigb = [p.bitcast(BF16) for p in pbig]
    ps = [pbig[i // 2][:, (i % 2) * 512:(i % 2 + 1) * 512] for i in range(8)]
    psb = [pbigb[i // 2][:, (i % 2) * 1024:(i % 2 + 1) * 1024] for i in range(8)]

    def DMA(eng, out, in_, r=(), w=(), slot=None):
        S.op(eng, lambda e: e.dma_start(out=out, in_=in_), r=r, w=w, dma=slot)

    def MM(out, lhsT, rhs, start, stop, r=(), w=()):
        S.op("pe", lambda e: e.matmul(out, lhsT, rhs, start=start, stop=stop), r=r, w=w)

    def TR(out, in_, ident, r=(), w=()):
        S.op("pe", lambda e: e.transpose(out, in_, ident), r=r, w=w)

    def ACT(out, in_, func, r=(), w=(), **kw):
        S.op("act", lambda e: e.activation(out=out, in_=in_, func=func, **kw), r=r, w=w)

    def TT(eng, out, in0, in1, op, r=(), w=()):
        S.op(eng, lambda e: e.tensor_tensor(out=out, in0=in0, in1=in1, op=op), r=r, w=w)

    def TS(eng, out, in0, s1, op0, s2=None, op1=None, r=(), w=(), accum=None):
        if op1 is None:
            S.op(eng, lambda e: e.tensor_single_scalar(out=out, in_=in0, scalar=s1, op=op0), r=r, w=w)
        else:
            S.op(eng, lambda e: e.tensor_scalar(out=out, in0=in0, scalar1=s1, scalar2=s2, op0=op0, op1=op1,
                                                accum_out=accum), r=r, w=w)

    def STT(eng, out, in0, scalar, in1, op0, op1, r=(), w=()):
        S.op(eng, lambda e: e.scalar_tensor_tensor(out=out, in0=in0, scalar=scalar, in1=in1, op0=op0, op1=op1),
             r=r, w=w)

    def CP(eng, out, in_, r=(), w=()):
        if eng == "act":
            ACT(out, in_, AF.Copy, r=r, w=w)
        else:
            S.op(eng, lambda e: e.tensor_copy(out=out, in_=in_), r=r, w=w)

    def RECIP(out, in_, r=(), w=()):
        S.op("dve", lambda e: e.reciprocal(out=out, in_=in_), r=r, w=w)

    def MSET(eng, ap, v, w=()):
        S.op(eng, lambda e: e.memset(ap, v), w=w)

    identf = S.alloc([128, 128], F32)
    identb = S.alloc([128, 128], BF16)
    bones = S.alloc([128, 128], BF16)
    RT = S.alloc([128, 128], BF16)
    Ust = S.alloc([128, 128], BF16)
    onesb = S.alloc([128, 128], BF16)
    iota = S.alloc([128, 512], F32)
    tokhl = S.alloc([128, 32, 16, 2], BF16)
    sel = S.alloc([2, 256], F32)
    MOD = S.alloc([128, 6, 1024], F32)
    CMOD = S.alloc([128, 2, 1024], F32)
    AFF = S.alloc([128, 32, 16], F32)
    small = S.alloc([128, 64], F32)
    DMA("sp", identf[:], cpk_d[:, 0:128], w=["identf"], slot="c0")
    DMA("sp", iota[:], cpk_d[:, 640:1152], w=["iota"], slot="c1")
    DMA("sp", sel[:], sel_d, w=["sel"], slot="c2")
    DMA("pool", identb[:], cpk_d[:, 0:128], w=["identb"], slot="c3")
    DMA("pool", bones[:], cpk_d[:, 128:256], w=["bones"], slot="c4")
    DMA("pool", RT[:], cpk_d[:, 256:384], w=["RT"], slot="c5")
    DMA("pool", Ust[:], cpk_d[:, 384:512], w=["Ust"], slot="c6")
    DMA("pool", onesb[:], cpk_d[:, 512:640], w=["onesb"], slot="c7")
    DMA("pool", tokhl.rearrange("p a b c -> p (a b c)"), tok_d, w=["tokhl"], slot="c8")
    S.mark_persistent()
    S.barrier()

    def ada(l, with_ctx):
        S.reset_phase()
        sc = S.alloc([128, 8, 2], F32)
        m2 = S.alloc([2, 6144], F32)
        ab = S.alloc([2, 6144], F32)
        wb = [S.alloc([128, 8, 512], F32) for _ in range(4)]
        nmt = S.alloc([128, 1024], F32)
        nft = S.alloc([128, 1024], F32)
        DMA("sp", sc[:], c2_d, w=["sc"], slot="a0")
        DMA("sp", ab[:], adab_d[l], w=["ab"], slot="a1")
        DMA("sp", nmt[:], nm_d[l], w=["nmt"], slot="a2")
        DMA("sp", nft[:], nf_d[l], w=["nft"], slot="a3")
        ACT(sc[:], sc[:], AF.Silu, r=["sc"], w=["sc"])
        wv = adaw_d[l].rearrange("(k p) n -> p k n", p=128)
        for cg in range(12):
            wt = wb[cg % 4]
            DMA("sp" if cg % 2 else "act", wt[:], wv[:, :, cg * 512:(cg + 1) * 512], w=[("wb", cg % 4)], slot=f"aw{cg % 4}")
            pp = ps[cg % 2]
            for k in range(8):
                MM(pp[0:2, :], sc[:, k, :], wt[:, k, :], k == 0, k == 7, r=["sc", ("wb", cg % 4)], w=[("ps", cg % 2)])
            TT("dve", m2[:, cg * 512:(cg + 1) * 512], pp[0:2, :], ab[:, cg * 512:(cg + 1) * 512], ALU.add,
               r=[("ps", cg % 2), "ab"], w=["m2"])
        for j in range(12):
            pp = ps[2 + j % 2]
            MM(pp[:, :], sel[0:2, 0:128], m2[0:2, j * 512:(j + 1) * 512], True, True, r=["sel", "m2"], w=[("ps", 2 + j % 2)])
            CP("act" if j % 2 else "dve", MOD[:, j // 2, (j % 2) * 512:(j % 2 + 1) * 512], pp[:, :],
               r=[("ps", 2 + j % 2)], w=["MOD"])
            if with_ctx and j < 4:
                pq = ps[4 + j % 2]
                MM(pq[:, :], sel[0:2, 128:256], m2[0:2, j * 512:(j + 1) * 512], True, True, r=["sel", "m2"],
                   w=[("ps", 4 + j % 2)])
                CP("act" if j % 2 else "dve", CMOD[:, j // 2, (j % 2) * 512:(j % 2 + 1) * 512], pq[:, :],
                   r=[("ps", 4 + j % 2)], w=["CMOD"])
        STT("dve", MOD[:, 1, :], MOD[:, 1, :], 1.0, nmt[:], ALU.add, ALU.mult, r=["MOD", "nmt"], w=["MOD"])
        TS("dve", MOD[:, 1, :], MOD[:, 1, :], 32.0, ALU.mult, r=["MOD"], w=["MOD"])
        STT("dve", MOD[:, 4, :], MOD[:, 4, :], 1.0, nft[:], ALU.add, ALU.mult, r=["MOD", "nft"], w=["MOD"])
        TS("dve", MOD[:, 4, :], MOD[:, 4, :], 32.0, ALU.mult, r=["MOD"], w=["MOD"])
        if with_ctx:
            STT("dve", CMOD[:, 1, :], CMOD[:, 1, :], 1.0, nmt[:], ALU.add, ALU.mult, r=["CMOD", "nmt"], w=["CMOD"])
            TS("dve", CMOD[:, 1, :], CMOD[:, 1, :], 32.0, ALU.mult, r=["CMOD"], w=["CMOD"])
        S.barrier()

    def modulate(xt, xkey, G32, SH, out, okey, st, skey, tmp, tkey, junk, jkey):
        ACT(junk, xt, AF.Square, r=[xkey], w=[jkey, skey], accum_out=st[:, 0:1])
        ACT(st[:, 1:2], st[:, 0:1], AF.Sqrt, r=[skey], w=[skey], bias=1024.0 * EPS, scale=1.0)
        RECIP(st[:, 2:3], st[:, 1:2], r=[skey], w=[skey])
        STT("dve", tmp, xt, st[:, 2:3], G32, ALU.mult, ALU.mult, r=[xkey, skey, "MOD", "CMOD"], w=[tkey])
        TT("pool", out, tmp, SH, ALU.add, r=[tkey, "MOD", "CMOD"], w=[okey])

    def proj0():
        S.reset_phase()
        w = S.alloc([128, 8, 2560], BF16)
        cosT = S.alloc([128, T], F32)
        sinT = S.alloc([128, T], F32)
        qk = S.alloc([128, 2], F32)
        CS = S.alloc([128, 2, 512], BF16)
        xb = [S.alloc([128, 1024], F32) for _ in range(2)]
        tmpb = [S.alloc([128, 1024], F32) for _ in range(2)]
        hb = [S.alloc([128, 1024], BF16) for _ in range(2)]
        junk = S.alloc([128, 1024], BF16)
        stt = [S.alloc([128, 4], F32) for _ in range(2)]
        hTb = [S.alloc([128, 8, 512], BF16) for _ in range(2)]
        sqb2 = [S.alloc([128, 512], BF16) for _ in range(2)]
        rs2 = [S.alloc([128, 512], F32) for _ in range(2)]
        knb = [S.alloc([128, 512], BF16) for _ in range(2)]
        t12 = [S.alloc([128, 512], F32) for _ in range(2)]
        t22 = [S.alloc([128, 512], F32) for _ in range(2)]
        kout = [S.alloc([128, 512], BF16) for _ in range(2)]
        v1t = [S.alloc([128, 6, 129], BF16) for _ in range(2)]
        fT = S.alloc([128, 2, 512], BF16)
        fcst = [S.alloc([128, 512], BF16) for _ in range(2)]
        wv = win_d.rearrange("(k p) n -> p k n", p=128)
        for q in range(4):
            DMA("pool", w[:, :, q * 640:(q + 1) * 640], wv[:, :, q * 640:(q + 1) * 640], w=["w"], slot=f"pw{q}")
        DMA("sp", cosT[:], rope_d[0], w=["cos"], slot="p0")
        DMA("sp", sinT[:], rope_d[1], w=["sin"], slot="p1")
        DMA("sp", qk[:], qk_d, w=["qk"], slot="p2")
        DMA("pool", CS[:], cs_d.rearrange("(c p) n -> p c n", p=128), w=["CS"], slot="p3")
        TS("dve", qk[:, 1:2], qk[:, 1:2], 8.0, ALU.mult, r=["qk"], w=["qk"])
        for i in range(2):
            MSET("pool", v1t[i][:], 1.0, w=[("v1t", i)])
        nrc = [0]

        def normrope(pp, pkey, nt, gain, rope, pos0, dst):
            c = nrc[0]
            nrc[0] += 1
            kn = knb[c % 2]
            ko = kout[c % 2]
            c2 = c % 2
            sqb, rs, t1, t2 = sqb2[c2], rs2[c2], t12[c2], t22[c2]
            bss, brot = 4 + c2, 6 + c2
            ACT(sqb[:, :nt], pp[:, :nt], AF.Square, r=[pkey], w=[("sqb", c2)])
            MM(ps[bss][:, :nt], bones[:], sqb[:, :nt], True, True, r=["bones", ("sqb", c2)], w=[("ps", bss)])
            ACT(rs[:, :nt], ps[bss][:, :nt], AF.Sqrt, r=[("ps", bss)], w=[("rs", c2)], bias=64.0 * EPS, scale=1.0)
            RECIP(rs[:, :nt], rs[:, :nt], r=[("rs", c2)], w=[("rs", c2)])
            STT("dve", kn[:, :nt], pp[:, :nt], gain, rs[:, :nt], ALU.mult, ALU.mult, r=[pkey, "qk", ("rs", c2)],
                w=[("kn", c2)])
            if rope:
                MM(ps[brot][:, :nt], RT[:], kn[:, :nt], True, True, r=["RT", ("kn", c2)], w=[("ps", brot)])
                TT("pool", t1[:, :nt], kn[:, :nt], cosT[:, pos0:pos0 + nt], ALU.mult, r=[("kn", c2), "cos"], w=[("t1", c2)])
                TT("dve", t2[:, :nt], ps[brot][:, :nt], sinT[:, pos0:pos0 + nt], ALU.mult, r=[("ps", brot), "sin"], w=[("t2", c2)])
                TT("pool", ko[:, :nt], t1[:, :nt], t2[:, :nt], ALU.add, r=[("t1", c2), ("t2", c2)], w=[("ko", c2)])
                DMA("sp", dst, ko[:, :nt], r=[("ko", c2)], slot=f"ko{c2}")
            else:
                DMA("sp", dst, kn[:, :nt], r=[("kn", c2)], slot=f"kn{c2}")

        pc = [0]

        def PRE(b):
            isctx = b == 8
            ntile = 2 if isctx else 4
            hT = hTb[b % 2]
            hk = ("hT", b % 2)
            for t in range(ntile):
                i2 = t % 2
                src = ctx_d[t * 128:(t + 1) * 128, :] if isctx else xloc_d[b * 512 + t * 128: b * 512 + (t + 1) * 128, :]
                DMA("sp", xb[i2][:], src, w=[("xb", i2)], slot=f"xb{i2}")
                G = CMOD[:, 1, :] if isctx else MOD[:, 1, :]
                SHh = CMOD[:, 0, :] if isctx else MOD[:, 0, :]
                modulate(xb[i2][:], ("xb", i2), G, SHh, hb[i2][:], ("hb", i2), stt[i2], ("st", i2),
                         tmpb[i2][:], ("tmp", i2), junk[:], "junk")
                pT = psb[t % 2]
                for k in range(8):
                    TR(pT[:, k * 128:(k + 1) * 128], hb[i2][:, k * 128:(k + 1) * 128], identb[:],
                       r=[("hb", i2), "identb"], w=[("ps", t % 2)])
                CP("act", hT[:, :, t * 128:(t + 1) * 128], pT[:, 0:1024].rearrange("p (k n) -> p k n", k=8),
                   r=[("ps", t % 2)], w=[hk])

        def MAIN(b):
            isctx = b == 8
            ntile = 2 if isctx else 4
            nt = ntile * 128
            hT = hTb[b % 2]
            hk = ("hT", b % 2)
            for h in range(6):
                for which in ((0, 1) if b < 4 else (1,)):
                    bank = 2 + pc[0] % 2
                    pc[0] += 1
                    col0 = (768 if which == 1 else 0) + h * 128
                    for k in range(8):
                        MM(ps[bank][:, :nt], w[:, k, col0:col0 + 128], hT[:, k, :nt], k == 0, k == 7,
                           r=["w", hk], w=[("ps", bank)])
                    dst = (kTd if which == 1 else qTd)[h, :, b * 512:b * 512 + nt]
                    normrope(ps[bank], ("ps", bank), nt, qk[:, which:which + 1], not isctx, b * 512 if not isctx else 0, dst)
            for t in range(ntile):
                vt = v1t[t % 2]
                for half in range(2):
                    bank = 2 + pc[0] % 2
                    pc[0] += 1
                    for k in range(8):
                        MM(ps[bank][:, 0:384], hT[:, k, t * 128:(t + 1) * 128],
                           w[:, k, 1536 + half * 384:1536 + (half + 1) * 384], k == 0, k == 7, r=["w", hk],
                           w=[("ps", bank)])
                    CP("act" if half else "dve", vt[:, half * 3:(half + 1) * 3, 0:128],
                       ps[bank][:, 0:384].rearrange("p (a b) -> p a b", a=3), r=[("ps", bank)], w=[("v1t", t % 2)])
                row0 = b * 512 + t * 128
                DMA("sp", V1d[row0:row0 + 128, :], vt.rearrange("p a b -> p (a b)"), r=[("v1t", t % 2)], slot=f"v1t{t % 2}")
            if not isctx:
                for c in range(2):
                    bank = 2 + pc[0] % 2
                    pc[0] += 1
                    for k in range(8):
                        MM(ps[bank][:, :], w[:, k, 2304 + c * 128:2304 + (c + 1) * 128], hT[:, k, :], k == 0, k == 7,
                           r=["w", hk], w=[("ps", bank)])
                    CP("act", fT[:, c, :], ps[bank][:, :], r=[("ps", bank)], w=["fT"])
                for t in range(4):
                    bank = 2 + pc[0] % 2
                    pc[0] += 1
                    for c in range(2):
                        MM(ps[bank][:, :], fT[:, c, t * 128:(t + 1) * 128], CS[:, c, :], c == 0, c == 1, r=["fT", "CS"],
                           w=[("ps", bank)])
                    CP("dve", fcst[t % 2][:], ps[bank][:, :], r=[("ps", bank)], w=[("fcst", t % 2)])
                    row0 = b * 512 + t * 128
                    DMA("sp", fcsd[row0:row0 + 128, :], fcst[t % 2][:], r=[("fcst", t % 2)], slot=f"fc{t % 2}")

        PRE(0)
        for b in range(9):
            if b + 1 < 9:
                PRE(b + 1)
            MAIN(b)
        S.barrier()

    def attn():
        S.reset_phase()
        kT = S.alloc([128, 6, NKEY], BF16)
        V1 = S.alloc([128, NKT, 774], BF16)
        lamt = S.alloc([128, 256], F32)
        SUB = S.alloc([128, 128], F32)
        qb = [S.alloc([128, 2, 6, 512], BF16) for _ in range(2)]
        ob = [S.alloc([128, 6, 512], BF16) for _ in range(2)]
        for i in range(2):
            MSET("pool", qb[i].rearrange("p a b c -> p (a b c)"), 0.0, w=[("qT", i)])
        a1 = S.alloc([128, 128], F32)
        att = S.alloc([128, 128], F32)
        attb4 = S.alloc([128, 4, 128], BF16)
        junk = S.alloc([128, 128], BF16)
        sm = S.alloc([128, 16], F32)
        accS = S.alloc([128, 3, 387], F32)
        for h in range(6):
            DMA("sp", kT[:, h, :], kTd[h], w=["kT"], slot=f"kT{h % 2}")
        v1v = V1d.rearrange("(t p) c -> p t c", p=128)
        for q in range(2):
            DMA("sp", V1[:, q * 17:(q + 1) * 17, :], v1v[:, q * 17:(q + 1) * 17, :], w=["V1"], slot=f"V1{q}")
        DMA("sp", lamt[:], lam_d, w=["lamt"], slot="lam")
        DMA("sp", SUB[:], sub_d, w=["SUB"], slot="sub")
        TS("dve", SUB[:], SUB[:], 0.8, ALU.mult, r=["SUB"], w=["SUB"])
        TT("dve", lamt[:, 0:64], lamt[:, 0:64], lamt[:, 64:128], ALU.mult, r=["lamt"], w=["lamt"])
        TT("dve", lamt[:, 128:192], lamt[:, 128:192], lamt[:, 192:256], ALU.mult, r=["lamt"], w=["lamt"])
        S.op("dve", lambda e: e.reduce_sum(out=sm[:, 0:1], in_=lamt[:, 0:64], axis=AX.X), r=["lamt"], w=["sm"])
        S.op("dve", lambda e: e.reduce_sum(out=sm[:, 1:2], in_=lamt[:, 128:192], axis=AX.X), r=["lamt"], w=["sm"])
        ACT(sm[:, 0:2], sm[:, 0:2], AF.Exp, r=["sm"], w=["sm"])
        TT("dve", sm[:, 2:3], sm[:, 1:2], sm[:, 0:1], ALU.subtract, r=["sm"], w=["sm"])
        TS("dve", sm[:, 2:3], sm[:, 2:3], -0.2, ALU.add, r=["sm"], w=["sm"])
        nlam = sm[:, 2:3]
        accs = {}
        for m in range(2):
            for qs in range(4):
                j = m * 4 + qs
                accs[(m, qs)] = ps[4 + j // 3][:, (j % 3) * 129:(j % 3) * 129 + 129]
        PT2 = [S.alloc([128, 1024], BF16) for _ in range(2)]
        steps = [(h, m, kt) for h in range(6) for m in range(2) for kt in range(NKT)]
        npair = len(steps) // 2
        for g in range(4):
            qT = qb[g % 2]
            for m_ in range(2):
                DMA("sp", qT[m_ * 64:(m_ + 1) * 64, m_, :, :],
                    qTd[:, m_ * 64:(m_ + 1) * 64, g * 512:(g + 1) * 512].rearrange("h p n -> p h n"), r=[("qT", g % 2)],
                    w=[("qT", g % 2)], slot=f"qT{g % 2}")
            oT = ob[g % 2]

            def QKEXP(p):
                pb = p % 2
                for u in range(2):
                    h, m, kt = steps[2 * p + u]
                    MM(pbig[pb][:, u * 512:(u + 1) * 512], kT[:, h, kt * 128:(kt + 1) * 128],
                       qT[:, m, h, :], True, True, r=["kT", ("qT", g % 2)], w=[("pq", pb)])
                ACT(PT2[pb][:], pbig[pb][:, :], AF.Exp, r=[("pq", pb)], w=[("PT", pb)])

            def AV(p):
                pb = p % 2
                for u in range(2):
                    h, m, kt = steps[2 * p + u]
                    for qs in range(4):
                        st_ = (kt == 0 and (m * 4 + qs) % 3 == 0)
                        S.op("pe", lambda e, o_=accs[(m, qs)], l_=PT2[pb][:, u * 512 + qs * 128:u * 512 + (qs + 1) * 128],
                             r_=V1[:, kt, h * 129:(h + 1) * 129], st_=st_, sp_=(kt == NKT - 1):
                             e.matmul(o_, l_, r_, start=st_, stop=sp_, skip_group_check=True),
                             r=[("PT", pb), "V1"], w=[("ps", 4 + (m * 4 + qs) // 3)])

            def EPI(h):
                for bk in range(3):
                    wdt = 387 if bk < 2 else 258
                    CP("act", accS[:, bk, 0:wdt], ps[4 + bk][:, 0:wdt], r=[("ps", 4 + bk)], w=["accS"])
                for qs in range(4):
                    j0, j1 = qs, 4 + qs
                    A0 = accS[:, j0 // 3, (j0 % 3) * 129:(j0 % 3) * 129 + 129]
                    A1 = accS[:, j1 // 3, (j1 % 3) * 129:(j1 % 3) * 129 + 129]
                    RECIP(sm[:, 4:5], A0[:, 128:129], r=["accS"], w=["sm4"])
                    RECIP(sm[:, 5:6], A1[:, 128:129], r=["accS"], w=["sm5"])
                    TT("dve", sm[:, 6:7], sm[:, 5:6], nlam, ALU.mult, r=["sm5", "sm"], w=["sm6"])
                    TS("dve", a1[:], A0[:, 0:128], sm[:, 4:5], ALU.mult, r=["accS", "sm4"], w=["a1"])
                    STT("dve", att[:], A1[:, 0:128], sm[:, 6:7], a1[:], ALU.mult, ALU.add, r=["accS", "sm6", "a1"],
                        w=["att"])
                    ACT(junk[:], att[:], AF.Square, r=["att"], w=["junkA", "sm7"], accum_out=sm[:, 7:8])
                    ACT(sm[:, 8:9], sm[:, 7:8], AF.Sqrt, r=["sm7"], w=["sm8"], bias=EPS, scale=1.0 / 128.0)
                    RECIP(sm[:, 9:10], sm[:, 8:9], r=["sm8"], w=["sm9"])
                    STT("dve", attb4[:, qs, :], att[:], sm[:, 9:10], SUB[:], ALU.mult, ALU.mult, r=["att", "sm9", "SUB"],
                        w=["attb"])

            def EPI_T(h):
                for qs in range(4):
                    TR(psb[7][:, qs * 128:(qs + 1) * 128], attb4[:, qs, :], identb[:], r=["attb", "identb"], w=[("ps", 7)])
                CP("act", oT[:, h, :], psb[7][:, 0:512], r=[("ps", 7)], w=[("oT", g % 2)])

            QKEXP(0)
            pend = None
            for p in range(npair):
                if p + 1 < npair:
                    QKEXP(p + 1)
                AV(p)
                if pend is not None and p == pend[1]:
                    EPI_T(pend[0])
                    pend = None
                hh, mm, kk = steps[2 * p + 1]
                if mm == 1 and kk == NKT - 1:
                    EPI(hh)
                    pend = (hh, p + 6)
                    if p == npair - 1:
                        EPI_T(hh)
                        pend = None
            DMA("sp", oThv[0:6, :, g * 512:(g + 1) * 512].rearrange("c p n -> p c n"), oT[:], r=[("oT", g % 2)], slot=f"oT{g % 2}")
        S.barrier()

    def fourier():
        S.reset_phase()
        fcs = S.alloc([128, 32, 512], BF16)
        cb = [S.alloc([128, 16, 512], BF16) for _ in range(2)]
        sb = [S.alloc([128, 16, 512], BF16) for _ in range(2)]
        oF = [S.alloc([128, 2, 512], BF16) for _ in range(2)]
        DMA("sp", fcs[:], fcsd.rearrange("(t p) c -> p t c", p=128), w=["fcs"], slot="fcs")
        cv = cos_d.rearrange("(t p) k -> p t k", p=128)
        sv = sin_d.rearrange("(t p) k -> p t k", p=128)
        n = 0
        for g in range(4):
            for half in range(2):
                cbb, sbb = cb[n % 2], sb[n % 2]
                DMA("sp", cbb[:], cv[:, half * 16:(half + 1) * 16, g * 512:(g + 1) * 512], w=[("cb", n % 2)], slot=f"cb{n % 2}")
                DMA("act", sbb[:], sv[:, half * 16:(half + 1) * 16, g * 512:(g + 1) * 512], w=[("sb", n % 2)], slot=f"sb{n % 2}")
                for c in range(2):
                    for t in range(16):
                        tt = half * 16 + t
                        MM(ps[c][:, :], fcs[:, tt, c * 128:(c + 1) * 128], cbb[:, t, :], tt == 0, False,
                           r=["fcs", ("cb", n % 2)], w=[("ps", c)])
                        MM(ps[c][:, :], fcs[:, tt, 256 + c * 128:256 + (c + 1) * 128], sbb[:, t, :], False, tt == 31,
                           r=["fcs", ("sb", n % 2)], w=[("ps", c)])
                n += 1
            for c in range(2):
                CP("dve" if c else "act", oF[g % 2][:, c, :], ps[c][:, :], r=[("ps", c)], w=[("oF", g % 2)])
            DMA("sp", oThv[6:8, :, g * 512:(g + 1) * 512].rearrange("c p n -> p c n"), oF[g % 2][:], r=[("oF", g % 2)],
                slot=f"oF{g % 2}")
        S.barrier()
        for c_ in range(8):
            CC("AllGather", ALU.bypass, oTh[c_ * 128:(c_ + 1) * 128, :], oTg[c_ * 256:(c_ + 1) * 256, :], [], [], "ccg")
        S.barrier()

    def outproj(l, wsrc, xin, xout, gathered):
        S.reset_phase()
        wo = S.alloc([128, 8, 1024], BF16)
        wr = S.alloc([128, 8, 16], F32)
        ob = [S.alloc([128, 8, 128], BF16) for _ in range(2)]
        xb = [S.alloc([128, 1024], F32) for _ in range(2)]
        yb = [S.alloc([128, 1024], F32) for _ in range(2)]
        x1b = [S.alloc([128, 1024], F32) for _ in range(2)]
        tmpb = [S.alloc([128, 1024], F32) for _ in range(2)]
        hfb = [S.alloc([128, 1024], F32) for _ in range(2)]
        hbb = [S.alloc([128, 1024], BF16) for _ in range(2)]
        hfT = [S.alloc([128, 8, 128], F32) for _ in range(2)]
        junk = S.alloc([128, 1024], BF16)
        stt = [S.alloc([128, 8], F32) for _ in range(2)]
        ex = S.alloc([128, 16], F32)
        wv = wsrc.rearrange("(k p) n -> p k n", p=128)
        for q in range(2):
            DMA("pool", wo[:, :, q * 512:(q + 1) * 512], wv[:, :, q * 512:(q + 1) * 512], w=["wo"], slot=f"wo{q}")
        DMA("sp", wr[:], rw_d[l].rearrange("(k p) e -> p k e", p=128), w=["wr"], slot="wr")
        def STA(i):
                i2 = i % 2
                osrc = (oTgv[i // 16][:, :, (i % 16) * 128:(i % 16 + 1) * 128] if gathered else oTd[:, :, i * 128:(i + 1) * 128])
                DMA("sp", ob[i2][:], osrc.rearrange("c p n -> p c n"), w=[("ob", i2)], slot=f"ob{i2}")
                DMA("sp", xb[i2][:], xin[i * 128:(i + 1) * 128, :], w=[("xb", i2)], slot=f"xb{i2}")
                for half in range(2):
                    for c in range(8):
                        MM(ps[half][:, :], ob[i2][:, c, :], wo[:, c, half * 512:(half + 1) * 512], c == 0, c == 7,
                           r=[("ob", i2), "wo"], w=[("ps", half)])
                    TT("dve", yb[i2][:, half * 512:(half + 1) * 512], ps[half][:, :], MOD[:, 2, half * 512:(half + 1) * 512],
                       ALU.mult, r=[("ps", half), "MOD"], w=[("yb", i2)])
                TT("pool", x1b[i2][:], yb[i2][:], xb[i2][:], ALU.add, r=[("yb", i2), ("xb", i2)], w=[("x1b", i2)])
                DMA("sp", xout[i * 128:(i + 1) * 128, :], x1b[i2][:], r=[("x1b", i2)], slot=f"x1o{i2}")
                modulate(x1b[i2][:], ("x1b", i2), MOD[:, 4, :], MOD[:, 3, :], hfb[i2][:], ("hfb", i2), stt[i2], ("st", i2),
                         tmpb[i2][:], ("tmp", i2), junk[:], "junk")
                CP("act", hbb[i2][:], hfb[i2][:], r=[("hfb", i2)], w=[("hbb", i2)])
                DMA("sp", hfd[i * 128:(i + 1) * 128, :], hbb[i2][:], r=[("hbb", i2)], slot=f"hfo{i2}")

        def STB(i):
                i2 = i % 2
                for k in range(8):
                    bank = 2 + k // 4
                    TR(ps[bank][:, (k % 4) * 128:(k % 4 + 1) * 128], hfb[i2][:, k * 128:(k + 1) * 128], identf[:],
                       r=[("hfb", i2), "identf"], w=[("ps", bank)])
                CP("act", hfT[i2][:, 0:4, :], ps[2][:, :].rearrange("p (k n) -> p k n", k=4), r=[("ps", 2)], w=[("hfT", i2)])
                CP("dve", hfT[i2][:, 4:8, :], ps[3][:, :].rearrange("p (k n) -> p k n", k=4), r=[("ps", 3)], w=[("hfT", i2)])
                for k in range(8):
                    MM(ps[4][:, 0:16], hfT[i2][:, k, :], wr[:, k, :], k == 0, k == 7, r=[("hfT", i2), "wr"], w=[("ps", 4)])
                st = stt[i2]
                S.op("dve", lambda e, st=st: e.reduce_max(out=st[:, 4:5], in_=ps[4][:, 0:16], axis=AX.X), r=[("ps", 4)],
                     w=[("st5", i2)])
                TS("dve", st[:, 5:6], st[:, 4:5], -1.0, ALU.mult, r=[("st5", i2)], w=[("st6", i2)])
                ACT(ex[:], ps[4][:, 0:16], AF.Exp, r=[("ps", 4), ("st6", i2)], w=["ex", ("st7", i2)], bias=st[:, 5:6], scale=1.0,
                    accum_out=st[:, 6:7])
                RECIP(st[:, 7:8], st[:, 6:7], r=[("st7", i2)], w=[("st8", i2)])
                TS("dve", AFF[:, i, :], ex[:], st[:, 7:8], ALU.mult, r=["ex", ("st8", i2)], w=["AFF"])

        STA(0)
        for i in range(NT):
            if i + 1 < NT:
                STA(i + 1)
            STB(i)
        S.barrier()

    def moe(l):
        S.reset_phase()
        slotm = S.alloc([128, 32, 16], F32)
        VALS = S.alloc([128, 32, 16, 5], BF16)
        VA = S.alloc([128, 512, 20], BF16)
        mark = S.sb_off
        affT = S.alloc([16, T], F32)
        junkb = S.alloc([16, T], BF16)
        maskT = S.alloc([16, T], BF16)
        bs = S.alloc([16, 8], F32)
        MASK = S.alloc([128, 512], F32)
        MASKb = S.alloc([128, 512], BF16)
        tots = S.alloc([128, 32, 16], F32)
        base = S.alloc([128, 32, 16], F32)
        r1 = S.alloc([128, 512], F32)
        gtmp = S.alloc([128, 512], BF16)
        zt = S.alloc([128, 2, 1024], F32)
        bm = S.alloc([128, 512], F32)
        ag = S.alloc([128, 512], F32)
        aoh = S.alloc([128, 512], BF16)
        AFFf = AFF.rearrange("p a b -> p (a b)")
        MSET("pool", zt[:], 0.0, w=["zt"])
        yv = ymoe.rearrange("(t p) d -> p t d", p=128)
        for q in range(16):
            DMA("sp", yv[:, q * 2:(q + 1) * 2, :], zt[:], r=["zt"], w=[("ymoe", 0)], slot="zy")
        for i in range(NT):
            bank = i // 4 % 2
            TR(ps[bank][0:16, (i % 4) * 128:(i % 4 + 1) * 128], AFF[:, i, :], identf[:], r=["AFF", "identf"], w=[("ps", bank)])
            if i % 4 == 3:
                CP("act", affT[:, (i // 4) * 512:(i // 4 + 1) * 512], ps[bank][0:16, :], r=[("ps", bank)], w=["affT"])
        MSET("dve", bs[:], 0.0, w=["bs"])
        for it in range(28):
            step = 2.0 ** -(it + 1)
            TS("dve", bs[:, 1:2], bs[:, 0:1], step, ALU.add, r=["bs"], w=["bs1"])
            TS("dve", junkb[:], affT[:], bs[:, 1:2], ALU.is_gt, 0.0, ALU.add, r=["affT", "bs1"], w=["junkb", "bs2"],
               accum=bs[:, 2:3])
            TS("dve", bs[:, 3:4], bs[:, 2:3], 511.5, ALU.is_gt, step, ALU.mult, r=["bs2"], w=["bs3"])
            TT("dve", bs[:, 0:1], bs[:, 0:1], bs[:, 3:4], ALU.add, r=["bs", "bs3"], w=["bs"])
        TS("dve", maskT[:], affT[:], bs[:, 0:1], ALU.is_gt, r=["affT", "bs"], w=["maskT"])
        for i in range(NT):
            TR(psb[2][:, i * 16:(i + 1) * 16], maskT[:, i * 128:(i + 1) * 128], identb[0:16, 0:16], r=["maskT", "identb"],
               w=[("ps", 2)])
        CP("act", MASKb[:], psb[2][:, 0:512], r=[("ps", 2)], w=["MASKb"])
        CP("dve", MASK[:], MASKb[:], r=["MASKb"], w=["MASK"])
        MM(ps[3][:, :], Ust[:], MASKb[:], True, True, r=["Ust", "MASKb"], w=[("ps", 3)])
        MM(ps[4][:, :], onesb[:], MASKb[:], True, True, r=["onesb", "MASKb"], w=[("ps", 4)])
        CP("act", tots.rearrange("p a b -> p (a b)"), ps[4][:, :], r=[("ps", 4)], w=["tots"])
        MSET("dve", base[:, 0, :], 0.0, w=["base"])
        for i in range(1, NT):
            TT("dve", base[:, i, :], base[:, i - 1, :], tots[:, i - 1, :], ALU.add, r=["base", "tots"], w=["base"])
        sf = slotm.rearrange("p a b -> p (a b)")
        TT("dve", sf, ps[3][:, :], base.rearrange("p a b -> p (a b)"), ALU.add, r=[("ps", 3), "base"], w=["slotm"])
        TS("dve", ag[:], sf, 128.0, ALU.is_ge, r=["slotm"], w=["ag"])
        STT("dve", ag[:], sf, 256.0, ag[:], ALU.is_ge, ALU.add, r=["slotm", "ag"], w=["ag"])
        STT("dve", ag[:], sf, 384.0, ag[:], ALU.is_ge, ALU.add, r=["slotm", "ag"], w=["ag"])
        STT("dve", bm[:], ag[:], -128.0, sf, ALU.mult, ALU.add, r=["slotm", "ag"], w=["bm"])
        STT("dve", sf, bm[:], 1.0, MASK[:], ALU.add, ALU.mult, r=["bm", "MASK"], w=["slotm"])
        TS("dve", sf, sf, -1.0, ALU.add, r=["slotm"], w=["slotm"])
        CP("pool", VALS[:, :, :, 0:2], tokhl[:], r=["tokhl"], w=["VALS"])
        Vg = lambda j: VALS[:, :, :, j].rearrange("p a b -> p (a b)")
        CP("dve", gtmp[:], AFFf, r=["AFF"], w=["gtmp"])
        CP("dve", VALS[:, :, :, 2], gtmp.rearrange("p (a b) -> p a b", a=32), r=["gtmp"], w=["VALS"])
        TT("dve", r1[:], AFFf, gtmp[:], ALU.subtract, r=["AFF", "gtmp"], w=["r1"])
        CP("dve", gtmp[:], r1[:], r=["r1"], w=["gtmp"])
        CP("dve", VALS[:, :, :, 3], gtmp.rearrange("p (a b) -> p a b", a=32), r=["gtmp"], w=["VALS"])
        TT("dve", r1[:], r1[:], gtmp[:], ALU.subtract, r=["r1", "gtmp"], w=["r1"])
        CP("dve", VALS[:, :, :, 4], r1.rearrange("p (a b) -> p a b", a=32), r=["r1"], w=["VALS"])
        VALf = VALS.rearrange("p a b c -> p (a b) c")
        for a_ in range(4):
            TS("dve", aoh[:], ag[:], float(a_), ALU.is_equal, r=["ag"], w=["aoh"])
            for v_ in range(5):
                TT("dve", VA[:, :, v_ * 4 + a_], VALf[:, :, v_], aoh[:], ALU.mult, r=["VALS", "aoh"], w=["VA"])

        S.barrier()
        S.sb_off = mark
        wgb = [S.alloc([128, 8, 1024], BF16) for _ in range(2)]
        wub = [S.alloc([128, 8, 1024], BF16) for _ in range(2)]
        wdb = [S.alloc([128, 8, 1024], BF16) for _ in range(2)]
        selb = [S.alloc([128, 128], BF16) for _ in range(4)]
        ivt = S.alloc([128, 5, 4], F32)
        tokf = S.alloc([128, 4], F32)
        idx = [S.alloc([128, 4], I32) for _ in range(2)]
        gg = [S.alloc([128, 4], F32) for _ in range(2)]
        xs = [S.alloc([128, 1024], BF16) for _ in range(4)]
        xsT = S.alloc([128, 8, 512], BF16)
        sg = [S.alloc([128, 512], F32) for _ in range(2)]
        aT = S.alloc([128, 8, 512], BF16)
        ysb = [S.alloc([128, 1024], F32) for _ in range(2)]
        yc = [0]

        def W(ex_):
            e2 = ex_ % 2
            for (buf, src, nm) in ((wgb, wg_d, "wg"), (wub, wu_d, "wu"), (wdb, wd_d, "wd")):
                sv = src[l, ex_].rearrange("(k p) n -> p k n", p=128)
                for q in range(2):
                    DMA("pool", buf[e2][:, :, q * 512:(q + 1) * 512], sv[:, :, q * 512:(q + 1) * 512], w=[(nm, e2)],
                        slot=f"{nm}{e2}{q}")

        def IDX(ex_):
            e2 = ex_ % 2
            for i in range(NT):
                sb_ = selb[i % 4]
                TS("dve", sb_[:], iota[:, 0:128], slotm[:, i, ex_:ex_ + 1], ALU.is_equal,
                   r=["iota", "slotm"], w=[("selb", i % 4)])
                MM(ps[5][:, 0:20], sb_[:], VA[:, i * 16 + ex_, :], i == 0, i == NT - 1, r=["VA", ("selb", i % 4)], w=[("ps", 5)])
            CP("dve", ivt.rearrange("p a b -> p (a b)"), ps[5][:, 0:20], r=[("ps", 5)], w=["ivt"])
            STT("dve", tokf[:], ivt[:, 0, :], 64.0, ivt[:, 1, :], ALU.mult, ALU.add, r=["ivt"], w=["tokf"])
            CP("dve", idx[e2][:], tokf[:], r=["tokf"], w=[("idx", e2)])
            TT("dve", gg[e2][:], ivt[:, 2, :], ivt[:, 3, :], ALU.add, r=["ivt"], w=[("gg", e2)])
            TT("dve", gg[e2][:], gg[e2][:], ivt[:, 4, :], ALU.add, r=["ivt", ("gg", e2)], w=[("gg", e2)])

        def G(ex_):
            e2 = ex_ % 2
            for grp in range(4):
                S.op("pool", lambda e, grp=grp, e2=e2: e.indirect_dma_start(
                    out=xs[grp][:], out_offset=None, in_=hfd,
                    in_offset=bass.IndirectOffsetOnAxis(ap=idx[e2][:, grp:grp + 1], axis=0)),
                    r=[("idx", e2)], w=[("xs", grp)], dma=f"xs{grp}")

        def XT(ex_):
            for grp in range(4):
                pT = psb[6 + grp % 2]
                for k in range(8):
                    TR(pT[:, k * 128:(k + 1) * 128], xs[grp][:, k * 128:(k + 1) * 128], identb[:], r=[("xs", grp), "identb"],
                       w=[("ps", 6 + grp % 2)])
                CP("act" if grp % 2 else "dve", xsT[:, :, grp * 128:(grp + 1) * 128],
                   pT[:, 0:1024].rearrange("p (k n) -> p k n", k=8), r=[("ps", 6 + grp % 2)], w=["xsT"])

        def FFN(ex_):
            e2 = ex_ % 2
            for fc in range(8):
                for k in range(8):
                    MM(ps[0][:, :], wgb[e2][:, k, fc * 128:(fc + 1) * 128], xsT[:, k, :], k == 0, k == 7, r=[("wg", e2), "xsT"],
                       w=[("ps", 0)])
                for k in range(8):
                    MM(ps[1][:, :], wub[e2][:, k, fc * 128:(fc + 1) * 128], xsT[:, k, :], k == 0, k == 7, r=[("wu", e2), "xsT"],
                       w=[("ps", 1)])
                ACT(sg[fc % 2][:], ps[0][:, :], AF.Silu, r=[("ps", 0)], w=[("sg", fc % 2)])
                TT("dve", aT[:, fc, :], sg[fc % 2][:], ps[1][:, :], ALU.mult, r=[("sg", fc % 2), ("ps", 1)], w=["aT"])
            for grp in range(4):
                y2 = yc[0] % 2
                yc[0] += 1
                for half in range(2):
                    bank = 2 + half
                    for fc in range(8):
                        MM(ps[bank][:, :], aT[:, fc, grp * 128:(grp + 1) * 128], wdb[e2][:, fc, half * 512:(half + 1) * 512],
                           fc == 0, fc == 7, r=["aT", ("wd", e2)], w=[("ps", bank)])
                    if half:
                        ACT(ysb[y2][:, 512:1024], ps[bank][:, :], AF.Copy, r=[("ps", bank), ("gg", e2)], w=[("ysb", y2)],
                            scale=gg[e2][:, grp:grp + 1])
                    else:
                        TS("dve", ysb[y2][:, 0:512], ps[bank][:, :], gg[e2][:, grp:grp + 1], ALU.mult,
                           r=[("ps", bank), ("gg", e2)], w=[("ysb", y2)])
                S.op("pool", lambda e, grp=grp, e2=e2, y2=y2: e.indirect_dma_start(
                    out=ymoe, out_offset=bass.IndirectOffsetOnAxis(ap=idx[e2][:, grp:grp + 1], axis=0),
                    in_=ysb[y2][:], in_offset=None, compute_op=ALU.add),
                    r=[("idx", e2), ("ysb", y2), ("ymoe", ex_)], w=[("ymoe", ex_ + 1), ("ymoeg", grp)], dma=f"ys{y2}")

        NEXP = 8
        IDX(0)
        G(0)
        W(0)
        W(1)
        for ex_ in range(NEXP):
            if ex_ + 1 < NEXP:
                IDX(ex_ + 1)
            XT(ex_)
            if ex_ + 1 < NEXP:
                G(ex_ + 1)
            FFN(ex_)
            if ex_ + 2 < NEXP:
                W(ex_ + 2)
        S.barrier()
        for c_ in range(4):
            CC("AllReduce", ALU.add, ymoe[c_ * 1024:(c_ + 1) * 1024, :], ysum[c_ * 1024:(c_ + 1) * 1024, :], [], [], "ccr")
        S.barrier()

    def combine(src, dst):
        S.reset_phase()
        xb = [S.alloc([128, 1024], F32) for _ in range(2)]
        yb = [S.alloc([128, 1024], F32) for _ in range(2)]
        ob = [S.alloc([128, 1024], F32) for _ in range(2)]
        for i in range(NT):
            i2 = i % 2
            DMA("sp", xb[i2][:], src[i * 128:(i + 1) * 128, :], w=[("xb", i2)], slot=f"cx{i2}")
            DMA("act", yb[i2][:], ysum[i * 128:(i + 1) * 128, :], w=[("yb", i2)], slot=f"cy{i2}")
            TT("dve", yb[i2][:], yb[i2][:], MOD[:, 5, :], ALU.mult, r=[("yb", i2), "MOD"], w=[("yb", i2)])
            TT("pool", ob[i2][:], yb[i2][:], xb[i2][:], ALU.add, r=[("yb", i2), ("xb", i2)], w=[("ob", i2)])
            DMA("sp", dst[i * 128:(i + 1) * 128, :], ob[i2][:], r=[("ob", i2)], slot=f"co{i2}")
        S.barrier()

    def conv():
        S.reset_phase()
        hT = S.alloc([128, 8, T], BF16)
        xb = [S.alloc([128, 1024], F32) for _ in range(2)]
        tmpb = [S.alloc([128, 1024], F32) for _ in range(2)]
        hb = [S.alloc([128, 1024], BF16) for _ in range(2)]
        junk = S.alloc([128, 1024], BF16)
        stt = [S.alloc([128, 4], F32) for _ in range(2)]
        cw = S.alloc([128, 8, 3], F32)
        w3 = [S.alloc([128, 8, 3, 128], BF16) for _ in range(2)]
        z = S.alloc([128, T + 2], F32)
        tt_ = S.alloc([128, T], F32)
        bgs = S.alloc([128, T], BF16)
        cgs = [S.alloc([128, 512], F32) for _ in range(2)]
        vT = [S.alloc([128, T], BF16) for _ in range(2)]
        DMA("sp", cw[:], convw_d, w=["cw"], slot="cw")
        MSET("pool", z[:], 0.0, w=["z"])
        for i in range(NT):
            i2 = i % 2
            DMA("sp", xb[i2][:], x2d[i * 128:(i + 1) * 128, :], w=[("xb", i2)], slot=f"xb{i2}")
            modulate(xb[i2][:], ("xb", i2), MOD[:, 1, :], MOD[:, 0, :], hb[i2][:], ("hb", i2), stt[i2], ("st", i2),
                     tmpb[i2][:], ("tmp", i2), junk[:], "junk")
            pT = psb[i % 2]
            for k in range(8):
                TR(pT[:, k * 128:(k + 1) * 128], hb[i2][:, k * 128:(k + 1) * 128], identb[:], r=[("hb", i2), "identb"],
                   w=[("ps", i % 2)])
            CP("act", hT[:, :, i * 128:(i + 1) * 128], pT[:, 0:1024].rearrange("p (k n) -> p k n", k=8), r=[("ps", i % 2)],
               w=["hT"])
        wv = cwin_d.rearrange("(k p) n -> p k n", p=128)
        for fc in range(8):
            f2 = fc % 2
            for j in range(3):
                DMA("pool", w3[f2][:, :, j, :], wv[:, :, j * 1024 + fc * 128:j * 1024 + (fc + 1) * 128], w=[("w3", f2)],
                    slot=f"w3{f2}{j}")
            for b in range(8):
                for j in range(3):
                    bank = 2 + j * 2 + b % 2
                    for k in range(8):
                        MM(ps[bank][:, :], w3[f2][:, k, j, :], hT[:, k, b * 512:(b + 1) * 512], k == 0, k == 7,
                           r=[("w3", f2), "hT"], w=[("ps", bank)])
                CP("act", bgs[:, b * 512:(b + 1) * 512], ps[2 + b % 2][:, :], r=[("ps", 2 + b % 2)], w=["bgs"])
                CP("act", cgs[b % 2][:], ps[4 + b % 2][:, :], r=[("ps", 4 + b % 2)], w=[("cgs", b % 2)])
                TT("dve", z[:, 1 + b * 512:1 + (b + 1) * 512], cgs[b % 2][:], ps[6 + b % 2][:, :], ALU.mult,
                   r=[("cgs", b % 2), ("ps", 6 + b % 2)], w=["z"])
            ACT(tt_[:], z[:, 0:T], AF.Copy, r=["z", "cw"], w=["tt"], scale=cw[:, fc, 0:1])
            STT("dve", tt_[:], z[:, 1:T + 1], cw[:, fc, 1:2], tt_[:], ALU.mult, ALU.add, r=["z", "cw", "tt"], w=["tt"])
            STT("dve", tt_[:], z[:, 2:T + 2], cw[:, fc, 2:3], tt_[:], ALU.mult, ALU.add, r=["z", "cw", "tt"], w=["tt"])
            TT("pool", vT[f2][:], bgs[:], tt_[:], ALU.mult, r=["bgs", "tt"], w=[("vT", f2)])
            DMA("sp", oTd[fc], vT[f2][:], r=[("vT", f2)], slot=f"vT{f2}")
        S.barrier()

    phases = [
        lambda: ada(0, True),
        proj0,
        attn,
        fourier,
        lambda: outproj(0, wout_d, x_d, x1d, True),
        lambda: moe(0),
        lambda: combine(x1d, x2d),
        lambda: ada(1, False),
        conv,
        lambda: outproj(1, cwout_d, x2d, x3d, False),
        lambda: moe(1),
        lambda: combine(x3d, out_d),
    ]
    for i, ph in enumerate(phases):
        if stop is not None and i >= stop:
            break
        ph()
    S.emit()
    return nc, S


import ml_dtypes

_CACHE = {}


def _consts():
    if "c" in _CACHE:
        return _CACHE["c"]
    bf = ml_dtypes.bfloat16
    cpk = np.zeros((128, 1152), np.float32)
    cpk[:, 0:128] = np.eye(128)
    bo = np.zeros((128, 128), np.float32)
    bo[0:64, 0:64] = 1.0
    bo[64:128, 64:128] = 1.0
    cpk[:, 128:256] = bo
    R = np.zeros((64, 64), np.float32)
    for j in range(64):
        q = j // 16
        if q % 2 == 0:
            R[j, j + 16] = -1.0
        else:
            R[j, j - 16] = 1.0
    R2 = np.zeros((128, 128), np.float32)
    R2[0:64, 0:64] = R
    R2[64:128, 64:128] = R
    cpk[:, 256:384] = R2.T
    cpk[:, 384:512] = np.triu(np.ones((128, 128), np.float32), 1)
    cpk[:, 512:640] = 1.0
    cpk[:, 640:1152] = np.arange(512, dtype=np.float32)[None, :]
    tok = (np.arange(32)[None, :] * 128 + np.arange(128)[:, None])
    tokhl = np.zeros((128, 32, 16, 2), np.float32)
    tokhl[:, :, :, 0] = (tok // 64)[:, :, None]
    tokhl[:, :, :, 1] = (tok % 64)[:, :, None]
    sel = np.zeros((2, 256), np.float32)
    sel[0, 0:128] = 1.0
    sel[1, 128:256] = 1.0
    n = 4096
    r = np.repeat(np.arange(n // 64, dtype=np.float32), 64)
    col = np.tile(np.arange(64, dtype=np.float32), n // 64)
    inv = (np.float32(10000.0) ** (-np.arange(16, dtype=np.float32) / np.float32(16))).astype(np.float32)
    ar = r[:, None] * inv
    ac = col[:, None] * inv
    ang = np.concatenate([ar, ar, ac, ac], axis=-1).astype(np.float32)
    cosT = np.cos(ang).astype(np.float32).T
    sinT = np.sin(ang).astype(np.float32).T
    rope = np.stack([np.concatenate([cosT, cosT], 0), np.concatenate([sinT, sinT], 0)]).astype(np.float32)
    cm = np.arange(64)[:, None] * np.arange(64)[None, :]
    cb = np.cos(2 * np.pi * (cm % 64) / 64.0) / 512.0
    sb = np.sin(2 * np.pi * (cm % 64) / 64.0) / 512.0
    csd = np.zeros((256, 512), np.float32)
    for g in range(4):
        csd[g * 64:(g + 1) * 64, g * 64:(g + 1) * 64] = cb
        csd[g * 64:(g + 1) * 64, 256 + g * 64:256 + (g + 1) * 64] = sb
    kn = (np.arange(n, dtype=np.int64)[:, None] * np.arange(n, dtype=np.int64)[None, :]) % n
    angp = kn.astype(np.float64) * (2 * np.pi / n)
    cosd = np.cos(angp).astype(bf)
    sind = (-np.sin(angp)).astype(bf)
    perms = [np.arange(n), np.concatenate([np.arange(2048, 4096), np.arange(0, 2048)])]
    cos_r = [np.ascontiguousarray(cosd[p][:, p[:2048]]) for p in perms]
    sin_r = [np.ascontiguousarray(sind[p][:, p[:2048]]) for p in perms]
    rope_r = [np.ascontiguousarray(rope[:, :, p]) for p in perms]
    c = dict(cpk=cpk, tokhl=tokhl.reshape(128, 1024), sel=sel, csd=csd, perms=perms, cos_r=cos_r, sin_r=sin_r, rope_r=rope_r)
    _CACHE["c"] = c
    return c


def kernel(x, c, ctx, c_ctx, ada_w, ada_b, norm_mix, norm_ffn, attn_w_in, attn_q_norm, attn_k_norm,
           lam_q1, lam_k1, lam_q2, lam_k2, attn_subln, attn_w_out, conv_w_in, conv_w, conv_w_out,
           router_w, moe_w_gate, moe_w_up, moe_w_down, _stop=None, _debug=False, _ncores=8):
    f = lambda a: np.ascontiguousarray(np.asarray(a, dtype=np.float32))
    x, c, ctx, c_ctx = f(x), f(c), f(ctx), f(c_ctx)
    K = _consts()
    ck = ("nc", _stop, _debug, _ncores)
    if ck not in _CACHE:
        _CACHE[ck] = build_nc(_stop, _debug, _ncores)[0]
    nc = _CACHE[ck]
    shared = {k: K[k] for k in ("cpk", "tokhl", "sel", "csd")}
    shared["ada_w"] = f(ada_w)
    shared["adab2"] = f(np.repeat(f(ada_b)[:, None, :], 2, axis=1))
    shared["nmrep"] = f(np.repeat(f(norm_mix)[:, None, :], 128, axis=1))
    shared["nfrep"] = f(np.repeat(f(norm_ffn)[:, None, :], 128, axis=1))
    shared["attn_w_in"] = f(attn_w_in)[0]
    shared["attn_w_out"] = f(attn_w_out)[0]
    shared["conv_w_in"] = f(conv_w_in)[0]
    shared["conv_w_out"] = f(conv_w_out)[0]
    shared["qkcol"] = f(np.stack([np.tile(f(attn_q_norm)[0], 2), np.tile(f(attn_k_norm)[0], 2)], axis=1))
    lamrow = np.concatenate([f(lam_q1)[0], f(lam_k1)[0], f(lam_q2)[0], f(lam_k2)[0]])
    shared["lamrep"] = f(np.repeat(lamrow[None, :], 128, axis=0))
    shared["sublnrep"] = f(np.repeat(f(attn_subln)[0][None, :], 128, axis=0))
    shared["convw"] = f(f(conv_w)[0].T.reshape(8, 128, 3).transpose(1, 0, 2))
    rw, wg, wu, wd = f(router_w), f(moe_w_gate), f(moe_w_up), f(moe_w_down)
    eperm = [np.arange(16), np.concatenate([np.arange(8, 16), np.arange(0, 8)])]
    per_rank = []
    for r in range(2):
        per_rank.append(dict(
            router_w=f(rw[:, :, eperm[r]]), moe_w_gate=f(wg[:, r * 8:(r + 1) * 8]), moe_w_up=f(wu[:, r * 8:(r + 1) * 8]),
            moe_w_down=f(wd[:, r * 8:(r + 1) * 8]), cosd=K["cos_r"][r], sind=K["sin_r"][r], rope=K["rope_r"][r]))
    in_maps = []
    for core in range(_ncores):
        b, r = core // 2, core % 2
        m = dict(shared)
        m.update(per_rank[r])
        m["x"] = x[b]
        m["xloc"] = f(x[b][K["perms"][r]])
        m["ctx"] = ctx[b]
        c2 = np.stack([c[b].reshape(8, 128).T, c_ctx.reshape(8, 128).T], axis=-1)
        m["c2"] = f(c2)
        in_maps.append(m)
    res = run_bass_kernel_spmd(nc, in_maps, core_ids=list(range(_ncores)))
    _CACHE["res"] = res
    if _debug:
        return res
    return np.stack([np.asarray(res.results[2 * b]["out"], dtype=np.float32) for b in range(4)], axis=0)
```
